# Optimizing a Trainium2 kernel written in Bass

```python
import jax, jax.numpy as jnp
from jax import lax
import numpy as np

D_MODEL = 1024
BATCH = 2
SEQ = 8192
DEPTH = 2
DEC_BATCH = 32
DEC_SEQ = 2048
PAST_LEN = 128

GRID_W = 64
NA_HEADS = 8
NA_DH = 64
NA_W = NA_HEADS * NA_DH
NA_KH = 8
NA_KW = 16
NA_SPAN = 2 * NA_KW
DN_HEADS = 4
DN_DK = 128
DN_DV = 128
DN_QK_W = DN_HEADS * DN_DK
DN_W = DN_HEADS * DN_DV
CONV_W = 5
CHUNK = 64
D_FF = 4 * D_MODEL
N_MOD = 6
EPS = 1e-6
IN_SIZES = (3 * NA_W, 2 * DN_QK_W + DN_W, DN_W, 4 * DN_HEADS, 2 * D_MODEL)
IN_COLS = 3 * NA_W + 2 * DN_QK_W + DN_W + DN_W + 4 * DN_HEADS + 2 * D_MODEL

kernel_name = "hybrid_natten_gdn_encoder"


def _rmsnorm(x, g):
    xf = x.astype(jnp.float32)
    y = xf * lax.rsqrt(jnp.mean(xf * xf, axis=-1, keepdims=True) + EPS)
    return (y * g.astype(jnp.float32)).astype(x.dtype)


def _l2norm(x):
    return x * lax.rsqrt(jnp.sum(x * x, axis=-1, keepdims=True) + EPS)


def _neighborhood_attention(q, k, v, rpb):
    B, L, H, Dh = q.shape
    rows = L // GRID_W
    kh = min(NA_KH, rows)
    n_cb = GRID_W // NA_KW
    scale = Dh ** -0.5
    qg = q.reshape(B, rows, GRID_W, H, Dh)
    kg = k.reshape(B, rows, GRID_W, H, Dh)
    vg = v.reshape(B, rows, GRID_W, H, Dh)
    cols = np.arange(GRID_W)
    col_start = np.clip(cols - NA_KW // 2, 0, GRID_W - NA_KW)
    cb0 = np.arange(n_cb) * NA_KW
    kc0 = np.clip(cb0 - NA_KW // 2, 0, GRID_W - NA_SPAN)
    kcols = kc0[:, None] + np.arange(NA_SPAN)[None, :]
    qcols = cb0[:, None] + np.arange(NA_KW)[None, :]
    qs = col_start[qcols]
    valid = (kcols[:, None, :] >= qs[..., None]) & (kcols[:, None, :] < qs[..., None] + NA_KW)
    neg = jnp.asarray(np.where(valid, 0.0, -1e30).astype(np.float32))
    dc_idx = np.clip(kcols[:, None, :] - qcols[:, :, None] + NA_KW - 1, 0, 2 * NA_KW - 2)
    rpb_c = rpb.astype(jnp.float32)[:, :, dc_idx]

    def one_row(r):
        sr = jnp.clip(r - kh // 2, 0, rows - kh)
        qr = lax.dynamic_index_in_dim(qg, r, axis=1, keepdims=False)
        kb = lax.dynamic_slice_in_dim(kg, sr, kh, axis=1)[:, :, kcols]
        vb = lax.dynamic_slice_in_dim(vg, sr, kh, axis=1)[:, :, kcols]
        qr = qr.reshape(B, n_cb, NA_KW, H, Dh)
        s = jnp.einsum('bcqhd,bicjhd->bhcqij', qr, kb).astype(jnp.float32) * scale
        dr_idx = sr + jnp.arange(kh) - r + (NA_KH - 1)
        bias = jnp.take(rpb_c, dr_idx, axis=1).transpose(0, 2, 3, 1, 4)
        s = s + bias[None] + neg[:, :, None, :]
        p = jax.nn.softmax(s.reshape(s.shape[:4] + (kh * NA_SPAN,)), axis=-1).reshape(s.shape)
        o = jnp.einsum('bhcqij,bicjhd->bcqhd', p.astype(vb.dtype), vb)
        return o.reshape(B, GRID_W, H, Dh)

    out = lax.map(one_row, jnp.arange(rows))
    return out.transpose(1, 0, 2, 3, 4).reshape(B, L, H * Dh)


def _gated_delta_chunked(q, k, v, g, beta):
    B, L, H, Dk = q.shape
    Dv = v.shape[-1]
    n = L // CHUNK
    f32 = jnp.float32
    to_c = lambda t: t.astype(f32).reshape((B, n, CHUNK, H) + t.shape[3:]).swapaxes(2, 3).swapaxes(1, 2)
    qc, kc, vc = to_c(q), to_c(k), to_c(v)
    gc, bc = to_c(g), to_c(beta)
    G = jnp.cumsum(gc, axis=-1)
    diff = G[..., :, None] - G[..., None, :]
    incl = np.tril(np.ones((CHUNK, CHUNK), dtype=bool))
    strict = np.tril(np.ones((CHUNK, CHUNK), dtype=bool), -1)
    decay_in = jnp.exp(jnp.where(incl, diff, -jnp.inf))
    kk = jnp.einsum('bhnik,bhnjk->bhnij', kc, kc)
    A = jnp.where(strict, bc[..., :, None] * kk * decay_in, 0.0)
    T = A + jnp.eye(CHUNK, dtype=f32)
    W = lax.linalg.triangular_solve(T, bc[..., None] * jnp.exp(G)[..., None] * kc,
                                    left_side=True, lower=True, unit_diagonal=True)
    U = lax.linalg.triangular_solve(T, bc[..., None] * vc,
                                    left_side=True, lower=True, unit_diagonal=True)
    qk = jnp.einsum('bhnik,bhnjk->bhnij', qc, kc) * decay_in
    q_dec = qc * jnp.exp(G)[..., None]
    k_dec = kc * jnp.exp(G[..., -1:] - G)[..., None]
    g_last = jnp.exp(G[..., -1])
    xs = (jnp.moveaxis(W, 2, 0), jnp.moveaxis(U, 2, 0), jnp.moveaxis(qk, 2, 0),
          jnp.moveaxis(q_dec, 2, 0), jnp.moveaxis(k_dec, 2, 0), jnp.moveaxis(g_last, 2, 0))

    def step(S, inp):
        w_c, u_c, qk_c, qd_c, kd_c, gl_c = inp
        v_new = u_c - jnp.einsum('bhck,bhkv->bhcv', w_c, S)
        o = jnp.einsum('bhck,bhkv->bhcv', qd_c, S) + jnp.einsum('bhij,bhjv->bhiv', qk_c, v_new)
        S = S * gl_c[..., None, None] + jnp.einsum('bhck,bhcv->bhkv', kd_c, v_new)
        return S, o

    S0 = jnp.zeros((B, H, Dk, Dv), f32)
    _, o = lax.scan(step, S0, xs)
    return o.transpose(1, 0, 3, 2, 4).reshape(B, L, H, Dv)


def _bidir_delta(q, k, v, g, beta):
    flip = lambda t: jnp.flip(t, axis=1)
    o_f = _gated_delta_chunked(q, k, v, g[:, :, 0], beta[:, :, 0])
    o_b = flip(_gated_delta_chunked(flip(q), flip(k), flip(v), flip(g[:, :, 1]), flip(beta[:, :, 1])))
    return o_f + o_b


def _centred_conv(x, w):
    C = x.shape[-1]
    return lax.conv_general_dilated(x, w[:, None, :].astype(x.dtype), window_strides=(1,),
                                    padding=[(CONV_W // 2, CONV_W // 2)],
                                    dimension_numbers=('NWC', 'WIO', 'NWC'),
                                    feature_group_count=C)


def _mixer(h, w_in, rpb, conv_w, a_log, dt_bias, dn_norm_g, w_br_attn, w_br_dn, w_out):
    B, L, _ = h.shape
    proj = h @ w_in
    na_qkv, dn_qkv, z, ab, gates = jnp.split(proj, np.cumsum(IN_SIZES)[:-1].tolist(), axis=-1)
    qa, ka, va = jnp.split(na_qkv.reshape(B, L, 3, NA_HEADS, NA_DH), 3, axis=2)
    o_a = _neighborhood_attention(qa[:, :, 0], ka[:, :, 0], va[:, :, 0], rpb)
    dn_qkv = jax.nn.silu(_centred_conv(dn_qkv, conv_w))
    qd, kd, vd = jnp.split(dn_qkv, [DN_QK_W, 2 * DN_QK_W], axis=-1)
    qd = _l2norm(qd.reshape(B, L, DN_HEADS, DN_DK).astype(jnp.float32)) * (DN_DK ** -0.5)
    kd = _l2norm(kd.reshape(B, L, DN_HEADS, DN_DK).astype(jnp.float32))
    vd = vd.reshape(B, L, DN_HEADS, DN_DV)
    ab = ab.reshape(B, L, 4, DN_HEADS).astype(jnp.float32)
    beta = jax.nn.sigmoid(ab[:, :, 0:2])
    g = -jnp.exp(a_log.astype(jnp.float32)) * jax.nn.softplus(ab[:, :, 2:4] + dt_bias.astype(jnp.float32))
    o_d = _bidir_delta(qd, kd, vd, g, beta)
    o_d = _rmsnorm(o_d, dn_norm_g) * jax.nn.silu(z.reshape(B, L, DN_HEADS, DN_DV).astype(jnp.float32))
    o_d = o_d.reshape(B, L, DN_W).astype(h.dtype)
    g_a, g_d = jnp.split(jax.nn.sigmoid(gates), 2, axis=-1)
    merged = g_a * (o_a @ w_br_attn) + g_d * (o_d @ w_br_dn)
    return merged @ w_out


def _trunk(x, c, norm_mix_g, norm_mlp_g, w_ada, b_ada, w_in, na_rpb, dn_conv, dn_a_log,
           dn_dt_bias, dn_norm_g, w_br_attn, w_br_dn, w_out, w_mlp1, w_mlp2, final_norm_g):
    c_act = jax.nn.silu(c)
    for l in range(DEPTH):
        mod = (c_act @ w_ada[l] + b_ada[l])[:, None, :]
        sh1, sc1, gt1, sh2, sc2, gt2 = jnp.split(mod, N_MOD, axis=-1)
        h = _rmsnorm(x, norm_mix_g[l]) * (1 + sc1) + sh1
        x = x + gt1 * _mixer(h, w_in[l], na_rpb[l], dn_conv[l], dn_a_log[l], dn_dt_bias[l],
                             dn_norm_g[l], w_br_attn[l], w_br_dn[l], w_out[l])
        h = _rmsnorm(x, norm_mlp_g[l]) * (1 + sc2) + sh2
        x = x + gt2 * (jnp.square(jax.nn.relu(h @ w_mlp1[l])) @ w_mlp2[l])
    return _rmsnorm(x, final_norm_g)


def setup_inputs(seed: int = 0) -> dict:
    key = jax.random.key(seed)
    ks = jax.random.split(key, 20)
    nrm = lambda k, shape, s: jax.random.normal(k, shape, jnp.float32) * s
    return {
        "x_prompt": nrm(ks[0], (BATCH, SEQ, D_MODEL), 1.0),
        "x_sample": nrm(ks[1], (DEC_BATCH, DEC_SEQ, D_MODEL), 1.0),
        "c_prompt": nrm(ks[2], (BATCH, D_MODEL), 1.0),
        "c_sample": nrm(ks[3], (DEC_BATCH, D_MODEL), 1.0),
        "norm_mix_g": 1.0 + nrm(ks[4], (DEPTH, D_MODEL), 0.05),
        "norm_mlp_g": 1.0 + nrm(ks[5], (DEPTH, D_MODEL), 0.05),
        "w_ada": nrm(ks[6], (DEPTH, D_MODEL, N_MOD * D_MODEL), 0.5 * D_MODEL ** -0.5),
        "b_ada": nrm(ks[7], (DEPTH, N_MOD * D_MODEL), 0.02),
        "w_in": nrm(ks[8], (DEPTH, D_MODEL, IN_COLS), D_MODEL ** -0.5),
        "na_rpb": nrm(ks[9], (DEPTH, NA_HEADS, 2 * NA_KH - 1, 2 * NA_KW - 1), 0.1),
        "dn_conv": nrm(ks[10], (DEPTH, CONV_W, 2 * DN_QK_W + DN_W), CONV_W ** -0.5),
        "dn_a_log": jnp.log(jax.random.uniform(ks[11], (DEPTH, 2, DN_HEADS), jnp.float32, 1.0, 16.0)),
        "dn_dt_bias": jnp.log(jnp.expm1(jax.random.uniform(ks[12], (DEPTH, 2, DN_HEADS), jnp.float32, 0.001, 0.1))),
        "dn_norm_g": 1.0 + nrm(ks[13], (DEPTH, DN_DV), 0.05),
        "w_br_attn": nrm(ks[14], (DEPTH, NA_W, D_MODEL), NA_W ** -0.5),
        "w_br_dn": nrm(ks[15], (DEPTH, DN_W, D_MODEL), DN_W ** -0.5),
        "w_out": nrm(ks[16], (DEPTH, D_MODEL, D_MODEL), D_MODEL ** -0.5),
        "w_mlp1": nrm(ks[17], (DEPTH, D_MODEL, D_FF), D_MODEL ** -0.5),
        "w_mlp2": nrm(ks[18], (DEPTH, D_FF, D_MODEL), D_FF ** -0.5),
        "final_norm_g": 1.0 + nrm(ks[19], (D_MODEL,), 0.05),
    }


def reference(x_prompt, x_sample, c_prompt, c_sample, norm_mix_g, norm_mlp_g, w_ada, b_ada,
              w_in, na_rpb, dn_conv, dn_a_log, dn_dt_bias, dn_norm_g, w_br_attn, w_br_dn,
              w_out, w_mlp1, w_mlp2, final_norm_g):
    y_prompt = _trunk(x_prompt, c_prompt, norm_mix_g, norm_mlp_g, w_ada, b_ada, w_in, na_rpb,
                      dn_conv, dn_a_log, dn_dt_bias, dn_norm_g, w_br_attn, w_br_dn, w_out,
                      w_mlp1, w_mlp2, final_norm_g)
    y_sample = _trunk(x_sample, c_sample, norm_mix_g, norm_mlp_g, w_ada, b_ada, w_in, na_rpb,
                      dn_conv, dn_a_log, dn_dt_bias, dn_norm_g, w_br_attn, w_br_dn, w_out,
                      w_mlp1, w_mlp2, final_norm_g)
    return (y_prompt, y_sample)
```

```python
import os
import numpy as np
import ml_dtypes
from contextlib import ExitStack
import concourse.bass as bass
import concourse.mybir as mybir
from concourse.bass_utils import run_bass_kernel_spmd

F32 = mybir.dt.float32
BF16 = mybir.dt.bfloat16
AF = mybir.ActivationFunctionType
ALU = mybir.AluOpType

D = 1024
DEPTH = 2
NCORES = 8
SEG = 2048
IN_COLS = 5648
DFF = 4096
EPS = 1e-6
PADT = 256
NEGBIG = -30000.0

C_NQ, C_NK, C_NV = 0, 512, 1024
C_DQ = 1536
C_Z = 3072
C_AB = 3584
C_G = 3600


class Trk:
    __slots__ = ("w", "r", "excl")

    def __init__(self, excl=False):
        self.w = None
        self.r = []
        self.excl = excl


class Sched:
    def __init__(self, nc, es):
        self.nc = nc
        self.eng = {"pe": nc.tensor, "act": nc.scalar, "dve": nc.vector, "pool": nc.gpsimd, "sp": nc.sync}
        self.sem = {}
        self.cnt = {}
        for e in self.eng:
            self.sem[e] = es.enter_context(nc.semaphore("sem_" + e))
            self.cnt[e] = 0
        self.waited = {e: {} for e in self.eng}
        self.nslot = {"sp": 12, "pool": 6}
        self.slots = {}
        for q, n in self.nslot.items():
            self.slots[q] = []
            for i in range(n):
                key = "dq_%s_%d" % (q, i)
                self.sem[key] = es.enter_context(nc.semaphore(key))
                self.cnt[key] = 0
                self.slots[q].append(key)
        self.rr = {q: 0 for q in self.nslot}
        self.nwait = 0
        self.nops = 0

    def _wait(self, e, deps):
        best = {}
        for d in deps:
            if d is None:
                continue
            k, v = d
            if best.get(k, 0) < v:
                best[k] = v
        for k, v in best.items():
            if self.waited[e].get(k, 0) < v:
                self.eng[e].wait_ge(self.sem[k], v)
                self.waited[e][k] = v
                self.nwait += 1

    def _deps(self, e, r, w):
        deps = []
        for t in r:
            deps.append(t.w)
            if t.excl:
                for rd in t.r:
                    if rd[0] != e:
                        deps.append(rd)
        for t in w:
            deps.append(t.w)
            for rd in t.r:
                if rd[0] != e:
                    deps.append(rd)
        return deps

    def _stamp(self, st, r, w):
        for t in r:
            t.r.append(st)
        for t in w:
            t.w = st
            t.r = []

    def op(self, e, fn, r=(), w=()):
        self._wait(e, self._deps(e, r, w))
        ins = fn()
        self.cnt[e] += 1
        ins.then_inc(self.sem[e], 1)
        self._stamp((e, self.cnt[e]), r, w)
        self.nops += 1

    def dma(self, q, out, in_, r=(), w=(), **kw):
        key = self.slots[q][self.rr[q]]
        self.rr[q] = (self.rr[q] + 1) % self.nslot[q]
        deps = self._deps(q, r, w)
        deps.append((key, self.cnt[key]))
        self._wait(q, deps)
        self.eng[q].dma_start(out=out, in_=in_, **kw).then_inc(self.sem[key], 16)
        self.cnt[key] += 16
        self._stamp((key, self.cnt[key]), r, w)
        self.nops += 1

    def barrier(self):
        allst = [(k, v) for k, v in self.cnt.items() if v > 0]
        for e in self.eng:
            self._wait(e, [d for d in allst if d[0] != e])


def _bf(a):
    return np.asarray(a, dtype=np.float32)


def build_program(nseg=5, depth=DEPTH, debug=False, phases="ABCDE", phases_last=None):
    ntok = nseg * SEG
    nt = ntok // 512
    nc = bass.Bass("TRN2", target_bir_lowering=False)
    es = ExitStack()
    S = Sched(nc, es)

    def din(name, shape, dt=F32):
        return nc.dram_tensor(name, list(shape), dt, kind="ExternalInput")

    okind = "ExternalOutput" if debug else "Internal"

    def dscr(name, shape, dt):
        return nc.dram_tensor(name, list(shape), dt, kind=okind)

    x_in = din("x", [ntok, D])
    cT_in = din("cT", [128, 8, 8])
    flags_in = din("flags", [1, 8])
    g1_in = din("g1", [128, depth * 8])
    g2_in = din("g2", [128, depth * 8])
    gf_in = din("gf", [128, 8])
    bada_in = din("bada", [128, depth * 48])
    wada_in = din("w_ada", [depth, D, 6 * D])
    win_in = din("w_in", [depth, D, IN_COLS])
    rpb_in = din("rpbp", [depth, 8, 15, 128])
    conv_in = din("convw", [128, depth * 5 * 12])
    alog_in = din("alog", [1, depth * 8])
    dtb_in = din("dtb", [1, depth * 8])
    dng_in = din("dng", [1, depth * 128])
    wbra_in = din("w_br_attn", [depth, 512, D])
    wbrd_in = din("w_br_dn", [depth, 512, D])
    wout_in = din("w_out", [depth, D, D])
    w1_in = din("w_mlp1", [depth, D, DFF])
    w2_in = din("w_mlp2", [depth, DFF, D])
    cst_in = din("consts", [128, 1024])
    cst2_in = din("consts2", [128, 640])
    y_out = nc.dram_tensor("y", [ntok, D], F32, kind="ExternalOutput")

    XT = dscr("XT", [D, ntok], F32)
    QT = dscr("QT", [512, ntok], BF16)
    KT = dscr("KT", [512, ntok + 2 * PADT], BF16)
    VV = dscr("VV", [ntok + 2 * PADT, 512], BF16)
    DQ = dscr("DQ", [1536, ntok + 128], BF16)
    ZZ = dscr("ZZ", [ntok, 512], BF16)
    ABs = dscr("ABs", [128, ntok // 128, 16], F32)
    GT = dscr("GT", [2048, ntok], BF16)
    OAT = dscr("OAT", [512, ntok], BF16)
    ODT = dscr("ODT", [512, ntok], BF16)
    OF = dscr("OF", [ntok, 512], F32)
    QN = dscr("QN", [1536, ntok], BF16)

    def dtr(n):
        return [Trk() for _ in range(n)]

    T_XT, T_QT, T_DQ, T_ZZ, T_AB, T_GT, T_OAT, T_ODT, T_OF, T_QN = (dtr(nt) for _ in range(10))
    T_KT = dtr(nt + 2)
    T_VV = dtr(nt + 2)

    def sbp(name, shape, dt):
        return es.enter_context(nc.sbuf_tensor(name, list(shape), dt))

    cst = sbp("cst", [128, 1024], F32)
    t_cst = Trk()
    ident_f = cst[:, 0:128]
    TRI = [cst[:, 128:256], cst[:, 256:384]]
    NEGM = [cst[:, 384:512], cst[:, 512:640]]
    cbf = sbp("cbf", [128, 512], BF16)
    t_cbf = Trk()
    ident_b = cbf[:, 0:128]
    ones_b = cbf[:, 128:256]
    STRICT = [cbf[:, 256:384], cbf[:, 384:512]]
    cb2 = sbp("cb2", [128, 640], BF16)
    t_cb2 = Trk()
    ones_f = sbp("ones_f", [128, 128], F32)
    t_onesf = Trk()
    flags = sbp("flags_sb", [128, 8], F32)
    t_flags = Trk()
    g1 = sbp("g1_sb", [128, depth * 8], F32)
    g2 = sbp("g2_sb", [128, depth * 8], F32)
    gf = sbp("gf_sb", [128, 8], F32)
    bada = sbp("bada_sb", [128, depth * 48], F32)
    mod = sbp("mod_sb", [128, depth * 48, 8], F32)
    A1 = sbp("A1_sb", [128, depth * 8, 8], F32)
    A2 = sbp("A2_sb", [128, depth * 8, 8], F32)
    t_par = Trk()
    t_mod = Trk()

    PS = [es.enter_context(nc.psum_tensor("ps%d" % i, [128, 512], F32)) for i in range(8)]
    T_PS = [Trk(excl=True) for _ in range(8)]

    def psbf(i):
        return PS[i][:].bitcast(BF16)

    V_ = nc.vector
    A_ = nc.scalar
    P_ = nc.tensor
    G_ = nc.gpsimd

    S.dma("sp", cst[:], cst_in.ap(), w=[t_cst])
    S.dma("sp", flags[:], flags_in.ap().partition_broadcast(128), w=[t_flags])
    S.dma("sp", g1[:], g1_in.ap(), w=[t_par])
    S.dma("sp", g2[:], g2_in.ap(), w=[t_par])
    S.dma("sp", gf[:], gf_in.ap(), w=[t_par])
    S.dma("sp", bada[:], bada_in.ap(), w=[t_par])
    S.op("dve", lambda: V_.tensor_copy(out=cbf[:, 0:128], in_=cst[:, 0:128]), r=[t_cst], w=[t_cbf])
    S.op("dve", lambda: V_.memset(cbf[:, 128:256], 1.0), w=[t_cbf])
    S.op("dve", lambda: V_.tensor_copy(out=cbf[:, 256:512], in_=cst[:, 640:896]), r=[t_cst], w=[t_cbf])
    S.op("dve", lambda: V_.memset(ones_f[:], 1.0), w=[t_onesf])
    with ExitStack() as c2es:
        c2f = c2es.enter_context(nc.sbuf_tensor("c2f", [128, 640], F32))
        t_c2f = Trk()
        S.dma("sp", c2f[:], cst2_in.ap(), w=[t_c2f])
        S.op("dve", lambda: V_.tensor_copy(out=cb2[:], in_=c2f[:]), r=[t_c2f], w=[t_cb2])
        S.barrier()

    with ExitStack() as pes:
        zt = pes.enter_context(nc.sbuf_tensor("zt", [128, 4, 512], BF16))
        t_zt = Trk()
        S.op("dve", lambda: V_.memset(zt[:], 0.0), w=[t_zt])
        S.dma("sp", KT.ap()[:, 0:PADT].rearrange("(c p) t -> p c t", p=128), zt[:, :, 0:PADT], r=[t_zt], w=[T_KT[0]])
        S.dma("sp", KT.ap()[:, PADT + ntok:PADT + ntok + PADT].rearrange("(c p) t -> p c t", p=128), zt[:, :, 0:PADT],
              r=[t_zt], w=[T_KT[nt + 1]])
        S.dma("sp", VV.ap()[0:PADT, :].rearrange("(b p) c -> p b c", p=128), zt[:, 0:2, :], r=[t_zt], w=[T_VV[0]])
        S.dma("sp", VV.ap()[PADT + ntok:PADT + ntok + PADT, :].rearrange("(b p) c -> p b c", p=128), zt[:, 0:2, :],
              r=[t_zt], w=[T_VV[nt + 1]])

        cact = pes.enter_context(nc.sbuf_tensor("cact", [128, 8, 8], F32))
        t_cact = Trk()
        S.dma("sp", cact[:], cT_in.ap(), w=[t_cact])
        S.op("act", lambda: A_.activation(out=cact[:], in_=cact[:], func=AF.Silu), r=[t_cact], w=[t_cact])
        wa = [pes.enter_context(nc.sbuf_tensor("wa%d" % i, [128, 8, 512], F32)) for i in range(2)]
        t_wa = [Trk(), Trk()]
        it = 0
        for l in range(depth):
            for cb in range(12):
                b = it % 2
                S.dma("sp", wa[b][:], wada_in.ap()[l, :, cb * 512:(cb + 1) * 512].rearrange("(k p) c -> p k c", p=128),
                      w=[t_wa[b]])
                bank = it % 2

                def mm(b=b, bank=bank):
                    ins = None
                    for j in range(4):
                        for k in range(8):
                            ins = P_.matmul(PS[bank][:, j * 8:(j + 1) * 8], lhsT=wa[b][:, k, j * 128:(j + 1) * 128],
                                            rhs=cact[:, k, :], start=(k == 0), stop=(k == 7))
                    return ins
                S.op("pe", mm, r=[t_wa[b], t_cact], w=[T_PS[bank]])
                c0 = l * 48 + cb * 4
                S.op("dve", lambda bank=bank, c0=c0: V_.tensor_tensor(
                    out=mod[:, c0:c0 + 4, :], in0=PS[bank][:, 0:32].rearrange("p (j s) -> p j s", j=4),
                    in1=bada[:, c0:c0 + 4].unsqueeze(2).to_broadcast([128, 4, 8]), op=ALU.add),
                    r=[T_PS[bank], t_par], w=[t_mod])
                it += 1
        for l in range(depth):
            for (Ax, gx, off) in ((A1, g1, 8), (A2, g2, 32)):
                S.op("dve", lambda Ax=Ax, off=off, l=l: V_.tensor_scalar(
                    out=Ax[:, l * 8:(l + 1) * 8, :], in0=mod[:, l * 48 + off:l * 48 + off + 8, :], scalar1=1.0, scalar2=None,
                    op0=ALU.add), r=[t_mod], w=[t_mod])
                S.op("dve", lambda Ax=Ax, gx=gx, l=l: V_.tensor_tensor(
                    out=Ax[:, l * 8:(l + 1) * 8, :], in0=Ax[:, l * 8:(l + 1) * 8, :],
                    in1=gx[:, l * 8:(l + 1) * 8].unsqueeze(2).to_broadcast([128, 8, 8]), op=ALU.mult),
                    r=[t_mod, t_par], w=[t_mod])
        S.barrier()

    def modv(l, which, ch, seg):
        return mod[:, l * 48 + which * 8 + ch, seg:seg + 1]

    def norm_mod(xT, t_x, xsq, t_xsq, hT, t_h, rstd, t_rstd, tmp, t_tmp, nb, Ax, l, sh_which, seg, psb, N):
        for dch in range(8):
            S.op("act", lambda dch=dch: A_.activation(out=xsq[:, dch, :], in_=xT[:, dch, :], func=AF.Square),
                 r=[t_x], w=[t_xsq])

        def mm():
            ins = None
            for dch in range(8):
                ins = P_.matmul(PS[psb][:, 0:N], lhsT=ones_b, rhs=xsq[:, dch, :], start=(dch == 0), stop=(dch == 7))
            return ins
        S.op("pe", mm, r=[t_xsq, t_cbf], w=[T_PS[psb]])
        S.op("act", lambda: A_.activation(out=rstd[:], in_=PS[psb][:, 0:N], func=AF.Ln, bias=epsD[:, 0:1], scale=1.0 / D),
             r=[T_PS[psb], t_eps], w=[t_rstd])
        S.op("act", lambda: A_.activation(out=rstd[:], in_=rstd[:], func=AF.Exp, scale=-0.5), r=[t_rstd], w=[t_rstd])
        for dch in range(8):
            b = dch % nb
            S.op("dve", lambda dch=dch, b=b: V_.tensor_tensor(out=tmp[b][:], in0=xT[:, dch, :], in1=rstd[:], op=ALU.mult),
                 r=[t_x, t_rstd], w=[t_tmp[b]])
            S.op("act", lambda dch=dch, b=b: A_.activation(
                out=hT[:, dch, :], in_=tmp[b][:], func=AF.Identity, bias=modv(l, sh_which, dch, seg),
                scale=Ax[:, l * 8 + dch, seg:seg + 1]), r=[t_tmp[b], t_mod], w=[t_h])

    epsD = sbp("epsD", [128, 4], F32)
    t_eps = Trk()
    S.op("dve", lambda: V_.memset(epsD[:, 0:1], EPS), w=[t_eps])
    S.op("dve", lambda: V_.memset(epsD[:, 1:2], float(-0.5 * np.log(128.0))), w=[t_eps])

    def phase_A(l):
        with ExitStack() as pes:
            def sb(name, shape, dt):
                return pes.enter_context(nc.sbuf_tensor(name + "_L%d" % l, list(shape), dt))
            win = sb("win", [128, 8, IN_COLS], BF16)
            t_winp = [Trk() for _ in range(4)]
            for cp in range(4):
                c0 = cp * 1412
                S.dma("pool", win[:, :, c0:c0 + 1412], win_in.ap()[l, :, c0:c0 + 1412].rearrange("(k p) c -> p k c", p=128),
                      w=[t_winp[cp]])

            def twin(ca, cb):
                return [t_winp[i] for i in range(ca // 1412, (cb - 1) // 1412 + 1)]
            xtok = sb("xtok", [128, 4, D], F32) if l == 0 else None
            t_xtok = Trk()
            xT = [sb("xT%d" % i, [128, 8, 512], F32) for i in range(2)]
            t_xT = [Trk(), Trk()]
            xsq2 = [sb("xsq%d" % i, [128, 8, 512], BF16) for i in range(2)]
            t_xsq2 = [Trk(), Trk()]
            hT2 = [sb("hT%d" % i, [128, 8, 512], BF16) for i in range(2)]
            t_h2 = [Trk(), Trk()]
            rstd = sb("rstd", [128, 512], F32)
            t_rstd = Trk()
            tmp = [sb("tmpA%d" % i, [128, 512], F32) for i in range(2)]
            t_tmp = [Trk(), Trk()]
            stg = [sb("stg%d" % i, [128, 4, 512], BF16) for i in range(4)]
            t_stg = [Trk() for _ in range(4)]
            abst = sb("abst", [128, 4, 16], F32)
            t_abst = Trk()
            si = [0]

            def load_x(t):
                b = t % 2
                if l == 0:
                    S.dma("sp", xtok[:], x_in.ap()[t * 512:(t + 1) * 512, :].rearrange("(b p) d -> p b d", p=128), w=[t_xtok])
                else:
                    S.dma("sp", xT[b][:], XT.ap()[:, t * 512:(t + 1) * 512].rearrange("(c p) t -> p c t", p=128),
                          r=[T_XT[t]], w=[t_xT[b]])

            def prep(t):
                seg = t // 4
                b = t % 2
                if l == 0:
                    load_x(t)
                    for dch in range(8):
                        bank = dch % 2

                        def tp(dch=dch, bank=bank):
                            ins = None
                            for blk in range(4):
                                ins = P_.transpose(out=PS[bank][:, blk * 128:(blk + 1) * 128],
                                                   in_=xtok[:, blk, dch * 128:(dch + 1) * 128], identity=ident_f)
                            return ins
                        S.op("pe", tp, r=[t_xtok, t_cst], w=[T_PS[bank]])
                        S.op("dve", lambda dch=dch, bank=bank: V_.tensor_copy(out=xT[b][:, dch, :], in_=PS[bank][:]),
                             r=[T_PS[bank]], w=[t_xT[b]])
                    S.dma("sp", XT.ap()[:, t * 512:(t + 1) * 512].rearrange("(c p) t -> p c t", p=128), xT[b][:],
                          r=[t_xT[b]], w=[T_XT[t]])
                norm_mod(xT[b], t_xT[b], xsq2[b], t_xsq2[b], hT2[b], t_h2[b], rstd, t_rstd, tmp, t_tmp, 2, A1, l, 0, seg, 2, 512)

            if l != 0:
                load_x(0)
            prep(0)
            for t in range(nt):
                seg = t // 4
                b = t % 2
                hT = hT2[b]
                t_h = t_h2[b]
                if l != 0 and t + 1 < nt:
                    load_x(t + 1)

                def fm_proj(col0, nch, dst, dst_trk, dst_col0, scale=None):
                    for g in range(0, nch, 4):
                        s_ = si[0] % 4
                        si[0] += 1
                        n4 = min(4, nch - g)
                        for c in range(n4):
                            bank = 3 + (c % 4)
                            cc = col0 + (g + c) * 128

                            def mm(cc=cc, bank=bank):
                                ins = None
                                for k in range(8):
                                    ins = P_.matmul(PS[bank][:], lhsT=win[:, k, cc:cc + 128], rhs=hT[:, k, :],
                                                    start=(k == 0), stop=(k == 7))
                                return ins
                            S.op("pe", mm, r=twin(cc, cc + 128) + [t_h], w=[T_PS[bank]])
                            if (c % 2) == 0:
                                if scale is None:
                                    S.op("act", lambda c=c, bank=bank, s_=s_: A_.copy(out=stg[s_][:, c, :], in_=PS[bank][:]),
                                         r=[T_PS[bank]], w=[t_stg[s_]])
                                else:
                                    S.op("act", lambda c=c, bank=bank, s_=s_: A_.mul(out=stg[s_][:, c, :], in_=PS[bank][:],
                                                                                     mul=scale),
                                         r=[T_PS[bank]], w=[t_stg[s_]])
                            else:
                                if scale is None:
                                    S.op("dve", lambda c=c, bank=bank, s_=s_: V_.tensor_copy(out=stg[s_][:, c, :], in_=PS[bank][:]),
                                         r=[T_PS[bank]], w=[t_stg[s_]])
                                else:
                                    S.op("dve", lambda c=c, bank=bank, s_=s_: V_.tensor_scalar(
                                        out=stg[s_][:, c, :], in0=PS[bank][:], scalar1=scale, scalar2=None, op0=ALU.mult),
                                        r=[T_PS[bank]], w=[t_stg[s_]])
                        r0 = dst_col0 + g * 128
                        S.dma("sp", dst[r0:r0 + n4 * 128, :].rearrange("(c p) t -> p c t", p=128), stg[s_][:, 0:n4, :],
                              r=[t_stg[s_]], w=[dst_trk])

                tk = slice(t * 512, (t + 1) * 512)
                fm_proj(C_NQ, 4, QT.ap()[:, tk], T_QT[t], 0, scale=0.125)
                fm_proj(C_NK, 4, KT.ap()[:, PADT + t * 512:PADT + (t + 1) * 512], T_KT[t + 1], 0)
                fm_proj(C_DQ, 12, DQ.ap()[:, 64 + t * 512:64 + (t + 1) * 512], T_DQ[t], 0)
                if t + 1 < nt:
                    prep(t + 1)
                fm_proj(C_G, 16, GT.ap()[:, tk], T_GT[t], 0)

                def tm_proj(col0, dst_ap, dst_trk):
                    s_ = si[0] % 4
                    si[0] += 1
                    for blk in range(4):
                        bank = 3 + blk

                        def mm(blk=blk, bank=bank):
                            ins = None
                            for k in range(8):
                                ins = P_.matmul(PS[bank][:], lhsT=hT[:, k, blk * 128:(blk + 1) * 128],
                                                rhs=win[:, k, col0:col0 + 512], start=(k == 0), stop=(k == 7))
                            return ins
                        S.op("pe", mm, r=twin(col0, col0 + 512) + [t_h], w=[T_PS[bank]])
                        if blk % 2 == 0:
                            S.op("act", lambda blk=blk, bank=bank, s_=s_: A_.copy(out=stg[s_][:, blk, :], in_=PS[bank][:]),
                                 r=[T_PS[bank]], w=[t_stg[s_]])
                        else:
                            S.op("dve", lambda blk=blk, bank=bank, s_=s_: V_.tensor_copy(out=stg[s_][:, blk, :], in_=PS[bank][:]),
                                 r=[T_PS[bank]], w=[t_stg[s_]])
                    S.dma("sp", dst_ap.rearrange("(b p) c -> p b c", p=128), stg[s_][:], r=[t_stg[s_]], w=[dst_trk])

                tm_proj(C_NV, VV.ap()[PADT + t * 512:PADT + (t + 1) * 512, :], T_VV[t + 1])
                tm_proj(C_Z, ZZ.ap()[tk, :], T_ZZ[t])

                def mmab():
                    ins = None
                    for blk in range(4):
                        for k in range(8):
                            ins = P_.matmul(PS[7][:, blk * 128:(blk + 1) * 128], lhsT=hT[:, k, blk * 128:(blk + 1) * 128],
                                            rhs=win[:, k, C_AB - 112:C_AB + 16], start=(k == 0), stop=(k == 7))
                    return ins
                S.op("pe", mmab, r=twin(C_AB - 112, C_AB + 16) + [t_h], w=[T_PS[7]])
                S.op("dve", lambda: V_.tensor_copy(out=abst[:], in_=PS[7][:].rearrange("p (b c) -> p b c", b=4)[:, :, 112:128]),
                     r=[T_PS[7]], w=[t_abst])
                S.dma("sp", ABs.ap()[:, t * 4:(t + 1) * 4, :], abst[:], r=[t_abst], w=[T_AB[t]])
            S.barrier()

    def phase_B(l):
        with ExitStack() as pes:
            def sb(name, shape, dt):
                return pes.enter_context(nc.sbuf_tensor(name + "_L%d" % l, list(shape), dt))
            T2 = sb("T2", [128, 14, 512], BF16)
            t_T2 = Trk()
            with ExitStack() as tes:
                hk = tes.enter_context(nc.sbuf_tensor("hk_L%d" % l, [64, 8, 15 * 64], F32))
                t_hk = Trk()
                src = bass.AP(rpb_in, l * 8 * 15 * 128, [[1, 64], [128, 15], [15 * 128, 8], [1, 64]])
                for h0 in range(8):
                    srch = bass.AP(rpb_in, l * 8 * 15 * 128 + h0 * 15 * 128, [[1, 64], [128, 15], [1, 64]])
                    S.dma("sp", hk[:, h0, :].rearrange("p (a b) -> p a b", a=15), srch, w=[t_hk])
                traw = tes.enter_context(nc.sbuf_tensor("traw_L%d" % l, [128, 512], F32))
                t_traw = Trk()
                Jx = cst[0:64, 896:960]
                for d in range(14):
                    bank = d % 2

                    def mm(d=d, bank=bank):
                        ins = None
                        for h in range(8):
                            ins = P_.matmul(PS[bank][:, h * 64:(h + 1) * 64], lhsT=hk[:, h, d * 64:(d + 2) * 64], rhs=Jx,
                                            start=True, stop=True)
                        return ins
                    S.op("pe", mm, r=[t_hk, t_cst], w=[T_PS[bank]])
                    S.op("act", lambda bank=bank: A_.activation(out=traw[:], in_=PS[bank][:], func=AF.Exp),
                         r=[T_PS[bank]], w=[t_traw])
                    S.op("dve", lambda d=d: V_.tensor_tensor(
                        out=T2[:, d, :].rearrange("p (h q) -> p h q", h=8), in0=traw[:].rearrange("p (h q) -> p h q", h=8),
                        in1=cst[:, 960:1024].unsqueeze(1).to_broadcast([128, 8, 64]), op=ALU.mult),
                        r=[t_traw, t_cst], w=[t_T2])
                S.barrier()

            qt = [sb("qt%d" % i, [128, 4, 512], BF16) for i in range(2)]
            t_qt = [Trk(), Trk()]
            qz = sb("qz", [128, 8, 512], BF16)
            t_qz = Trk()
            kw = [sb("kw%d" % i, [128, 4, 1024], BF16) for i in range(2)]
            t_kw = [Trk(), Trk()]
            ve = [sb("ve%d" % i, [128, 8, 512], BF16) for i in range(2)]
            vo = [sb("vo%d" % i, [128, 7, 512], BF16) for i in range(2)]
            t_ve = [Trk(), Trk()]
            t_vo = [Trk(), Trk()]
            ex = [sb("ex%d" % i, [128, 512], BF16) for i in range(4)]
            t_ex = [Trk() for _ in range(4)]
            pt = [sb("pt%d" % i, [128, 512], BF16) for i in range(8)]
            t_pt = [Trk() for _ in range(8)]
            lnd = [sb("lnd%d" % i, [64, 512], F32) for i in range(2)]
            t_lnd = [Trk(), Trk()]
            osb = [sb("osb%d" % i, [64, 512], F32) for i in range(2)]
            t_osb = [Trk(), Trk()]
            ost = [sb("ost%d" % i, [64, 8, 512], BF16) for i in range(2)]
            t_ost = [Trk(), Trk()]
            S.op("dve", lambda: V_.memset(qz[:], 0.0), w=[t_qz])

            def loads(t):
                b = t % 2
                tk = slice(t * 512, (t + 1) * 512)
                S.dma("sp", qt[b][:], QT.ap()[:, tk].rearrange("(c p) t -> p c t", p=128), r=[T_QT[t]], w=[t_qt[b]])
                k0 = t * 512
                trk = [T_KT[i] for i in (t, t + 1, t + 2) if 0 <= i < nt + 2]
                S.dma("sp", kw[b][:], KT.ap()[:, k0:k0 + 1024].rearrange("(c p) t -> p c t", p=128), r=trk, w=[t_kw[b]])
                trv = [T_VV[i] for i in (t, t + 1, t + 2) if 0 <= i < nt + 2]
                S.dma("sp", ve[b][:], VV.ap()[k0:k0 + 1024, :].rearrange("(m p) c -> p m c", p=128), r=trv, w=[t_ve[b]])
                S.dma("sp", vo[b][:], VV.ap()[k0 + 64:k0 + 64 + 896, :].rearrange("(m p) c -> p m c", p=128), r=trv,
                      w=[t_vo[b]])

            loads(0)
            rowi = [0]
            for t in range(nt):
                b = t % 2
                seg = t // 4
                tpos = t % 4
                if t + 1 < nt:
                    loads(t + 1)
                for hp in range(2):
                    S.op("pool", lambda hp=hp: G_.tensor_copy(
                        out=qz[hp * 64:(hp + 1) * 64, :, :].rearrange("p (c two) t -> p c two t", two=2)[:, :, hp, :],
                        in_=qt[b][hp * 64:(hp + 1) * 64, :, :]), r=[t_qt[b]], w=[t_qz])
                ob = t % 2
                jobs = []
                for i in range(8):
                    alts = []
                    edge_start = (tpos == 0 and i < 4)
                    edge_end = (tpos == 3 and i > 4)
                    if edge_start:
                        fidx = seg - 1
                        if fidx >= 0:
                            alts = [("std", i, -4), ("clamp", 4, -i)]
                        else:
                            alts = [("clamp", 4, -i)]
                            fidx = None
                    elif edge_end:
                        fidx = seg
                        if seg + 1 < nseg:
                            alts = [("std", i, -4), ("clamp", 4, -i)]
                        else:
                            alts = [("clamp", 4, -i)]
                            fidx = None
                    else:
                        alts = [("std", i, -4)]
                        fidx = None
                    jobs.append((i, alts, fidx))

                def stage1(i, rel, o, rb):
                    for kb in range(4):
                        bank = kb

                        def mm(kb=kb, bank=bank):
                            ins = None
                            ks = (rel + 2 * kb) * 64
                            for h in range(8):
                                ins = P_.matmul(PS[bank][:, h * 64:(h + 1) * 64], lhsT=kw[b][:, h // 2, ks:ks + 128],
                                                rhs=qz[:, h, i * 64:(i + 1) * 64], start=True, stop=True)
                            return ins
                        S.op("pe", mm, r=[t_kw[b], t_qz], w=[T_PS[bank]])
                        S.op("act", lambda kb=kb, bank=bank: A_.activation(out=ex[kb][:], in_=PS[bank][:], func=AF.Exp),
                             r=[T_PS[bank]], w=[t_ex[kb]])
                        d = o + 2 * kb + 7
                        pi = rb * 4 + kb
                        S.op("dve", lambda kb=kb, d=d, pi=pi: V_.tensor_tensor(out=pt[pi][:], in0=ex[kb][:], in1=T2[:, d, :],
                                                                              op=ALU.mult),
                             r=[t_ex[kb], t_T2], w=[t_pt[pi]])

                def stage2(i, rel, rb):
                    dbank = 4 + rb
                    obank = 6 + rb

                    def mmd():
                        ins = None
                        for kb in range(4):
                            ins = P_.matmul(PS[dbank][0:64, :], lhsT=ones_b[:, 0:64], rhs=pt[rb * 4 + kb][:],
                                            start=(kb == 0), stop=(kb == 3))
                        return ins
                    S.op("pe", mmd, r=[t_pt[rb * 4 + k_] for k_ in range(4)] + [t_cbf], w=[T_PS[dbank]])

                    def mmo():
                        ins = None
                        for h in range(8):
                            for kb in range(4):
                                rr = rel + 2 * kb
                                vsrc = ve[b][:, rr // 2, h * 64:(h + 1) * 64] if rr % 2 == 0 else \
                                    vo[b][:, (rr - 1) // 2, h * 64:(h + 1) * 64]
                                ins = P_.matmul(PS[obank][0:64, h * 64:(h + 1) * 64], lhsT=vsrc,
                                                rhs=pt[rb * 4 + kb][:, h * 64:(h + 1) * 64], start=(kb == 0), stop=(kb == 3))
                        return ins
                    S.op("pe", mmo, r=[t_pt[rb * 4 + k_] for k_ in range(4)] + [t_ve[b], t_vo[b]], w=[T_PS[obank]])
                    S.op("act", lambda: A_.activation(out=lnd[rb][:], in_=PS[dbank][0:64, :], func=AF.Ln),
                         r=[T_PS[dbank]], w=[t_lnd[rb]])
                    S.op("act", lambda: A_.activation(out=lnd[rb][:], in_=lnd[rb][:], func=AF.Exp, scale=-1.0),
                         r=[t_lnd[rb]], w=[t_lnd[rb]])

                def finish(i, res, fidx):
                    dst = ost[ob][:, :, i * 64:(i + 1) * 64]
                    if len(res) == 1:
                        rb, obank = res[0]
                        S.op("dve", lambda: V_.tensor_tensor(
                            out=dst, in0=PS[obank][0:64, :].rearrange("p (h q) -> p h q", h=8),
                            in1=lnd[rb][:].rearrange("p (h q) -> p h q", h=8), op=ALU.mult),
                            r=[T_PS[obank], t_lnd[rb]], w=[t_ost[ob]])
                    else:
                        (rb0, ob0), (rb1, ob1) = res
                        S.op("dve", lambda: V_.tensor_tensor(out=osb[0][:], in0=PS[ob0][0:64, :], in1=lnd[rb0][:], op=ALU.mult),
                             r=[T_PS[ob0], t_lnd[rb0]], w=[t_osb[0]])
                        S.op("dve", lambda: V_.tensor_tensor(out=osb[1][:], in0=PS[ob1][0:64, :], in1=lnd[rb1][:], op=ALU.mult),
                             r=[T_PS[ob1], t_lnd[rb1]], w=[t_osb[1]])
                        S.op("dve", lambda: V_.tensor_tensor(out=osb[0][:], in0=osb[0][:], in1=osb[1][:], op=ALU.subtract),
                             r=[t_osb[0], t_osb[1]], w=[t_osb[0]])
                        S.op("dve", lambda: V_.scalar_tensor_tensor(
                            out=dst, in0=osb[0][:].rearrange("p (h q) -> p h q", h=8), scalar=flags[0:64, fidx:fidx + 1],
                            in1=osb[1][:].rearrange("p (h q) -> p h q", h=8), op0=ALU.mult, op1=ALU.add),
                            r=[t_osb[0], t_osb[1], t_flags], w=[t_ost[ob]])

                flat = []
                for (i, alts, fidx) in jobs:
                    for ai, (nm, rel, o) in enumerate(alts):
                        flat.append((i, rel, o, ai == len(alts) - 1, fidx))
                pend = None
                resacc = []
                for (i, rel, o, lastalt, fidx) in flat:
                    rb = rowi[0] % 2
                    rowi[0] += 1
                    stage1(i, rel, o, rb)
                    if pend is not None:
                        pi_, prel, prb, plast, pfidx = pend
                        stage2(pi_, prel, prb)
                        resacc.append((prb, 6 + prb))
                        if plast:
                            finish(pi_, resacc, pfidx)
                            resacc = []
                    pend = (i, rel, rb, lastalt, fidx)
                pi_, prel, prb, plast, pfidx = pend
                stage2(pi_, prel, prb)
                resacc.append((prb, 6 + prb))
                finish(pi_, resacc, pfidx)
                S.dma("sp", OAT.ap()[:, t * 512:(t + 1) * 512].rearrange("(h p) t -> p h t", p=64), ost[ob][:],
                      r=[t_ost[ob]], w=[T_OAT[t]])
            S.barrier()

    def phase_C(l):
        with ExitStack() as pes:
            def sb(name, shape, dt):
                return pes.enter_context(nc.sbuf_tensor(name + "_L%d" % l, list(shape), dt))
            NCH = SEG // 128
            diagw = sb("diagw", [128, 60, 128], BF16)
            t_diagw = Trk()
            cw = sb("cw", [128, depth * 60], F32)
            t_cw = Trk()
            S.dma("sp", cw[:], conv_in.ap(), w=[t_cw])
            for j in range(60):
                S.op("dve", lambda j=j: V_.tensor_scalar(out=diagw[:, j, :], in0=ident_f, scalar1=cw[:, l * 60 + j:l * 60 + j + 1],
                                                         scalar2=None, op0=ALU.mult), r=[t_cw, t_cst], w=[t_diagw])
            nexpa = sb("nexpa", [128, 8], F32)
            dtb = sb("dtb_sb", [128, 8], F32)
            dng = sb("dng_sb", [128, 128], F32)
            t_hp = Trk()
            S.dma("sp", nexpa[:], alog_in.ap()[:, l * 8:(l + 1) * 8].partition_broadcast(128), w=[t_hp])
            S.dma("sp", dtb[:], dtb_in.ap()[:, l * 8:(l + 1) * 8].partition_broadcast(128), w=[t_hp])
            S.dma("sp", dng[:], dng_in.ap()[:, l * 128:(l + 1) * 128].partition_broadcast(128), w=[t_hp])
            S.op("act", lambda: A_.activation(out=nexpa[:], in_=nexpa[:], func=AF.Exp), r=[t_hp], w=[t_hp])
            S.op("dve", lambda: V_.tensor_scalar(out=nexpa[:], in0=nexpa[:], scalar1=-1.0, scalar2=None, op0=ALU.mult),
                 r=[t_hp], w=[t_hp])

            rawb = [sb("raw%d" % i, [128, 12, 516], BF16) for i in range(2)]
            t_rawb = [Trk(), Trk()]
            qkv = sb("qkv", [128, 12, SEG], BF16)
            t_qkv = Trk()
            sq = [sb("sqC%d" % i, [128, 512], BF16) for i in range(2)]
            t_sq = [Trk(), Trk()]
            rn = [sb("rnC%d" % i, [128, 512], F32) for i in range(2)]
            t_rn = [Trk(), Trk()]
            ab = sb("abC", [128, NCH, 16], F32)
            t_ab = Trk()
            beta = sb("beta", [128, NCH, 4], F32)
            gg = sb("gg", [128, NCH, 4], F32)
            Gc = sb("Gc", [128, NCH, 4], F32)
            nGc = sb("nGc", [128, NCH, 4], F32)
            eG = sb("eG", [128, NCH, 4], F32)
            negb = sb("negb", [128, NCH, 4], F32)
            Gt = sb("Gt", [128, NCH, 4], F32)
            egt = sb("egt", [128, NCH, 4], F32)
            kds = sb("kds", [128, NCH, 4], F32)
            t_gate = Trk()
            NSLOT = 3
            def slotbufs(k):
                d = {}
                d["gbc"] = sb("gbc%d" % k, [128, 4, 128], F32)
                for nm in ("DT", "DTS", "Pa", "PaT", "Pb0", "Pb1", "PbT0", "PbT1", "XT", "tmpP"):
                    d[nm] = sb("%s_s%d" % (nm, k), [128, 512], BF16)
                for nm in list(d.keys()):
                    d["t_" + nm] = Trk()
                for a_, b_ in (("Off", "Pb0"), ("OffT", "PbT0"), ("T1", "Pb1"), ("T1p", "PbT1")):
                    d[a_] = d[b_]
                    d["t_" + a_] = d["t_" + b_]
                return d
            SL = [slotbufs(k) for k in range(NSLOT)]
            T4b = Trk()
            qkT = sb("qkT", [128, NCH, 512], BF16)
            t_qkT = Trk()
            Yk = sb("Yk", [128, NCH, 512], BF16)
            t_Y = Trk()
            kd = [sb("kd%d" % i, [128, 512], BF16) for i in range(2)]
            t_kd = [Trk(), Trk()]
            vtk = [sb("vtk%d" % i, [128, 512], BF16) for i in range(2)]
            t_vtk = [Trk(), Trk()]
            Sst = sb("Sst", [128, 512], F32)
            Sbf = sb("Sbf", [128, 512], BF16)
            t_S = Trk()
            t_Sbf = Trk()
            tmpc = [sb("tmpc%d" % i, [128, 512], F32) for i in range(2)]
            t_tmpc = [Trk(), Trk()]
            Rr = sb("Rr", [128, 512], BF16)
            t_R = Trk()
            vnew = sb("vnew", [128, 512], BF16)
            t_vnew = Trk()
            och = [sb("och%d" % i, [128, 512], F32) for i in range(2)]
            t_och = [Trk(), Trk()]
            ofl = [sb("ofl%d" % i, [128, 512], F32) for i in range(2)]
            t_ofl = [Trk(), Trk()]
            zt_ = [sb("ztC%d" % i, [128, 512], BF16) for i in range(2)]
            t_zt = [Trk(), Trk()]
            ms = sb("msC", [128, 8], F32)
            t_ms = Trk()
            odt = [sb("odt%d" % i, [128, 4, 128], BF16) for i in range(2)]
            t_odt = [Trk(), Trk()]
            odtok = sb("odtok", [128, 512], BF16)
            t_odtok = Trk()

            def bc4(apx):
                return apx.unsqueeze(2).to_broadcast([128, 4, 128])

            def v4(apx):
                return apx.rearrange("p (h d) -> p h d", h=4)

            def seg_pass(s, dr_):
                t0 = s * SEG
                tiles = [s * 4 + i for i in range(4)]
                if dr_ == 0:
                    k_ = 0
                    for tb in range(4):
                        raw = rawb[tb % 2]
                        t_raw = t_rawb[tb % 2]
                        trk = [T_DQ[s * 4 + tb]]
                        if s * 4 + tb - 1 >= 0:
                            trk.append(T_DQ[s * 4 + tb - 1])
                        if s * 4 + tb + 1 < nt:
                            trk.append(T_DQ[s * 4 + tb + 1])
                        c0_ = 64 + t0 + tb * 512 - 2
                        S.dma("sp", raw[:], DQ.ap()[:, c0_:c0_ + 516].rearrange("(c p) t -> p c t", p=128), r=trk, w=[t_raw])
                        if tb == 0:
                            if s > 0:
                                S.op("dve", lambda raw=raw: V_.tensor_scalar(out=raw[:, :, 0:2], in0=raw[:, :, 0:2], scalar1=flags[:, s - 1:s],
                                                                             scalar2=None, op0=ALU.mult), r=[t_raw, t_flags], w=[t_raw])
                            else:
                                S.op("dve", lambda raw=raw: V_.memset(raw[:, :, 0:2], 0.0), w=[t_raw])
                        if tb == 3:
                            if s + 1 < nseg:
                                S.op("dve", lambda raw=raw: V_.tensor_scalar(out=raw[:, :, 514:516], in0=raw[:, :, 514:516],
                                                                             scalar1=flags[:, s:s + 1], scalar2=None, op0=ALU.mult),
                                     r=[t_raw, t_flags], w=[t_raw])
                            else:
                                S.op("dve", lambda raw=raw: V_.memset(raw[:, :, 514:516], 0.0), w=[t_raw])
                        for ch in range(12):
                            bank = k_ % 2
                            k_ += 1

                            def mm(ch=ch, raw=raw, bank=bank):
                                ins = None
                                for tap in range(5):
                                    ins = P_.matmul(PS[bank][:], lhsT=diagw[:, tap * 12 + ch, :], rhs=raw[:, ch, tap:tap + 512],
                                                    start=(tap == 0), stop=(tap == 4))
                                return ins
                            S.op("pe", mm, r=[t_diagw, t_raw], w=[T_PS[bank]])
                            S.op("act", lambda ch=ch, tb=tb, bank=bank: A_.activation(
                                out=qkv[:, ch, tb * 512:(tb + 1) * 512], in_=PS[bank][:], func=AF.Silu), r=[T_PS[bank]], w=[t_qkv])
                else:
                    S.dma("sp", qkv[:], QN.ap()[:, t0:t0 + SEG].rearrange("(c p) t -> p c t", p=128), r=[T_QN[i] for i in tiles], w=[t_qkv])
                S.dma("sp", ab[:], ABs.ap()[:, s * NCH:(s + 1) * NCH, :], r=[T_AB[i] for i in tiles], w=[t_ab])
                if dr_ == 0:
                    k_ = 0
                    for ch in range(8):
                        for tb in range(4):
                            b2 = k_ % 2
                            bank = 2 + b2
                            k_ += 1
                            sl = qkv[:, ch, tb * 512:(tb + 1) * 512]
                            S.op("act", lambda sl=sl, b2=b2: A_.activation(out=sq[b2][:], in_=sl, func=AF.Square), r=[t_qkv], w=[t_sq[b2]])
                            S.op("pe", lambda b2=b2, bank=bank: P_.matmul(PS[bank][:], lhsT=ones_b, rhs=sq[b2][:], start=True, stop=True),
                                 r=[t_sq[b2], t_cbf], w=[T_PS[bank]])
                            S.op("act", lambda b2=b2, bank=bank: A_.activation(out=rn[b2][:], in_=PS[bank][:], func=AF.Ln,
                                                                              bias=epsD[:, 0:1], scale=1.0),
                                 r=[T_PS[bank], t_eps], w=[t_rn[b2]])
                            if ch < 4:
                                S.op("act", lambda b2=b2: A_.activation(out=rn[b2][:], in_=rn[b2][:], func=AF.Exp, scale=-0.5,
                                                                        bias=epsD[:, 1:2]), r=[t_rn[b2], t_eps], w=[t_rn[b2]])
                            else:
                                S.op("act", lambda b2=b2: A_.activation(out=rn[b2][:], in_=rn[b2][:], func=AF.Exp, scale=-0.5),
                                     r=[t_rn[b2]], w=[t_rn[b2]])
                            S.op("dve", lambda sl=sl, b2=b2: V_.tensor_tensor(out=sl, in0=sl, in1=rn[b2][:], op=ALU.mult),
                                 r=[t_qkv, t_rn[b2]], w=[t_qkv])
                    S.dma("sp", QN.ap()[:, t0:t0 + SEG].rearrange("(c p) t -> p c t", p=128), qkv[:], r=[t_qkv], w=[T_QN[i] for i in tiles])
                bsl = ab[:, :, dr_ * 4:dr_ * 4 + 4]
                asl = ab[:, :, 8 + dr_ * 4:8 + dr_ * 4 + 4]
                hb = lambda tt: tt[:, dr_ * 4:dr_ * 4 + 4].unsqueeze(1).to_broadcast([128, NCH, 4])
                S.op("act", lambda: A_.activation(out=beta[:], in_=bsl, func=AF.Exp, scale=-1.0), r=[t_ab], w=[t_gate])
                S.op("dve", lambda: V_.tensor_scalar(out=beta[:], in0=beta[:], scalar1=1.0, scalar2=None, op0=ALU.add),
                     r=[t_gate], w=[t_gate])
                S.op("dve", lambda: V_.reciprocal(out=beta[:], in_=beta[:]), r=[t_gate], w=[t_gate])
                S.op("dve", lambda: V_.tensor_scalar(out=negb[:], in0=beta[:], scalar1=-1.0, scalar2=None, op0=ALU.mult),
                     r=[t_gate], w=[t_gate])
                S.op("dve", lambda: V_.tensor_tensor(out=gg[:], in0=asl, in1=hb(dtb), op=ALU.add), r=[t_ab, t_hp], w=[t_gate])
                S.op("act", lambda: A_.activation(out=gg[:], in_=gg[:], func=AF.Exp), r=[t_gate], w=[t_gate])
                S.op("act", lambda: A_.activation(out=gg[:], in_=gg[:], func=AF.Ln, bias=1.0, scale=1.0), r=[t_gate], w=[t_gate])
                S.op("dve", lambda: V_.tensor_tensor(out=gg[:], in0=gg[:], in1=hb(nexpa), op=ALU.mult), r=[t_gate, t_hp], w=[t_gate])
                ggf = gg[:].rearrange("p c h -> p (c h)")
                S.op("pe", lambda: P_.matmul(PS[6][:, 0:64], lhsT=TRI[dr_], rhs=ggf, start=True, stop=True),
                     r=[t_gate, t_cst], w=[T_PS[6]])
                S.op("pe", lambda: P_.matmul(PS[7][:, 0:64], lhsT=ones_f[:], rhs=ggf, start=True, stop=True),
                     r=[t_gate, t_onesf], w=[T_PS[7]])
                fl = lambda tt: tt[:].rearrange("p c h -> p (c h)")
                S.op("dve", lambda: V_.tensor_copy(out=fl(Gc), in_=PS[6][:, 0:64]), r=[T_PS[6]], w=[t_gate])
                S.op("dve", lambda: V_.tensor_scalar(out=fl(nGc), in0=PS[6][:, 0:64], scalar1=-1.0, scalar2=None, op0=ALU.mult),
                     r=[T_PS[6]], w=[t_gate])
                S.op("act", lambda: A_.activation(out=fl(eG), in_=PS[6][:, 0:64], func=AF.Exp), r=[T_PS[6]], w=[t_gate])
                S.op("dve", lambda: V_.tensor_copy(out=fl(Gt), in_=PS[7][:, 0:64]), r=[T_PS[7]], w=[t_gate])
                S.op("act", lambda: A_.activation(out=fl(egt), in_=PS[7][:, 0:64], func=AF.Exp), r=[T_PS[7]], w=[t_gate])
                S.op("dve", lambda: V_.tensor_tensor(out=kds[:], in0=Gt[:], in1=Gc[:], op=ALU.subtract), r=[t_gate], w=[t_gate])
                S.op("act", lambda: A_.activation(out=kds[:], in_=kds[:], func=AF.Exp), r=[t_gate], w=[t_gate])

                mk = lambda m: cb2[:, m * 128:(m + 1) * 128].unsqueeze(1).to_broadcast([128, 4, 128])
                idb4 = ident_b.unsqueeze(1).to_broadcast([128, 4, 128])

                def mm4(bank, lh, rh):
                    def f():
                        ins = None
                        for h in range(4):
                            hs = slice(h * 128, (h + 1) * 128)
                            ins = P_.matmul(PS[bank][:, hs], lhsT=lh[:, hs], rhs=rh[:, hs], start=True, stop=True)
                        return ins
                    return f

                def prep_gen(c, k):
                    B = SL[k]
                    bA, bB = 2 * k, 2 * k + 1
                    cs = slice(c * 128, (c + 1) * 128)
                    gbc, DT, DTS, Pa, PaT, XT, T1, T1p, Off, OffT, tmpP = (B[n] for n in (
                        "gbc", "DT", "DTS", "Pa", "PaT", "XT", "T1", "T1p", "Off", "OffT", "tmpP"))
                    Pb = [B["Pb0"], B["Pb1"]]
                    PbT = [B["PbT0"], B["PbT1"]]
                    t_Pb = [B["t_Pb0"], B["t_Pb1"]]
                    t_PbT = [B["t_PbT0"], B["t_PbT1"]]
                    S.op("dve", lambda: V_.tensor_copy(out=gbc[:], in_=bc4(gg[:, c, :])), r=[t_gate], w=[B["t_gbc"]])

                    def mmg():
                        ins = None
                        for h in range(4):
                            P_.matmul(PS[bA][:, h * 128:(h + 1) * 128], lhsT=gbc[:, h, :], rhs=TRI[dr_], start=True, stop=False)
                            ins = P_.matmul(PS[bA][:, h * 128:(h + 1) * 128], lhsT=ident_f, rhs=NEGM[dr_], start=False, stop=True)
                        return ins
                    S.op("pe", mmg, r=[B["t_gbc"], t_cst], w=[T_PS[bA]])

                    def mmkk():
                        ins = None
                        for h in range(4):
                            ins = P_.matmul(PS[bB][:, h * 128:(h + 1) * 128], lhsT=qkv[:, 4 + h, cs], rhs=qkv[:, 4 + h, cs],
                                            start=True, stop=True)
                        return ins
                    S.op("pe", mmkk, r=[t_qkv], w=[T_PS[bB]])
                    yield
                    for h in range(4):
                        S.op("act", lambda h=h: A_.activation(out=DT[:, h * 128:(h + 1) * 128], in_=PS[bA][:, h * 128:(h + 1) * 128],
                                                              func=AF.Exp, bias=nGc[:, c, h:h + 1], scale=1.0),
                             r=[T_PS[bA], t_gate], w=[B["t_DT"]])
                    yield
                    S.op("pool", lambda: G_.tensor_tensor(out=v4(DTS[:]), in0=v4(DT[:]),
                                                          in1=STRICT[dr_].unsqueeze(1).to_broadcast([128, 4, 128]), op=ALU.mult),
                         r=[B["t_DT"], t_cbf], w=[B["t_DTS"]])

                    def mmqk():
                        ins = None
                        for h in range(4):
                            ins = P_.matmul(PS[bA][:, h * 128:(h + 1) * 128], lhsT=qkv[:, 4 + h, cs], rhs=qkv[:, h, cs],
                                            start=True, stop=True)
                        return ins
                    S.op("pe", mmqk, r=[t_qkv], w=[T_PS[bA]])
                    yield
                    S.op("dve", lambda: V_.tensor_tensor(out=qkT[:, c, :], in0=PS[bA][:], in1=DT[:], op=ALU.mult),
                         r=[T_PS[bA], B["t_DT"]], w=[t_qkT])
                    S.op("dve", lambda: V_.tensor_tensor(out=tmpP[:], in0=PS[bB][:], in1=DTS[:], op=ALU.mult),
                         r=[T_PS[bB], B["t_DTS"]], w=[B["t_tmpP"]])
                    yield
                    S.op("dve", lambda: V_.tensor_tensor(out=v4(Pa[:]), in0=v4(tmpP[:]), in1=bc4(negb[:, c, :]), op=ALU.mult),
                         r=[B["t_tmpP"], t_gate], w=[B["t_Pa"]])

                    def tpn():
                        ins = None
                        for h in range(4):
                            ins = P_.transpose(out=psbf(bB)[:, h * 128:(h + 1) * 128], in_=Pa[:, h * 128:(h + 1) * 128],
                                               identity=ident_b)
                        return ins
                    S.op("pe", tpn, r=[B["t_Pa"], t_cbf], w=[T_PS[bB]])
                    yield
                    S.op("act", lambda: A_.copy(out=PaT[:], in_=psbf(bB)[:, 0:512]), r=[T_PS[bB]], w=[B["t_PaT"]])
                    X = Yk[:, c, :]
                    t_X = t_Yc[c]
                    S.op("pool", lambda: G_.tensor_tensor(out=v4(Pb[0][:]), in0=v4(Pa[:]), in1=mk(0), op=ALU.mult),
                         r=[B["t_Pa"], t_cb2], w=[t_Pb[0]])
                    yield
                    S.op("pool", lambda: G_.tensor_tensor(out=v4(PbT[0][:]), in0=v4(PaT[:]), in1=mk(0), op=ALU.mult),
                         r=[B["t_PaT"], t_cb2], w=[t_PbT[0]])
                    S.op("dve", lambda: V_.tensor_tensor(out=v4(X), in0=v4(Pb[0][:]), in1=idb4, op=ALU.add),
                         r=[t_Pb[0], t_cbf], w=[t_X])
                    yield
                    S.op("dve", lambda: V_.tensor_tensor(out=v4(XT[:]), in0=v4(PbT[0][:]), in1=idb4, op=ALU.add),
                         r=[t_PbT[0], t_cbf], w=[B["t_XT"]])
                    cur = 0
                    for st in range(2):
                        nx = 1 - cur
                        S.op("pe", mm4(bA, PbT[cur], Pb[cur]), r=[t_Pb[cur], t_PbT[cur]], w=[T_PS[bA]])
                        S.op("pe", mm4(bB, Pb[cur], PbT[cur]), r=[t_Pb[cur], t_PbT[cur]], w=[T_PS[bB]])
                        yield
                        S.op("act", lambda nx=nx: A_.copy(out=Pb[nx][:], in_=PS[bA][:]), r=[T_PS[bA]], w=[t_Pb[nx]])
                        S.op("act", lambda nx=nx: A_.copy(out=PbT[nx][:], in_=PS[bB][:]), r=[T_PS[bB]], w=[t_PbT[nx]])
                        yield
                        S.op("pe", mm4(bA, PbT[nx], X), r=[t_PbT[nx], t_X], w=[T_PS[bA]])
                        S.op("pe", mm4(bB, Pb[nx], XT), r=[t_Pb[nx], B["t_XT"]], w=[T_PS[bB]])
                        yield
                        S.op("dve", lambda: V_.tensor_tensor(out=X, in0=PS[bA][:], in1=X, op=ALU.add), r=[T_PS[bA], t_X], w=[t_X])
                        S.op("dve", lambda: V_.tensor_tensor(out=XT[:], in0=PS[bB][:], in1=XT[:], op=ALU.add),
                             r=[T_PS[bB], B["t_XT"]], w=[B["t_XT"]])
                        yield
                        cur = nx
                    for lv in range(1, 5):
                        last_lv = (lv == 4)
                        S.op("pool", lambda lv=lv: G_.tensor_tensor(out=v4(OffT[:]), in0=v4(PaT[:]), in1=mk(lv), op=ALU.mult),
                             r=[B["t_PaT"], t_cb2], w=[B["t_OffT"]])
                        if not last_lv:
                            S.op("pool", lambda lv=lv: G_.tensor_tensor(out=v4(Off[:]), in0=v4(Pa[:]), in1=mk(lv), op=ALU.mult),
                                 r=[B["t_Pa"], t_cb2], w=[B["t_Off"]])
                        yield
                        S.op("pe", mm4(bA, OffT, X), r=[B["t_OffT"], t_X], w=[T_PS[bA]])
                        if not last_lv:
                            S.op("pe", mm4(bB, Off, XT), r=[B["t_Off"], B["t_XT"]], w=[T_PS[bB]])
                        yield
                        S.op("act", lambda: A_.copy(out=T1[:], in_=PS[bA][:]), r=[T_PS[bA]], w=[B["t_T1"]])
                        if not last_lv:
                            S.op("act", lambda: A_.copy(out=T1p[:], in_=PS[bB][:]), r=[T_PS[bB]], w=[B["t_T1p"]])
                        yield
                        S.op("pe", mm4(bA, XT, T1), r=[B["t_XT"], B["t_T1"]], w=[T_PS[bA]])
                        if not last_lv:
                            S.op("pe", mm4(bB, X, T1p), r=[t_X, B["t_T1p"]], w=[T_PS[bB]])
                        yield
                        S.op("dve", lambda: V_.tensor_tensor(out=X, in0=PS[bA][:], in1=X, op=ALU.add), r=[T_PS[bA], t_X], w=[t_X])
                        if not last_lv:
                            S.op("dve", lambda: V_.tensor_tensor(out=XT[:], in0=PS[bB][:], in1=XT[:], op=ALU.add),
                                 r=[T_PS[bB], B["t_XT"]], w=[B["t_XT"]])
                        yield

                if dr_ == 0:
                    fi = s - 1 if s > 0 else None
                else:
                    fi = s if s + 1 < nseg else None
                if fi is None:
                    S.op("dve", lambda: V_.memset(Sst[:], 0.0), w=[t_S])
                else:
                    S.op("dve", lambda fi=fi: V_.tensor_scalar(out=Sst[:], in0=Sst[:], scalar1=flags[:, fi:fi + 1], scalar2=None,
                                                               op0=ALU.mult), r=[t_S, t_flags], w=[t_S])
                S.op("act", lambda: A_.copy(out=Sbf[:], in_=Sst[:]), r=[t_S], w=[t_Sbf])

                order = list(range(NCH)) if dr_ == 0 else list(range(NCH - 1, -1, -1))

                def scan_gen(n_, c):
                    cs = slice(c * 128, (c + 1) * 128)
                    ob = n_ % 2
                    tg = s * 4 + c // 4
                    t_X = t_Yc[c]
                    if dr_ == 1:
                        S.dma("sp", ofl[ob][:], OF.ap()[t0 + c * 128:t0 + (c + 1) * 128, :], r=[T_OF[tg]], w=[t_ofl[ob]])
                        S.dma("sp", zt_[ob][:], ZZ.ap()[t0 + c * 128:t0 + (c + 1) * 128, :], r=[T_ZZ[tg]], w=[t_zt[ob]])

                    def tpk():
                        ins = None
                        for h in range(4):
                            ins = P_.transpose(out=psbf(7)[:, h * 128:(h + 1) * 128], in_=qkv[:, 4 + h, cs], identity=ident_b)
                        return ins

                    def tpv():
                        ins = None
                        for h in range(4):
                            ins = P_.transpose(out=psbf(7)[:, 512 + h * 128:512 + (h + 1) * 128], in_=qkv[:, 8 + h, cs], identity=ident_b)
                        return ins

                    def tpkv():
                        tpk()
                        return tpv()
                    S.op("pe", tpkv, r=[t_qkv, t_cbf], w=[T_PS[7]])
                    yield
                    S.op("dve", lambda: V_.tensor_tensor(out=v4(kd[ob][:]), in0=v4(psbf(7)[:, 0:512]), in1=bc4(kds[:, c, :]),
                                                         op=ALU.mult), r=[T_PS[7], t_gate], w=[t_kd[ob]])
                    S.op("dve", lambda: V_.tensor_copy(out=vtk[ob][:], in_=psbf(7)[:, 512:1024]), r=[T_PS[7]], w=[t_vtk[ob]])
                    yield

                    def mmz():
                        ins = None
                        for h in range(4):
                            hs = slice(h * 128, (h + 1) * 128)
                            ins = P_.matmul(PS[6][:, hs], lhsT=qkv[:, 4 + h, cs], rhs=Sbf[:, hs], start=True, stop=True)
                        return ins
                    S.op("pe", mmz, r=[t_qkv, t_Sbf], w=[T_PS[6]])

                    def mmp1():
                        ins = None
                        for h in range(4):
                            hs = slice(h * 128, (h + 1) * 128)
                            ins = P_.matmul(PS[7][:, hs], lhsT=qkv[:, h, cs], rhs=Sbf[:, hs], start=True, stop=True)
                        return ins
                    S.op("pe", mmp1, r=[t_qkv, t_Sbf], w=[T_PS[7]])
                    yield
                    S.op("dve", lambda: V_.tensor_tensor(out=v4(tmpc[0][:]), in0=v4(PS[6][:]), in1=bc4(eG[:, c, :]), op=ALU.mult),
                         r=[T_PS[6], t_gate], w=[t_tmpc[0]])
                    yield
                    S.op("dve", lambda: V_.tensor_tensor(out=Rr[:], in0=vtk[ob][:], in1=tmpc[0][:], op=ALU.subtract),
                         r=[t_vtk[ob], t_tmpc[0]], w=[t_R])
                    yield

                    def mmv():
                        ins = None
                        for h in range(4):
                            hs = slice(h * 128, (h + 1) * 128)
                            ins = P_.matmul(PS[6][:, hs], lhsT=Yk[:, c, hs], rhs=Rr[:, hs], start=True, stop=True)
                        return ins
                    S.op("pe", mmv, r=[t_X, t_R], w=[T_PS[6]])
                    yield
                    S.op("act", lambda: A_.activation(out=tmpc[1][:], in_=PS[7][:], func=AF.Copy), r=[T_PS[7]], w=[t_tmpc[1]])
                    S.op("dve", lambda: V_.tensor_tensor(out=v4(vnew[:]), in0=v4(PS[6][:]), in1=bc4(beta[:, c, :]), op=ALU.mult),
                         r=[T_PS[6], t_gate], w=[t_vnew])
                    yield

                    def mmp2():
                        ins = None
                        for h in range(4):
                            hs = slice(h * 128, (h + 1) * 128)
                            ins = P_.matmul(PS[7][:, hs], lhsT=qkT[:, c, hs], rhs=vnew[:, hs], start=True, stop=True)
                        return ins
                    S.op("pe", mmp2, r=[t_qkT, t_vnew], w=[T_PS[7]])

                    def mms():
                        ins = None
                        for h in range(4):
                            hs = slice(h * 128, (h + 1) * 128)
                            ins = P_.matmul(PS[6][:, hs], lhsT=kd[ob][:, hs], rhs=vnew[:, hs], start=True, stop=True)
                        return ins
                    S.op("pe", mms, r=[t_kd[ob], t_vnew], w=[T_PS[6]])
                    yield
                    S.op("dve", lambda: V_.tensor_tensor(out=v4(Sst[:]), in0=v4(Sst[:]), in1=bc4(egt[:, c, :]), op=ALU.mult),
                         r=[t_S, t_gate], w=[t_S])
                    yield
                    S.op("dve", lambda: V_.tensor_tensor(out=Sst[:], in0=PS[6][:], in1=Sst[:], op=ALU.add), r=[T_PS[6], t_S], w=[t_S])
                    S.op("act", lambda: A_.copy(out=Sbf[:], in_=Sst[:]), r=[t_S], w=[t_Sbf])
                    yield
                    S.op("pool", lambda: G_.tensor_tensor(out=v4(tmpc[1][:]), in0=v4(tmpc[1][:]), in1=bc4(eG[:, c, :]), op=ALU.mult),
                         r=[t_tmpc[1], t_gate], w=[t_tmpc[1]])
                    yield
                    S.op("dve", lambda: V_.tensor_tensor(out=och[ob][:], in0=PS[7][:], in1=tmpc[1][:], op=ALU.add),
                         r=[T_PS[7], t_tmpc[1]], w=[t_och[ob]])
                    yield
                    if dr_ == 0:
                        S.dma("sp", OF.ap()[t0 + c * 128:t0 + (c + 1) * 128, :], och[ob][:], r=[t_och[ob]], w=[T_OF[tg]])
                    else:
                        S.op("dve", lambda: V_.tensor_tensor(out=och[ob][:], in0=och[ob][:], in1=ofl[ob][:], op=ALU.add),
                             r=[t_och[ob], t_ofl[ob]], w=[t_och[ob]])
                        yield
                        for h in range(4):
                            S.op("act", lambda h=h: A_.activation(out=tmpc[1][:, h * 128:(h + 1) * 128],
                                                                  in_=och[ob][:, h * 128:(h + 1) * 128], func=AF.Square,
                                                                  accum_out=ms[:, h:h + 1]),
                                 r=[t_och[ob]], w=[t_tmpc[1], t_ms])
                        yield
                        S.op("act", lambda: A_.activation(out=ms[:, 4:8], in_=ms[:, 0:4], func=AF.Ln, bias=epsD[:, 0:1], scale=1.0 / 128),
                             r=[t_ms, t_eps], w=[t_ms])
                        S.op("act", lambda: A_.activation(out=ms[:, 4:8], in_=ms[:, 4:8], func=AF.Exp, scale=-0.5), r=[t_ms], w=[t_ms])
                        yield
                        S.op("dve", lambda: V_.tensor_tensor(out=v4(och[ob][:]), in0=v4(och[ob][:]), in1=bc4(ms[:, 4:8]), op=ALU.mult),
                             r=[t_och[ob], t_ms], w=[t_och[ob]])
                        S.op("act", lambda: A_.activation(out=tmpc[0][:], in_=zt_[ob][:], func=AF.Silu), r=[t_zt[ob]], w=[t_tmpc[0]])
                        yield
                        S.op("pool", lambda: G_.tensor_tensor(out=v4(tmpc[0][:]), in0=v4(tmpc[0][:]),
                                                              in1=dng[:].unsqueeze(1).to_broadcast([128, 4, 128]), op=ALU.mult),
                             r=[t_tmpc[0], t_hp], w=[t_tmpc[0]])
                        yield
                        S.op("dve", lambda: V_.tensor_tensor(out=odtok[:], in0=och[ob][:], in1=tmpc[0][:], op=ALU.mult),
                             r=[t_och[ob], t_tmpc[0]], w=[t_odtok])
                        yield

                        def tpo():
                            ins = None
                            for h in range(4):
                                ins = P_.transpose(out=psbf(7)[:, h * 128:(h + 1) * 128], in_=odtok[:, h * 128:(h + 1) * 128],
                                                   identity=ident_b)
                            return ins
                        S.op("pe", tpo, r=[t_odtok, t_cbf], w=[T_PS[7]])
                        yield
                        S.op("act", lambda: A_.copy(out=odt[ob][:], in_=psbf(7)[:, 0:512].rearrange("p (h t) -> p h t", h=4)),
                             r=[T_PS[7]], w=[t_odt[ob]])
                        S.dma("sp", ODT.ap()[:, t0 + c * 128:t0 + (c + 1) * 128].rearrange("(h p) t -> p h t", p=128), odt[ob][:],
                              r=[t_odt[ob]], w=[T_ODT[tg]])

                t_Yc = [Trk() for _ in range(NCH)]
                prep_q = list(order)
                active = {}
                prep_done = set()
                scan_n = 0
                scan_g = None
                while scan_n < NCH:
                    for k in range(NSLOT):
                        if k not in active and prep_q:
                            c_ = prep_q.pop(0)
                            active[k] = (prep_gen(c_, k), c_)
                    if scan_g is None and order[scan_n] in prep_done:
                        scan_g = scan_gen(scan_n, order[scan_n])
                    for k in list(active.keys()):
                        g_, c_ = active[k]
                        try:
                            next(g_)
                        except StopIteration:
                            prep_done.add(c_)
                            del active[k]
                    if scan_g is not None:
                        try:
                            next(scan_g)
                        except StopIteration:
                            scan_g = None
                            scan_n += 1

            for s in range(nseg):
                seg_pass(s, 0)
            for s in range(nseg - 1, -1, -1):
                seg_pass(s, 1)
            S.barrier()

    def phase_D(l):
        with ExitStack() as pes:
            def sb(name, shape, dt):
                return pes.enter_context(nc.sbuf_tensor(name + "_L%d" % l, list(shape), dt))
            wba = sb("wba", [128, 4, D], BF16)
            wbd = sb("wbd", [128, 4, D], BF16)
            wo = sb("wo", [128, 8, D], BF16)
            t_wba, t_wbd, t_wo = Trk(), Trk(), Trk()
            S.dma("pool", wba[:], wbra_in.ap()[l].rearrange("(k p) c -> p k c", p=128), w=[t_wba])
            S.dma("pool", wbd[:], wbrd_in.ap()[l].rearrange("(k p) c -> p k c", p=128), w=[t_wbd])
            S.dma("pool", wo[:], wout_in.ap()[l].rearrange("(k p) c -> p k c", p=128), w=[t_wo])
            oa = [sb("oa%d" % i, [128, 4, 512], BF16) for i in range(2)]
            od = [sb("od%d" % i, [128, 4, 512], BF16) for i in range(2)]
            gt = [sb("gtD%d" % i, [128, 16, 512], BF16) for i in range(2)]
            xT = [sb("xTD%d" % i, [128, 8, 512], F32) for i in range(2)]
            t_in = [Trk(), Trk()]
            t_x = [Trk(), Trk()]
            sg = [sb("sgD%d" % i, [128, 512], F32) for i in range(2)]
            t_sg = [Trk(), Trk()]
            m1 = [sb("m1D%d" % i, [128, 512], F32) for i in range(2)]
            t_m1 = [Trk(), Trk()]
            mg = sb("mgD", [128, 8, 512], BF16)
            t_mg = Trk()

            def loads(t):
                b = t % 2
                tk = slice(t * 512, (t + 1) * 512)
                S.dma("sp", oa[b][:], OAT.ap()[:, tk].rearrange("(c p) t -> p c t", p=128), r=[T_OAT[t]], w=[t_in[b]])
                S.dma("sp", od[b][:], ODT.ap()[:, tk].rearrange("(c p) t -> p c t", p=128), r=[T_ODT[t]], w=[t_in[b]])
                S.dma("sp", gt[b][:], GT.ap()[:, tk].rearrange("(c p) t -> p c t", p=128), r=[T_GT[t]], w=[t_in[b]])
                S.dma("sp", xT[b][:], XT.ap()[:, tk].rearrange("(c p) t -> p c t", p=128), r=[T_XT[t]], w=[t_x[b]])

            loads(0)
            for t in range(nt):
                b = t % 2
                seg = t // 4
                if t + 1 < nt:
                    loads(t + 1)
                for c in range(8):
                    q2 = c % 2
                    ba, bd = 2 * q2, 2 * q2 + 1

                    def mma(c=c, ba=ba):
                        ins = None
                        for k in range(4):
                            ins = P_.matmul(PS[ba][:], lhsT=wba[:, k, c * 128:(c + 1) * 128], rhs=oa[b][:, k, :], start=(k == 0), stop=(k == 3))
                        return ins
                    S.op("pe", mma, r=[t_wba, t_in[b]], w=[T_PS[ba]])

                    def mmd(c=c, bd=bd):
                        ins = None
                        for k in range(4):
                            ins = P_.matmul(PS[bd][:], lhsT=wbd[:, k, c * 128:(c + 1) * 128], rhs=od[b][:, k, :], start=(k == 0), stop=(k == 3))
                        return ins
                    S.op("pe", mmd, r=[t_wbd, t_in[b]], w=[T_PS[bd]])
                    S.op("act", lambda c=c: A_.activation(out=sg[0][:], in_=gt[b][:, c, :], func=AF.Sigmoid), r=[t_in[b]], w=[t_sg[0]])
                    S.op("act", lambda c=c: A_.activation(out=sg[1][:], in_=gt[b][:, 8 + c, :], func=AF.Sigmoid), r=[t_in[b]], w=[t_sg[1]])
                    S.op("dve", lambda ba=ba, q2=q2: V_.tensor_tensor(out=m1[q2][:], in0=PS[ba][:], in1=sg[0][:], op=ALU.mult),
                         r=[T_PS[ba], t_sg[0]], w=[t_m1[q2]])
                    S.op("dve", lambda bd=bd: V_.tensor_tensor(out=sg[1][:], in0=PS[bd][:], in1=sg[1][:], op=ALU.mult),
                         r=[T_PS[bd], t_sg[1]], w=[t_sg[1]])
                    S.op("pool", lambda c=c, q2=q2: G_.tensor_tensor(out=mg[:, c, :], in0=m1[q2][:], in1=sg[1][:], op=ALU.add),
                         r=[t_m1[q2], t_sg[1]], w=[t_mg])
                for c in range(8):
                    bank = 4 + (c % 4)

                    def mmo(c=c, bank=bank):
                        ins = None
                        for k in range(8):
                            ins = P_.matmul(PS[bank][:], lhsT=wo[:, k, c * 128:(c + 1) * 128], rhs=mg[:, k, :], start=(k == 0), stop=(k == 7))
                        return ins
                    S.op("pe", mmo, r=[t_wo, t_mg], w=[T_PS[bank]])
                    S.op("dve", lambda c=c, bank=bank: V_.scalar_tensor_tensor(
                        out=xT[b][:, c, :], in0=PS[bank][:], scalar=modv(l, 2, c, seg), in1=xT[b][:, c, :], op0=ALU.mult, op1=ALU.add),
                        r=[T_PS[bank], t_x[b], t_mod], w=[t_x[b]])
                S.dma("sp", XT.ap()[:, t * 512:(t + 1) * 512].rearrange("(c p) t -> p c t", p=128), xT[b][:], r=[t_x[b]], w=[T_XT[t]])
            S.barrier()

    def phase_E(l, last):
        TE = 256
        nte = ntok // TE
        with ExitStack() as pes:
            def sb(name, shape, dt):
                return pes.enter_context(nc.sbuf_tensor(name + "_L%d" % l, list(shape), dt))
            w1 = sb("w1", [128, 8, DFF], BF16)
            w2 = sb("w2", [128, 32, D], BF16)
            t_w1p = [Trk() for _ in range(4)]
            t_w2p = [Trk() for _ in range(4)]
            for cp in range(4):
                S.dma("pool", w1[:, :, cp * 1024:(cp + 1) * 1024], w1_in.ap()[l, :, cp * 1024:(cp + 1) * 1024].rearrange("(k p) c -> p k c", p=128),
                      w=[t_w1p[cp]])
            for kp in range(4):
                S.dma("pool", w2[:, kp * 8:(kp + 1) * 8, :], w2_in.ap()[l, kp * 1024:(kp + 1) * 1024, :].rearrange("(k p) c -> p k c", p=128),
                      w=[t_w2p[kp]])
            xT = [sb("xTE%d" % i, [128, 8, TE], F32) for i in range(2)]
            t_x = [Trk(), Trk()]
            nb_ = 1 if last else 2
            xsq2 = [sb("xsqE%d" % i, [128, 8, TE], BF16) for i in range(nb_)] * (2 // nb_)
            t_xsq2 = [Trk() for _ in range(nb_)] * (2 // nb_)
            hT2 = [sb("hTE%d" % i, [128, 8, TE], BF16) for i in range(nb_)] * (2 // nb_)
            t_h2 = [Trk() for _ in range(nb_)] * (2 // nb_)
            xsq = xsq2[0]
            t_xsq = t_xsq2[0]
            rstd = sb("rstdE", [128, TE], F32)
            t_rstd = Trk()
            tmp = [sb("tmpE%d" % i, [128, TE], F32) for i in range(2)]
            t_tmp = [Trk(), Trk()]
            rl = [sb("rlE%d" % i, [128, 2, TE], BF16) for i in range(2)]
            t_rl = [Trk(), Trk()]
            hid = sb("hidE", [128, 32, TE], BF16)
            t_hid = Trk()
            if last:
                yT = sb("yTE", [128, 8, TE], F32)
                t_yT = Trk()
                ytok = sb("ytokE", [128, 2, D], F32)
                t_ytok = Trk()

            def loads(t):
                b = t % 2
                S.dma("sp", xT[b][:], XT.ap()[:, t * TE:(t + 1) * TE].rearrange("(c p) t -> p c t", p=128), r=[T_XT[t // 2]], w=[t_x[b]])

            def prep(t):
                b = t % 2
                norm_mod(xT[b], t_x[b], xsq2[b], t_xsq2[b], hT2[b], t_h2[b], rstd, t_rstd, tmp, t_tmp, 2, A2, l, 3, (t * TE) // SEG, 0, TE)

            loads(0)
            prep(0)
            for t in range(nte):
                b = t % 2
                seg = (t * TE) // SEG
                if t + 1 < nte:
                    loads(t + 1)
                hT = hT2[b]
                t_h = t_h2[b]
                for c2 in range(16):
                    bank = 1 + (c2 % 3)
                    r2 = c2 % 2

                    def mm1(c2=c2, bank=bank):
                        ins = None
                        for j in range(2):
                            cc = (c2 * 2 + j) * 128
                            for k in range(8):
                                ins = P_.matmul(PS[bank][:, j * TE:(j + 1) * TE], lhsT=w1[:, k, cc:cc + 128], rhs=hT[:, k, :],
                                                start=(k == 0), stop=(k == 7))
                        return ins
                    S.op("pe", mm1, r=[t_w1p[(c2 * 256) // 1024], t_h], w=[T_PS[bank]])
                    S.op("act", lambda bank=bank, r2=r2: A_.activation(out=rl[r2][:].rearrange("p j t -> p (j t)"), in_=PS[bank][:],
                                                                       func=AF.Relu), r=[T_PS[bank]], w=[t_rl[r2]])
                    S.op("dve", lambda c2=c2, r2=r2: V_.tensor_tensor(out=hid[:, c2 * 2:c2 * 2 + 2, :], in0=rl[r2][:], in1=rl[r2][:], op=ALU.mult),
                         r=[t_rl[r2]], w=[t_hid])
                    if c2 == 11 and t + 1 < nte and not last:
                        prep(t + 1)
                if last and t + 1 < nte:
                    pass
                for c2 in range(4):
                    bank = 4 + (c2 % 4)

                    def mm2(c2=c2, bank=bank):
                        ins = None
                        for j in range(2):
                            cc = (c2 * 2 + j) * 128
                            for k in range(32):
                                ins = P_.matmul(PS[bank][:, j * TE:(j + 1) * TE], lhsT=w2[:, k, cc:cc + 128], rhs=hid[:, k, :],
                                                start=(k == 0), stop=(k == 31))
                        return ins
                    S.op("pe", mm2, r=t_w2p + [t_hid], w=[T_PS[bank]])
                    for j in range(2):
                        c = c2 * 2 + j
                        S.op("dve", lambda c=c, j=j, bank=bank: V_.scalar_tensor_tensor(
                            out=xT[b][:, c, :], in0=PS[bank][:, j * TE:(j + 1) * TE], scalar=modv(l, 5, c, seg), in1=xT[b][:, c, :],
                            op0=ALU.mult, op1=ALU.add), r=[T_PS[bank], t_x[b], t_mod], w=[t_x[b]])
                if not last:
                    S.dma("sp", XT.ap()[:, t * TE:(t + 1) * TE].rearrange("(c p) t -> p c t", p=128), xT[b][:], r=[t_x[b]], w=[T_XT[t // 2]])
                else:
                    for dch in range(8):
                        S.op("act", lambda dch=dch: A_.activation(out=xsq[:, dch, :], in_=xT[b][:, dch, :], func=AF.Square),
                             r=[t_x[b]], w=[t_xsq])

                    def mmf():
                        ins = None
                        for dch in range(8):
                            ins = P_.matmul(PS[0][:, 0:TE], lhsT=ones_b, rhs=xsq[:, dch, :], start=(dch == 0), stop=(dch == 7))
                        return ins
                    S.op("pe", mmf, r=[t_xsq, t_cbf], w=[T_PS[0]])
                    S.op("act", lambda: A_.activation(out=rstd[:], in_=PS[0][:, 0:TE], func=AF.Ln, bias=epsD[:, 0:1], scale=1.0 / D),
                         r=[T_PS[0], t_eps], w=[t_rstd])
                    S.op("act", lambda: A_.activation(out=rstd[:], in_=rstd[:], func=AF.Exp, scale=-0.5), r=[t_rstd], w=[t_rstd])
                    for dch in range(8):
                        S.op("dve", lambda dch=dch: V_.scalar_tensor_tensor(
                            out=yT[:, dch, :], in0=xT[b][:, dch, :], scalar=gf[:, dch:dch + 1], in1=rstd[:], op0=ALU.mult, op1=ALU.mult),
                            r=[t_x[b], t_rstd, t_par], w=[t_yT])
                    for blk in range(TE // 128):
                        for half in range(2):
                            bank = 1 + half

                            def tpf(blk=blk, half=half, bank=bank):
                                ins = None
                                for j in range(4):
                                    dch = half * 4 + j
                                    ins = P_.transpose(out=PS[bank][:, j * 128:(j + 1) * 128], in_=yT[:, dch, blk * 128:(blk + 1) * 128],
                                                       identity=ident_f)
                                return ins
                            S.op("pe", tpf, r=[t_yT, t_cst], w=[T_PS[bank]])
                            if half == 0:
                                S.op("act", lambda blk=blk, bank=bank: A_.copy(out=ytok[:, blk, 0:512], in_=PS[bank][:]), r=[T_PS[bank]], w=[t_ytok])
                            else:
                                S.op("dve", lambda blk=blk, bank=bank: V_.tensor_copy(out=ytok[:, blk, 512:1024], in_=PS[bank][:]),
                                     r=[T_PS[bank]], w=[t_ytok])
                    S.dma("sp", y_out.ap()[t * TE:(t + 1) * TE, :].rearrange("(b p) d -> p b d", p=128), ytok[:], r=[t_ytok], w=[Trk()])
                    if t + 1 < nte:
                        prep(t + 1)
            S.barrier()

    phases_all = phases
    for l in range(depth):
        phases = phases_last if (phases_last is not None and l == depth - 1) else phases_all
        if "A" in phases:
            phase_A(l)
        if "B" in phases:
            phase_B(l)
        if "C" in phases:
            phase_C(l)
        if "D" in phases:
            phase_D(l)
        if "E" in phases:
            phase_E(l, last=(l == depth - 1))
    S.barrier()
    es.close()
    return nc, S


def make_consts():
    c = np.zeros((128, 1024), np.float32)
    p = np.arange(128)[:, None]
    i = np.arange(128)[None, :]
    c[:, 0:128] = (p == i)
    c[:, 128:256] = (p <= i)
    c[:, 256:384] = (p >= i)
    c[:, 384:512] = np.where(p > i, NEGBIG, 0.0)
    c[:, 512:640] = np.where(p < i, NEGBIG, 0.0)
    c[:, 640:768] = (p < i)
    c[:, 768:896] = (p > i)
    jj = np.arange(64)
    c[0:64, 896:960] = (jj[:, None] + jj[None, :] == 63)
    qc = np.arange(64)
    qs = np.clip(qc - 8, 0, 48)
    kc = np.arange(64)
    valid = (kc[:, None] >= qs[None, :]) & (kc[:, None] < qs[None, :] + 16)
    c[0:64, 960:1024] = valid
    c[64:128, 960:1024] = valid
    return c


def make_consts2():
    c = np.zeros((128, 640), np.float32)
    p = np.arange(128)[:, None]
    i = np.arange(128)[None, :]
    prev = (p // 8 == i // 8)
    c[:, 0:128] = prev
    for m, b in enumerate((16, 32, 64, 128)):
        cur = (p // b == i // b)
        c[:, (m + 1) * 128:(m + 2) * 128] = cur & ~prev
        prev = cur
    return c


def pp(v, nchunk):
    v = np.asarray(v, np.float32)
    lead = v.shape[:-1]
    v = v.reshape(lead + (nchunk, 128))
    v = np.moveaxis(v, -1, 0)
    return np.ascontiguousarray(v.reshape(128, -1))


def core_plan():
    plan = []
    for c in range(NCORES):
        if c < 2:
            segs = [("p", c, i * SEG) for i in range(4)] + [("s", c, 0)]
            flags = [1, 1, 1, 0, 0, 0, 0, 0]
        else:
            segs = [("s", 2 + (c - 2) * 5 + i, 0) for i in range(5)]
            flags = [0] * 8
        plan.append((segs, flags))
    return plan


def shared_inputs(norm_mix_g, norm_mlp_g, w_ada, b_ada, w_in, na_rpb, dn_conv, dn_a_log, dn_dt_bias, dn_norm_g,
                  w_br_attn, w_br_dn, w_out, w_mlp1, w_mlp2, final_norm_g, depth=DEPTH):
    f = lambda a: np.ascontiguousarray(np.asarray(a, np.float32))
    rp = np.zeros((depth, 8, 15, 128), np.float32)
    rp[:, :, :, 48:79] = np.asarray(na_rpb, np.float32)[:depth]
    cw = np.asarray(dn_conv, np.float32)[:depth].reshape(depth, 5, 12, 128)
    cw = np.ascontiguousarray(np.moveaxis(cw, -1, 0).reshape(128, depth * 60))
    return {
        "g1": pp(np.asarray(norm_mix_g)[:depth], 8), "g2": pp(np.asarray(norm_mlp_g)[:depth], 8), "gf": pp(final_norm_g, 8),
        "bada": pp(np.asarray(b_ada)[:depth], 48), "w_ada": f(w_ada)[:depth], "w_in": f(w_in)[:depth], "rpbp": rp, "convw": cw,
        "alog": f(dn_a_log)[:depth].reshape(1, depth * 8), "dtb": f(dn_dt_bias)[:depth].reshape(1, depth * 8),
        "dng": f(dn_norm_g)[:depth].reshape(1, depth * 128),
        "w_br_attn": f(w_br_attn)[:depth], "w_br_dn": f(w_br_dn)[:depth], "w_out": f(w_out)[:depth],
        "w_mlp1": f(w_mlp1)[:depth], "w_mlp2": f(w_mlp2)[:depth], "consts": make_consts(), "consts2": make_consts2(),
    }


def core_inputs(segs, flags, x_prompt, x_sample, c_prompt, c_sample):
    xs, cs = [], []
    for (g, b, st) in segs:
        if g == "p":
            xs.append(np.asarray(x_prompt[b, st:st + SEG], np.float32))
            cs.append(np.asarray(c_prompt[b], np.float32))
        else:
            xs.append(np.asarray(x_sample[b, st:st + SEG], np.float32))
            cs.append(np.asarray(c_sample[b], np.float32))
    x = np.ascontiguousarray(np.concatenate(xs, axis=0))
    cm = np.zeros((8, D), np.float32)
    cm[:len(cs)] = np.stack(cs)
    cT = np.ascontiguousarray(cm.reshape(8, 8, 128).transpose(2, 1, 0))
    return {"x": x, "cT": cT, "flags": np.asarray(flags, np.float32).reshape(1, 8)}


def kernel(x_prompt, x_sample, c_prompt, c_sample, norm_mix_g, norm_mlp_g, w_ada, b_ada, w_in, na_rpb, dn_conv,
           dn_a_log, dn_dt_bias, dn_norm_g, w_br_attn, w_br_dn, w_out, w_mlp1, w_mlp2, final_norm_g):
    x_prompt = np.asarray(x_prompt)
    x_sample = np.asarray(x_sample)
    c_prompt = np.asarray(c_prompt)
    c_sample = np.asarray(c_sample)
    nc, _ = build_program()
    shared = shared_inputs(norm_mix_g, norm_mlp_g, w_ada, b_ada, w_in, na_rpb, dn_conv, dn_a_log, dn_dt_bias, dn_norm_g,
                           w_br_attn, w_br_dn, w_out, w_mlp1, w_mlp2, final_norm_g)
    plan = core_plan()
    in_maps = []
    for (segs, flags) in plan:
        m = dict(shared)
        m.update(core_inputs(segs, flags, x_prompt, x_sample, c_prompt, c_sample))
        in_maps.append(m)
    res = run_bass_kernel_spmd(nc, in_maps, core_ids=list(range(NCORES)))
    y_prompt = np.zeros(x_prompt.shape, np.float32)
    y_sample = np.zeros(x_sample.shape, np.float32)
    for ci, (segs, flags) in enumerate(plan):
        y = np.asarray(res.results[ci]["y"])
        for si, (g, b, st) in enumerate(segs):
            blk = y[si * SEG:(si + 1) * SEG]
            if g == "p":
                y_prompt[b, st:st + SEG] = blk
            else:
                y_sample[b] = blk
    return (y_prompt, y_sample)
```

```python
import os
import numpy as np
import ml_dtypes
from contextlib import ExitStack
import concourse.bass as bass
import concourse.mybir as mybir
from concourse.bass_utils import run_bass_kernel_spmd

F32 = mybir.dt.float32
BF16 = mybir.dt.bfloat16
AF = mybir.ActivationFunctionType
ALU = mybir.AluOpType

D = 1024
DEPTH = 2
NCORES = 8
SEG = 2048
IN_COLS = 5648
DFF = 4096
EPS = 1e-6
PADT = 256
NEGBIG = -30000.0

C_NQ, C_NK, C_NV = 0, 512, 1024
C_DQ = 1536
C_Z = 3072
C_AB = 3584
C_G = 3600


class Trk:
    __slots__ = ("w", "r", "excl")

    def __init__(self, excl=False):
        self.w = None
        self.r = []
        self.excl = excl


class Sched:
    def __init__(self, nc, es):
        self.nc = nc
        self.eng = {"pe": nc.tensor, "act": nc.scalar, "dve": nc.vector, "pool": nc.gpsimd, "sp": nc.sync}
        self.sem = {}
        self.cnt = {}
        for e in self.eng:
            self.sem[e] = es.enter_context(nc.semaphore("sem_" + e))
            self.cnt[e] = 0
        self.waited = {e: {} for e in self.eng}
        self.nslot = {"sp": 12, "pool": 6}
        self.slots = {}
        for q, n in self.nslot.items():
            self.slots[q] = []
            for i in range(n):
                key = "dq_%s_%d" % (q, i)
                self.sem[key] = es.enter_context(nc.semaphore(key))
                self.cnt[key] = 0
                self.slots[q].append(key)
        self.rr = {q: 0 for q in self.nslot}
        self.nwait = 0
        self.nops = 0

    def _wait(self, e, deps):
        best = {}
        for d in deps:
            if d is None:
                continue
            k, v = d
            if best.get(k, 0) < v:
                best[k] = v
        for k, v in best.items():
            if self.waited[e].get(k, 0) < v:
                self.eng[e].wait_ge(self.sem[k], v)
                self.waited[e][k] = v
                self.nwait += 1

    def _deps(self, e, r, w):
        deps = []
        for t in r:
            deps.append(t.w)
            if t.excl:
                for rd in t.r:
                    if rd[0] != e:
                        deps.append(rd)
        for t in w:
            deps.append(t.w)
            for rd in t.r:
                if rd[0] != e:
                    deps.append(rd)
        return deps

    def _stamp(self, st, r, w):
        for t in r:
            t.r.append(st)
        for t in w:
            t.w = st
            t.r = []

    def op(self, e, fn, r=(), w=()):
        self._wait(e, self._deps(e, r, w))
        ins = fn()
        self.cnt[e] += 1
        ins.then_inc(self.sem[e], 1)
        self._stamp((e, self.cnt[e]), r, w)
        self.nops += 1

    def dma(self, q, out, in_, r=(), w=(), **kw):
        key = self.slots[q][self.rr[q]]
        self.rr[q] = (self.rr[q] + 1) % self.nslot[q]
        deps = self._deps(q, r, w)
        deps.append((key, self.cnt[key]))
        self._wait(q, deps)
        self.eng[q].dma_start(out=out, in_=in_, **kw).then_inc(self.sem[key], 16)
        self.cnt[key] += 16
        self._stamp((key, self.cnt[key]), r, w)
        self.nops += 1

    def barrier(self):
        allst = [(k, v) for k, v in self.cnt.items() if v > 0]
        for e in self.eng:
            self._wait(e, [d for d in allst if d[0] != e])


def _bf(a):
    return np.asarray(a, dtype=np.float32)


def build_program(nseg=5, depth=DEPTH, debug=False, phases="ABCDE", phases_last=None):
    ntok = nseg * SEG
    nt = ntok // 512
    nc = bass.Bass("TRN2", target_bir_lowering=False)
    es = ExitStack()
    S = Sched(nc, es)

    def din(name, shape, dt=F32):
        return nc.dram_tensor(name, list(shape), dt, kind="ExternalInput")

    okind = "ExternalOutput" if debug else "Internal"

    def dscr(name, shape, dt):
        return nc.dram_tensor(name, list(shape), dt, kind=okind)

    x_in = din("x", [ntok, D])
    cT_in = din("cT", [128, 8, 8])
    flags_in = din("flags", [1, 8])
    g1_in = din("g1", [128, depth * 8])
    g2_in = din("g2", [128, depth * 8])
    gf_in = din("gf", [128, 8])
    bada_in = din("bada", [128, depth * 48])
    wada_in = din("w_ada", [depth, D, 6 * D])
    win_in = din("w_in", [depth, D, IN_COLS])
    rpb_in = din("rpbp", [depth, 8, 15, 128])
    conv_in = din("convw", [128, depth * 5 * 12])
    alog_in = din("alog", [1, depth * 8])
    dtb_in = din("dtb", [1, depth * 8])
    dng_in = din("dng", [1, depth * 128])
    wbra_in = din("w_br_attn", [depth, 512, D])
    wbrd_in = din("w_br_dn", [depth, 512, D])
    wout_in = din("w_out", [depth, D, D])
    w1_in = din("w_mlp1", [depth, D, DFF])
    w2_in = din("w_mlp2", [depth, DFF, D])
    cst_in = din("consts", [128, 1024])
    cst2_in = din("consts2", [128, 640])
    y_out = nc.dram_tensor("y", [ntok, D], F32, kind="ExternalOutput")

    XT = dscr("XT", [D, ntok], F32)
    QT = dscr("QT", [512, ntok], BF16)
    KT = dscr("KT", [512, ntok + 2 * PADT], BF16)
    VV = dscr("VV", [ntok + 2 * PADT, 512], BF16)
    DQ = dscr("DQ", [1536, ntok + 128], BF16)
    ZZ = dscr("ZZ", [ntok, 512], BF16)
    ABs = dscr("ABs", [128, ntok // 128, 16], F32)
    GT = dscr("GT", [2048, ntok], BF16)
    OAT = dscr("OAT", [512, ntok], BF16)
    ODT = dscr("ODT", [512, ntok], BF16)
    OF = dscr("OF", [ntok, 512], F32)
    QN = dscr("QN", [1536, ntok], BF16)

    def dtr(n):
        return [Trk() for _ in range(n)]

    T_XT, T_QT, T_DQ, T_ZZ, T_AB, T_GT, T_OAT, T_ODT, T_OF, T_QN = (dtr(nt) for _ in range(10))
    T_KT = dtr(nt + 2)
    T_VV = dtr(nt + 2)

    def sbp(name, shape, dt):
        return es.enter_context(nc.sbuf_tensor(name, list(shape), dt))

    cst = sbp("cst", [128, 1024], F32)
    t_cst = Trk()
    ident_f = cst[:, 0:128]
    TRI = [cst[:, 128:256], cst[:, 256:384]]
    NEGM = [cst[:, 384:512], cst[:, 512:640]]
    cbf = sbp("cbf", [128, 512], BF16)
    t_cbf = Trk()
    ident_b = cbf[:, 0:128]
    ones_b = cbf[:, 128:256]
    STRICT = [cbf[:, 256:384], cbf[:, 384:512]]
    cb2 = sbp("cb2", [128, 640], BF16)
    t_cb2 = Trk()
    ones_f = sbp("ones_f", [128, 128], F32)
    t_onesf = Trk()
    flags = sbp("flags_sb", [128, 8], F32)
    t_flags = Trk()
    g1 = sbp("g1_sb", [128, depth * 8], F32)
    g2 = sbp("g2_sb", [128, depth * 8], F32)
    gf = sbp("gf_sb", [128, 8], F32)
    bada = sbp("bada_sb", [128, depth * 48], F32)
    mod = sbp("mod_sb", [128, depth * 48, 8], F32)
    A1 = sbp("A1_sb", [128, depth * 8, 8], F32)
    A2 = sbp("A2_sb", [128, depth * 8, 8], F32)
    t_par = Trk()
    t_mod = Trk()

    PS = [es.enter_context(nc.psum_tensor("ps%d" % i, [128, 512], F32)) for i in range(8)]
    T_PS = [Trk(excl=True) for _ in range(8)]

    def psbf(i):
        return PS[i][:].bitcast(BF16)

    V_ = nc.vector
    A_ = nc.scalar
    P_ = nc.tensor
    G_ = nc.gpsimd

    S.dma("sp", cst[:], cst_in.ap(), w=[t_cst])
    S.dma("sp", flags[:], flags_in.ap().partition_broadcast(128), w=[t_flags])
    S.dma("sp", g1[:], g1_in.ap(), w=[t_par])
    S.dma("sp", g2[:], g2_in.ap(), w=[t_par])
    S.dma("sp", gf[:], gf_in.ap(), w=[t_par])
    S.dma("sp", bada[:], bada_in.ap(), w=[t_par])
    S.op("dve", lambda: V_.tensor_copy(out=cbf[:, 0:128], in_=cst[:, 0:128]), r=[t_cst], w=[t_cbf])
    S.op("dve", lambda: V_.memset(cbf[:, 128:256], 1.0), w=[t_cbf])
    S.op("dve", lambda: V_.tensor_copy(out=cbf[:, 256:512], in_=cst[:, 640:896]), r=[t_cst], w=[t_cbf])
    S.op("dve", lambda: V_.memset(ones_f[:], 1.0), w=[t_onesf])
    with ExitStack() as c2es:
        c2f = c2es.enter_context(nc.sbuf_tensor("c2f", [128, 640], F32))
        t_c2f = Trk()
        S.dma("sp", c2f[:], cst2_in.ap(), w=[t_c2f])
        S.op("dve", lambda: V_.tensor_copy(out=cb2[:], in_=c2f[:]), r=[t_c2f], w=[t_cb2])
        S.barrier()

    with ExitStack() as pes:
        zt = pes.enter_context(nc.sbuf_tensor("zt", [128, 4, 512], BF16))
        t_zt = Trk()
        S.op("dve", lambda: V_.memset(zt[:], 0.0), w=[t_zt])
        S.dma("sp", KT.ap()[:, 0:PADT].rearrange("(c p) t -> p c t", p=128), zt[:, :, 0:PADT], r=[t_zt], w=[T_KT[0]])
        S.dma("sp", KT.ap()[:, PADT + ntok:PADT + ntok + PADT].rearrange("(c p) t -> p c t", p=128), zt[:, :, 0:PADT],
              r=[t_zt], w=[T_KT[nt + 1]])
        S.dma("sp", VV.ap()[0:PADT, :].rearrange("(b p) c -> p b c", p=128), zt[:, 0:2, :], r=[t_zt], w=[T_VV[0]])
        S.dma("sp", VV.ap()[PADT + ntok:PADT + ntok + PADT, :].rearrange("(b p) c -> p b c", p=128), zt[:, 0:2, :],
              r=[t_zt], w=[T_VV[nt + 1]])

        cact = pes.enter_context(nc.sbuf_tensor("cact", [128, 8, 8], F32))
        t_cact = Trk()
        S.dma("sp", cact[:], cT_in.ap(), w=[t_cact])
        S.op("act", lambda: A_.activation(out=cact[:], in_=cact[:], func=AF.Silu), r=[t_cact], w=[t_cact])
        wa = [pes.enter_context(nc.sbuf_tensor("wa%d" % i, [128, 8, 512], F32)) for i in range(2)]
        t_wa = [Trk(), Trk()]
        it = 0
        for l in range(depth):
            for cb in range(12):
                b = it % 2
                S.dma("sp", wa[b][:], wada_in.ap()[l, :, cb * 512:(cb + 1) * 512].rearrange("(k p) c -> p k c", p=128),
                      w=[t_wa[b]])
                bank = it % 2

                def mm(b=b, bank=bank):
                    ins = None
                    for j in range(4):
                        for k in range(8):
                            ins = P_.matmul(PS[bank][:, j * 8:(j + 1) * 8], lhsT=wa[b][:, k, j * 128:(j + 1) * 128],
                                            rhs=cact[:, k, :], start=(k == 0), stop=(k == 7))
                    return ins
                S.op("pe", mm, r=[t_wa[b], t_cact], w=[T_PS[bank]])
                c0 = l * 48 + cb * 4
                S.op("dve", lambda bank=bank, c0=c0: V_.tensor_tensor(
                    out=mod[:, c0:c0 + 4, :], in0=PS[bank][:, 0:32].rearrange("p (j s) -> p j s", j=4),
                    in1=bada[:, c0:c0 + 4].unsqueeze(2).to_broadcast([128, 4, 8]), op=ALU.add),
                    r=[T_PS[bank], t_par], w=[t_mod])
                it += 1
        for l in range(depth):
            for (Ax, gx, off) in ((A1, g1, 8), (A2, g2, 32)):
                S.op("dve", lambda Ax=Ax, off=off, l=l: V_.tensor_scalar(
                    out=Ax[:, l * 8:(l + 1) * 8, :], in0=mod[:, l * 48 + off:l * 48 + off + 8, :], scalar1=1.0, scalar2=None,
                    op0=ALU.add), r=[t_mod], w=[t_mod])
                S.op("dve", lambda Ax=Ax, gx=gx, l=l: V_.tensor_tensor(
                    out=Ax[:, l * 8:(l + 1) * 8, :], in0=Ax[:, l * 8:(l + 1) * 8, :],
                    in1=gx[:, l * 8:(l + 1) * 8].unsqueeze(2).to_broadcast([128, 8, 8]), op=ALU.mult),
                    r=[t_mod, t_par], w=[t_mod])
        S.barrier()

    def modv(l, which, ch, seg):
        return mod[:, l * 48 + which * 8 + ch, seg:seg + 1]

    def norm_mod(xT, t_x, xsq, t_xsq, hT, t_h, rstd, t_rstd, tmp, t_tmp, nb, Ax, l, sh_which, seg, psb, N):
        for dch in range(8):
            S.op("act", lambda dch=dch: A_.activation(out=xsq[:, dch, :], in_=xT[:, dch, :], func=AF.Square),
                 r=[t_x], w=[t_xsq])

        def mm():
            ins = None
            for dch in range(8):
                ins = P_.matmul(PS[psb][:, 0:N], lhsT=ones_b, rhs=xsq[:, dch, :], start=(dch == 0), stop=(dch == 7))
            return ins
        S.op("pe", mm, r=[t_xsq, t_cbf], w=[T_PS[psb]])
        S.op("act", lambda: A_.activation(out=rstd[:], in_=PS[psb][:, 0:N], func=AF.Ln, bias=epsD[:, 0:1], scale=1.0 / D),
             r=[T_PS[psb], t_eps], w=[t_rstd])
        S.op("act", lambda: A_.activation(out=rstd[:], in_=rstd[:], func=AF.Exp, scale=-0.5), r=[t_rstd], w=[t_rstd])
        for dch in range(8):
            b = dch % nb
            S.op("dve", lambda dch=dch, b=b: V_.tensor_tensor(out=tmp[b][:], in0=xT[:, dch, :], in1=rstd[:], op=ALU.mult),
                 r=[t_x, t_rstd], w=[t_tmp[b]])
            S.op("act", lambda dch=dch, b=b: A_.activation(
                out=hT[:, dch, :], in_=tmp[b][:], func=AF.Identity, bias=modv(l, sh_which, dch, seg),
                scale=Ax[:, l * 8 + dch, seg:seg + 1]), r=[t_tmp[b], t_mod], w=[t_h])

    epsD = sbp("epsD", [128, 4], F32)
    t_eps = Trk()
    S.op("dve", lambda: V_.memset(epsD[:, 0:1], EPS), w=[t_eps])
    S.op("dve", lambda: V_.memset(epsD[:, 1:2], float(-0.5 * np.log(128.0))), w=[t_eps])

    def phase_A(l):
        with ExitStack() as pes:
            def sb(name, shape, dt):
                return pes.enter_context(nc.sbuf_tensor(name + "_L%d" % l, list(shape), dt))
            win = sb("win", [128, 8, IN_COLS], BF16)
            t_winp = [Trk() for _ in range(4)]
            for cp in range(4):
                c0 = cp * 1412
                S.dma("pool", win[:, :, c0:c0 + 1412], win_in.ap()[l, :, c0:c0 + 1412].rearrange("(k p) c -> p k c", p=128),
                      w=[t_winp[cp]])

            def twin(ca, cb):
                return [t_winp[i] for i in range(ca // 1412, (cb - 1) // 1412 + 1)]
            xtok = sb("xtok", [128, 4, D], F32) if l == 0 else None
            t_xtok = Trk()
            xT = [sb("xT%d" % i, [128, 8, 512], F32) for i in range(2)]
            t_xT = [Trk(), Trk()]
            xsq2 = [sb("xsq%d" % i, [128, 8, 512], BF16) for i in range(2)]
            t_xsq2 = [Trk(), Trk()]
            hT2 = [sb("hT%d" % i, [128, 8, 512], BF16) for i in range(2)]
            t_h2 = [Trk(), Trk()]
            rstd = sb("rstd", [128, 512], F32)
            t_rstd = Trk()
            tmp = [sb("tmpA%d" % i, [128, 512], F32) for i in range(2)]
            t_tmp = [Trk(), Trk()]
            stg = [sb("stg%d" % i, [128, 4, 512], BF16) for i in range(4)]
            t_stg = [Trk() for _ in range(4)]
            abst = sb("abst", [128, 4, 16], F32)
            t_abst = Trk()
            si = [0]

            def load_x(t):
                b = t % 2
                if l == 0:
                    S.dma("sp", xtok[:], x_in.ap()[t * 512:(t + 1) * 512, :].rearrange("(b p) d -> p b d", p=128), w=[t_xtok])
                else:
                    S.dma("sp", xT[b][:], XT.ap()[:, t * 512:(t + 1) * 512].rearrange("(c p) t -> p c t", p=128),
                          r=[T_XT[t]], w=[t_xT[b]])

            def prep(t):
                seg = t // 4
                b = t % 2
                if l == 0:
                    load_x(t)
                    for dch in range(8):
                        bank = dch % 2

                        def tp(dch=dch, bank=bank):
                            ins = None
                            for blk in range(4):
                                ins = P_.transpose(out=PS[bank][:, blk * 128:(blk + 1) * 128],
                                                   in_=xtok[:, blk, dch * 128:(dch + 1) * 128], identity=ident_f)
                            return ins
                        S.op("pe", tp, r=[t_xtok, t_cst], w=[T_PS[bank]])
                        S.op("dve", lambda dch=dch, bank=bank: V_.tensor_copy(out=xT[b][:, dch, :], in_=PS[bank][:]),
                             r=[T_PS[bank]], w=[t_xT[b]])
                    S.dma("sp", XT.ap()[:, t * 512:(t + 1) * 512].rearrange("(c p) t -> p c t", p=128), xT[b][:],
                          r=[t_xT[b]], w=[T_XT[t]])
                norm_mod(xT[b], t_xT[b], xsq2[b], t_xsq2[b], hT2[b], t_h2[b], rstd, t_rstd, tmp, t_tmp, 2, A1, l, 0, seg, 2, 512)

            if l != 0:
                load_x(0)
            prep(0)
            for t in range(nt):
                seg = t // 4
                b = t % 2
                hT = hT2[b]
                t_h = t_h2[b]
                if l != 0 and t + 1 < nt:
                    load_x(t + 1)

                def fm_proj(col0, nch, dst, dst_trk, dst_col0, scale=None):
                    for g in range(0, nch, 4):
                        s_ = si[0] % 4
                        si[0] += 1
                        n4 = min(4, nch - g)
                        for c in range(n4):
                            bank = 3 + (c % 4)
                            cc = col0 + (g + c) * 128

                            def mm(cc=cc, bank=bank):
                                ins = None
                                for k in range(8):
                                    ins = P_.matmul(PS[bank][:], lhsT=win[:, k, cc:cc + 128], rhs=hT[:, k, :],
                                                    start=(k == 0), stop=(k == 7))
                                return ins
                            S.op("pe", mm, r=twin(cc, cc + 128) + [t_h], w=[T_PS[bank]])
                            if (c % 2) == 0:
                                if scale is None:
                                    S.op("act", lambda c=c, bank=bank, s_=s_: A_.copy(out=stg[s_][:, c, :], in_=PS[bank][:]),
                                         r=[T_PS[bank]], w=[t_stg[s_]])
                                else:
                                    S.op("act", lambda c=c, bank=bank, s_=s_: A_.mul(out=stg[s_][:, c, :], in_=PS[bank][:],
                                                                                     mul=scale),
                                         r=[T_PS[bank]], w=[t_stg[s_]])
                            else:
                                if scale is None:
                                    S.op("dve", lambda c=c, bank=bank, s_=s_: V_.tensor_copy(out=stg[s_][:, c, :], in_=PS[bank][:]),
                                         r=[T_PS[bank]], w=[t_stg[s_]])
                                else:
                                    S.op("dve", lambda c=c, bank=bank, s_=s_: V_.tensor_scalar(
                                        out=stg[s_][:, c, :], in0=PS[bank][:], scalar1=scale, scalar2=None, op0=ALU.mult),
                                        r=[T_PS[bank]], w=[t_stg[s_]])
                        r0 = dst_col0 + g * 128
                        S.dma("sp", dst[r0:r0 + n4 * 128, :].rearrange("(c p) t -> p c t", p=128), stg[s_][:, 0:n4, :],
                              r=[t_stg[s_]], w=[dst_trk])

                tk = slice(t * 512, (t + 1) * 512)
                fm_proj(C_NQ, 4, QT.ap()[:, tk], T_QT[t], 0, scale=0.125)
                fm_proj(C_NK, 4, KT.ap()[:, PADT + t * 512:PADT + (t + 1) * 512], T_KT[t + 1], 0)
                fm_proj(C_DQ, 12, DQ.ap()[:, 64 + t * 512:64 + (t + 1) * 512], T_DQ[t], 0)
                if t + 1 < nt:
                    prep(t + 1)
                fm_proj(C_G, 16, GT.ap()[:, tk], T_GT[t], 0)

                def tm_proj(col0, dst_ap, dst_trk):
                    s_ = si[0] % 4
                    si[0] += 1
                    for blk in range(4):
                        bank = 3 + blk

                        def mm(blk=blk, bank=bank):
                            ins = None
                            for k in range(8):
                                ins = P_.matmul(PS[bank][:], lhsT=hT[:, k, blk * 128:(blk + 1) * 128],
                                                rhs=win[:, k, col0:col0 + 512], start=(k == 0), stop=(k == 7))
                            return ins
                        S.op("pe", mm, r=twin(col0, col0 + 512) + [t_h], w=[T_PS[bank]])
                        if blk % 2 == 0:
                            S.op("act", lambda blk=blk, bank=bank, s_=s_: A_.copy(out=stg[s_][:, blk, :], in_=PS[bank][:]),
                                 r=[T_PS[bank]], w=[t_stg[s_]])
                        else:
                            S.op("dve", lambda blk=blk, bank=bank, s_=s_: V_.tensor_copy(out=stg[s_][:, blk, :], in_=PS[bank][:]),
                                 r=[T_PS[bank]], w=[t_stg[s_]])
                    S.dma("sp", dst_ap.rearrange("(b p) c -> p b c", p=128), stg[s_][:], r=[t_stg[s_]], w=[dst_trk])

                tm_proj(C_NV, VV.ap()[PADT + t * 512:PADT + (t + 1) * 512, :], T_VV[t + 1])
                tm_proj(C_Z, ZZ.ap()[tk, :], T_ZZ[t])

                def mmab():
                    ins = None
                    for blk in range(4):
                        for k in range(8):
                            ins = P_.matmul(PS[7][:, blk * 128:(blk + 1) * 128], lhsT=hT[:, k, blk * 128:(blk + 1) * 128],
                                            rhs=win[:, k, C_AB - 112:C_AB + 16], start=(k == 0), stop=(k == 7))
                    return ins
                S.op("pe", mmab, r=twin(C_AB - 112, C_AB + 16) + [t_h], w=[T_PS[7]])
                S.op("dve", lambda: V_.tensor_copy(out=abst[:], in_=PS[7][:].rearrange("p (b c) -> p b c", b=4)[:, :, 112:128]),
                     r=[T_PS[7]], w=[t_abst])
                S.dma("sp", ABs.ap()[:, t * 4:(t + 1) * 4, :], abst[:], r=[t_abst], w=[T_AB[t]])
            S.barrier()

    def phase_B(l):
        with ExitStack() as pes:
            def sb(name, shape, dt):
                return pes.enter_context(nc.sbuf_tensor(name + "_L%d" % l, list(shape), dt))
            T2 = sb("T2", [128, 14, 512], BF16)
            t_T2 = Trk()
            with ExitStack() as tes:
                hk = tes.enter_context(nc.sbuf_tensor("hk_L%d" % l, [64, 8, 15 * 64], F32))
                t_hk = Trk()
                src = bass.AP(rpb_in, l * 8 * 15 * 128, [[1, 64], [128, 15], [15 * 128, 8], [1, 64]])
                for h0 in range(8):
                    srch = bass.AP(rpb_in, l * 8 * 15 * 128 + h0 * 15 * 128, [[1, 64], [128, 15], [1, 64]])
                    S.dma("sp", hk[:, h0, :].rearrange("p (a b) -> p a b", a=15), srch, w=[t_hk])
                traw = tes.enter_context(nc.sbuf_tensor("traw_L%d" % l, [128, 512], F32))
                t_traw = Trk()
                Jx = cst[0:64, 896:960]
                for d in range(14):
                    bank = d % 2

                    def mm(d=d, bank=bank):
                        ins = None
                        for h in range(8):
                            ins = P_.matmul(PS[bank][:, h * 64:(h + 1) * 64], lhsT=hk[:, h, d * 64:(d + 2) * 64], rhs=Jx,
                                            start=True, stop=True)
                        return ins
                    S.op("pe", mm, r=[t_hk, t_cst], w=[T_PS[bank]])
                    S.op("act", lambda bank=bank: A_.activation(out=traw[:], in_=PS[bank][:], func=AF.Exp),
                         r=[T_PS[bank]], w=[t_traw])
                    S.op("dve", lambda d=d: V_.tensor_tensor(
                        out=T2[:, d, :].rearrange("p (h q) -> p h q", h=8), in0=traw[:].rearrange("p (h q) -> p h q", h=8),
                        in1=cst[:, 960:1024].unsqueeze(1).to_broadcast([128, 8, 64]), op=ALU.mult),
                        r=[t_traw, t_cst], w=[t_T2])
                S.barrier()

            qt = [sb("qt%d" % i, [128, 4, 512], BF16) for i in range(2)]
            t_qt = [Trk(), Trk()]
            qz = sb("qz", [128, 8, 512], BF16)
            t_qz = Trk()
            kw = [sb("kw%d" % i, [128, 4, 1024], BF16) for i in range(2)]
            t_kw = [Trk(), Trk()]
            ve = [sb("ve%d" % i, [128, 8, 512], BF16) for i in range(2)]
            vo = [sb("vo%d" % i, [128, 7, 512], BF16) for i in range(2)]
            t_ve = [Trk(), Trk()]
            t_vo = [Trk(), Trk()]
            ex = [sb("ex%d" % i, [128, 512], BF16) for i in range(4)]
            t_ex = [Trk() for _ in range(4)]
            pt = [sb("pt%d" % i, [128, 512], BF16) for i in range(8)]
            t_pt = [Trk() for _ in range(8)]
            lnd = [sb("lnd%d" % i, [64, 512], F32) for i in range(2)]
            t_lnd = [Trk(), Trk()]
            osb = [sb("osb%d" % i, [64, 512], F32) for i in range(2)]
            t_osb = [Trk(), Trk()]
            ost = [sb("ost%d" % i, [64, 8, 512], BF16) for i in range(2)]
            t_ost = [Trk(), Trk()]
            S.op("dve", lambda: V_.memset(qz[:], 0.0), w=[t_qz])

            def loads(t):
                b = t % 2
                tk = slice(t * 512, (t + 1) * 512)
                S.dma("sp", qt[b][:], QT.ap()[:, tk].rearrange("(c p) t -> p c t", p=128), r=[T_QT[t]], w=[t_qt[b]])
                k0 = t * 512
                trk = [T_KT[i] for i in (t, t + 1, t + 2) if 0 <= i < nt + 2]
                S.dma("sp", kw[b][:], KT.ap()[:, k0:k0 + 1024].rearrange("(c p) t -> p c t", p=128), r=trk, w=[t_kw[b]])
                trv = [T_VV[i] for i in (t, t + 1, t + 2) if 0 <= i < nt + 2]
                S.dma("sp", ve[b][:], VV.ap()[k0:k0 + 1024, :].rearrange("(m p) c -> p m c", p=128), r=trv, w=[t_ve[b]])
                S.dma("sp", vo[b][:], VV.ap()[k0 + 64:k0 + 64 + 896, :].rearrange("(m p) c -> p m c", p=128), r=trv,
                      w=[t_vo[b]])

            loads(0)
            rowi = [0]
            for t in range(nt):
                b = t % 2
                seg = t // 4
                tpos = t % 4
                if t + 1 < nt:
                    loads(t + 1)
                for hp in range(2):
                    S.op("pool", lambda hp=hp: G_.tensor_copy(
                        out=qz[hp * 64:(hp + 1) * 64, :, :].rearrange("p (c two) t -> p c two t", two=2)[:, :, hp, :],
                        in_=qt[b][hp * 64:(hp + 1) * 64, :, :]), r=[t_qt[b]], w=[t_qz])
                ob = t % 2
                jobs = []
                for i in range(8):
                    alts = []
                    edge_start = (tpos == 0 and i < 4)
                    edge_end = (tpos == 3 and i > 4)
                    if edge_start:
                        fidx = seg - 1
                        if fidx >= 0:
                            alts = [("std", i, -4), ("clamp", 4, -i)]
                        else:
                            alts = [("clamp", 4, -i)]
                            fidx = None
                    elif edge_end:
                        fidx = seg
                        if seg + 1 < nseg:
                            alts = [("std", i, -4), ("clamp", 4, -i)]
                        else:
                            alts = [("clamp", 4, -i)]
                            fidx = None
                    else:
                        alts = [("std", i, -4)]
                        fidx = None
                    jobs.append((i, alts, fidx))

                def stage1(i, rel, o, rb):
                    for kb in range(4):
                        bank = kb

                        def mm(kb=kb, bank=bank):
                            ins = None
                            ks = (rel + 2 * kb) * 64
                            for h in range(8):
                                ins = P_.matmul(PS[bank][:, h * 64:(h + 1) * 64], lhsT=kw[b][:, h // 2, ks:ks + 128],
                                                rhs=qz[:, h, i * 64:(i + 1) * 64], start=True, stop=True)
                            return ins
                        S.op("pe", mm, r=[t_kw[b], t_qz], w=[T_PS[bank]])
                        S.op("act", lambda kb=kb, bank=bank: A_.activation(out=ex[kb][:], in_=PS[bank][:], func=AF.Exp),
                             r=[T_PS[bank]], w=[t_ex[kb]])
                        d = o + 2 * kb + 7
                        pi = rb * 4 + kb
                        S.op("dve", lambda kb=kb, d=d, pi=pi: V_.tensor_tensor(out=pt[pi][:], in0=ex[kb][:], in1=T2[:, d, :],
                                                                              op=ALU.mult),
                             r=[t_ex[kb], t_T2], w=[t_pt[pi]])

                def stage2(i, rel, rb):
                    dbank = 4 + rb
                    obank = 6 + rb

                    def mmd():
                        ins = None
                        for kb in range(4):
                            ins = P_.matmul(PS[dbank][0:64, :], lhsT=ones_b[:, 0:64], rhs=pt[rb * 4 + kb][:],
                                            start=(kb == 0), stop=(kb == 3))
                        return ins
                    S.op("pe", mmd, r=[t_pt[rb * 4 + k_] for k_ in range(4)] + [t_cbf], w=[T_PS[dbank]])

                    def mmo():
                        ins = None
                        for h in range(8):
                            for kb in range(4):
                                rr = rel + 2 * kb
                                vsrc = ve[b][:, rr // 2, h * 64:(h + 1) * 64] if rr % 2 == 0 else \
                                    vo[b][:, (rr - 1) // 2, h * 64:(h + 1) * 64]
                                ins = P_.matmul(PS[obank][0:64, h * 64:(h + 1) * 64], lhsT=vsrc,
                                                rhs=pt[rb * 4 + kb][:, h * 64:(h + 1) * 64], start=(kb == 0), stop=(kb == 3))
                        return ins
                    S.op("pe", mmo, r=[t_pt[rb * 4 + k_] for k_ in range(4)] + [t_ve[b], t_vo[b]], w=[T_PS[obank]])
                    S.op("act", lambda: A_.activation(out=lnd[rb][:], in_=PS[dbank][0:64, :], func=AF.Ln),
                         r=[T_PS[dbank]], w=[t_lnd[rb]])
                    S.op("act", lambda: A_.activation(out=lnd[rb][:], in_=lnd[rb][:], func=AF.Exp, scale=-1.0),
                         r=[t_lnd[rb]], w=[t_lnd[rb]])

                def finish(i, res, fidx):
                    dst = ost[ob][:, :, i * 64:(i + 1) * 64]
                    if len(res) == 1:
                        rb, obank = res[0]
                        S.op("dve", lambda: V_.tensor_tensor(
                            out=dst, in0=PS[obank][0:64, :].rearrange("p (h q) -> p h q", h=8),
                            in1=lnd[rb][:].rearrange("p (h q) -> p h q", h=8), op=ALU.mult),
                            r=[T_PS[obank], t_lnd[rb]], w=[t_ost[ob]])
                    else:
                        (rb0, ob0), (rb1, ob1) = res
                        S.op("dve", lambda: V_.tensor_tensor(out=osb[0][:], in0=PS[ob0][0:64, :], in1=lnd[rb0][:], op=ALU.mult),
                             r=[T_PS[ob0], t_lnd[rb0]], w=[t_osb[0]])
                        S.op("dve", lambda: V_.tensor_tensor(out=osb[1][:], in0=PS[ob1][0:64, :], in1=lnd[rb1][:], op=ALU.mult),
                             r=[T_PS[ob1], t_lnd[rb1]], w=[t_osb[1]])
                        S.op("dve", lambda: V_.tensor_tensor(out=osb[0][:], in0=osb[0][:], in1=osb[1][:], op=ALU.subtract),
                             r=[t_osb[0], t_osb[1]], w=[t_osb[0]])
                        S.op("dve", lambda: V_.scalar_tensor_tensor(
                            out=dst, in0=osb[0][:].rearrange("p (h q) -> p h q", h=8), scalar=flags[0:64, fidx:fidx + 1],
                            in1=osb[1][:].rearrange("p (h q) -> p h q", h=8), op0=ALU.mult, op1=ALU.add),
                            r=[t_osb[0], t_osb[1], t_flags], w=[t_ost[ob]])

                flat = []
                for (i, alts, fidx) in jobs:
                    for ai, (nm, rel, o) in enumerate(alts):
                        flat.append((i, rel, o, ai == len(alts) - 1, fidx))
                pend = None
                resacc = []
                for (i, rel, o, lastalt, fidx) in flat:
                    rb = rowi[0] % 2
                    rowi[0] += 1
                    stage1(i, rel, o, rb)
                    if pend is not None:
                        pi_, prel, prb, plast, pfidx = pend
                        stage2(pi_, prel, prb)
                        resacc.append((prb, 6 + prb))
                        if plast:
                            finish(pi_, resacc, pfidx)
                            resacc = []
                    pend = (i, rel, rb, lastalt, fidx)
                pi_, prel, prb, plast, pfidx = pend
                stage2(pi_, prel, prb)
                resacc.append((prb, 6 + prb))
                finish(pi_, resacc, pfidx)
                S.dma("sp", OAT.ap()[:, t * 512:(t + 1) * 512].rearrange("(h p) t -> p h t", p=64), ost[ob][:],
                      r=[t_ost[ob]], w=[T_OAT[t]])
            S.barrier()

    def phase_C(l):
        with ExitStack() as pes:
            def sb(name, shape, dt):
                return pes.enter_context(nc.sbuf_tensor(name + "_L%d" % l, list(shape), dt))
            NCH = SEG // 128
            diagw = sb("diagw", [128, 60, 128], BF16)
            t_diagw = Trk()
            cw = sb("cw", [128, depth * 60], F32)
            t_cw = Trk()
            S.dma("sp", cw[:], conv_in.ap(), w=[t_cw])
            for j in range(60):
                S.op("dve", lambda j=j: V_.tensor_scalar(out=diagw[:, j, :], in0=ident_f, scalar1=cw[:, l * 60 + j:l * 60 + j + 1],
                                                         scalar2=None, op0=ALU.mult), r=[t_cw, t_cst], w=[t_diagw])
            nexpa = sb("nexpa", [128, 8], F32)
            dtb = sb("dtb_sb", [128, 8], F32)
            dng = sb("dng_sb", [128, 128], F32)
            t_hp = Trk()
            S.dma("sp", nexpa[:], alog_in.ap()[:, l * 8:(l + 1) * 8].partition_broadcast(128), w=[t_hp])
            S.dma("sp", dtb[:], dtb_in.ap()[:, l * 8:(l + 1) * 8].partition_broadcast(128), w=[t_hp])
            S.dma("sp", dng[:], dng_in.ap()[:, l * 128:(l + 1) * 128].partition_broadcast(128), w=[t_hp])
            S.op("act", lambda: A_.activation(out=nexpa[:], in_=nexpa[:], func=AF.Exp), r=[t_hp], w=[t_hp])
            S.op("dve", lambda: V_.tensor_scalar(out=nexpa[:], in0=nexpa[:], scalar1=-1.0, scalar2=None, op0=ALU.mult),
                 r=[t_hp], w=[t_hp])

            rawb = [sb("raw%d" % i, [128, 12, 516], BF16) for i in range(2)]
            t_rawb = [Trk(), Trk()]
            qkv = sb("qkv", [128, 12, SEG], BF16)
            t_qkv = Trk()
            sq = [sb("sqC%d" % i, [128, 512], BF16) for i in range(2)]
            t_sq = [Trk(), Trk()]
            rn = [sb("rnC%d" % i, [128, 512], F32) for i in range(2)]
            t_rn = [Trk(), Trk()]
            ab = sb("abC", [128, NCH, 16], F32)
            t_ab = Trk()
            beta = sb("beta", [128, NCH, 4], F32)
            gg = sb("gg", [128, NCH, 4], F32)
            Gc = sb("Gc", [128, NCH, 4], F32)
            nGc = sb("nGc", [128, NCH, 4], F32)
            eG = sb("eG", [128, NCH, 4], F32)
            negb = sb("negb", [128, NCH, 4], F32)
            Gt = sb("Gt", [128, NCH, 4], F32)
            egt = sb("egt", [128, NCH, 4], F32)
            kds = sb("kds", [128, NCH, 4], F32)
            t_gate = Trk()
            NSLOT = 2
            def slotbufs(k):
                d = {}
                d["gbc"] = sb("gbc%d" % k, [128, 4, 128], F32)
                for nm in ("DT", "DTS", "Pa", "PaT", "Pb0", "Pb1", "PbT0", "PbT1", "XT", "T1", "T1p", "Off", "OffT"):
                    d[nm] = sb("%s_s%d" % (nm, k), [128, 512], BF16)
                d["tmpP"] = sb("tmpP%d" % k, [128, 512], F32)
                for nm in list(d.keys()):
                    d["t_" + nm] = Trk()
                return d
            SL = [slotbufs(k) for k in range(NSLOT)]
            T4b = Trk()
            qkT = sb("qkT", [128, NCH, 512], BF16)
            t_qkT = Trk()
            Yk = sb("Yk", [128, NCH, 512], BF16)
            t_Y = Trk()
            kd = [sb("kd%d" % i, [128, 512], BF16) for i in range(2)]
            t_kd = [Trk(), Trk()]
            Sst = sb("Sst", [128, 512], F32)
            Sbf = sb("Sbf", [128, 512], BF16)
            t_S = Trk()
            t_Sbf = Trk()
            tmpc = [sb("tmpc%d" % i, [128, 512], F32) for i in range(2)]
            t_tmpc = [Trk(), Trk()]
            Rr = sb("Rr", [128, 512], BF16)
            t_R = Trk()
            vnew = sb("vnew", [128, 512], BF16)
            t_vnew = Trk()
            och = [sb("och%d" % i, [128, 512], F32) for i in range(2)]
            t_och = [Trk(), Trk()]
            ofl = [sb("ofl%d" % i, [128, 512], F32) for i in range(2)]
            t_ofl = [Trk(), Trk()]
            zt_ = [sb("ztC%d" % i, [128, 512], BF16) for i in range(2)]
            t_zt = [Trk(), Trk()]
            ms = sb("msC", [128, 8], F32)
            t_ms = Trk()
            odt = [sb("odt%d" % i, [128, 4, 128], BF16) for i in range(2)]
            t_odt = [Trk(), Trk()]
            odtok = sb("odtok", [128, 512], BF16)
            t_odtok = Trk()

            def bc4(apx):
                return apx.unsqueeze(2).to_broadcast([128, 4, 128])

            def v4(apx):
                return apx.rearrange("p (h d) -> p h d", h=4)

            def seg_pass(s, dr_):
                t0 = s * SEG
                tiles = [s * 4 + i for i in range(4)]
                if dr_ == 0:
                    k_ = 0
                    for tb in range(4):
                        raw = rawb[tb % 2]
                        t_raw = t_rawb[tb % 2]
                        trk = [T_DQ[s * 4 + tb]]
                        if s * 4 + tb - 1 >= 0:
                            trk.append(T_DQ[s * 4 + tb - 1])
                        if s * 4 + tb + 1 < nt:
                            trk.append(T_DQ[s * 4 + tb + 1])
                        c0_ = 64 + t0 + tb * 512 - 2
                        S.dma("sp", raw[:], DQ.ap()[:, c0_:c0_ + 516].rearrange("(c p) t -> p c t", p=128), r=trk, w=[t_raw])
                        if tb == 0:
                            if s > 0:
                                S.op("dve", lambda raw=raw: V_.tensor_scalar(out=raw[:, :, 0:2], in0=raw[:, :, 0:2], scalar1=flags[:, s - 1:s],
                                                                             scalar2=None, op0=ALU.mult), r=[t_raw, t_flags], w=[t_raw])
                            else:
                                S.op("dve", lambda raw=raw: V_.memset(raw[:, :, 0:2], 0.0), w=[t_raw])
                        if tb == 3:
                            if s + 1 < nseg:
                                S.op("dve", lambda raw=raw: V_.tensor_scalar(out=raw[:, :, 514:516], in0=raw[:, :, 514:516],
                                                                             scalar1=flags[:, s:s + 1], scalar2=None, op0=ALU.mult),
                                     r=[t_raw, t_flags], w=[t_raw])
                            else:
                                S.op("dve", lambda raw=raw: V_.memset(raw[:, :, 514:516], 0.0), w=[t_raw])
                        for ch in range(12):
                            bank = k_ % 2
                            k_ += 1

                            def mm(ch=ch, raw=raw, bank=bank):
                                ins = None
                                for tap in range(5):
                                    ins = P_.matmul(PS[bank][:], lhsT=diagw[:, tap * 12 + ch, :], rhs=raw[:, ch, tap:tap + 512],
                                                    start=(tap == 0), stop=(tap == 4))
                                return ins
                            S.op("pe", mm, r=[t_diagw, t_raw], w=[T_PS[bank]])
                            S.op("act", lambda ch=ch, tb=tb, bank=bank: A_.activation(
                                out=qkv[:, ch, tb * 512:(tb + 1) * 512], in_=PS[bank][:], func=AF.Silu), r=[T_PS[bank]], w=[t_qkv])
                else:
                    S.dma("sp", qkv[:], QN.ap()[:, t0:t0 + SEG].rearrange("(c p) t -> p c t", p=128), r=[T_QN[i] for i in tiles], w=[t_qkv])
                S.dma("sp", ab[:], ABs.ap()[:, s * NCH:(s + 1) * NCH, :], r=[T_AB[i] for i in tiles], w=[t_ab])
                if dr_ == 0:
                    k_ = 0
                    for ch in range(8):
                        for tb in range(4):
                            b2 = k_ % 2
                            bank = 2 + b2
                            k_ += 1
                            sl = qkv[:, ch, tb * 512:(tb + 1) * 512]
                            S.op("act", lambda sl=sl, b2=b2: A_.activation(out=sq[b2][:], in_=sl, func=AF.Square), r=[t_qkv], w=[t_sq[b2]])
                            S.op("pe", lambda b2=b2, bank=bank: P_.matmul(PS[bank][:], lhsT=ones_b, rhs=sq[b2][:], start=True, stop=True),
                                 r=[t_sq[b2], t_cbf], w=[T_PS[bank]])
                            S.op("act", lambda b2=b2, bank=bank: A_.activation(out=rn[b2][:], in_=PS[bank][:], func=AF.Ln,
                                                                              bias=epsD[:, 0:1], scale=1.0),
                                 r=[T_PS[bank], t_eps], w=[t_rn[b2]])
                            if ch < 4:
                                S.op("act", lambda b2=b2: A_.activation(out=rn[b2][:], in_=rn[b2][:], func=AF.Exp, scale=-0.5,
                                                                        bias=epsD[:, 1:2]), r=[t_rn[b2], t_eps], w=[t_rn[b2]])
                            else:
                                S.op("act", lambda b2=b2: A_.activation(out=rn[b2][:], in_=rn[b2][:], func=AF.Exp, scale=-0.5),
                                     r=[t_rn[b2]], w=[t_rn[b2]])
                            S.op("dve", lambda sl=sl, b2=b2: V_.tensor_tensor(out=sl, in0=sl, in1=rn[b2][:], op=ALU.mult),
                                 r=[t_qkv, t_rn[b2]], w=[t_qkv])
                    S.dma("sp", QN.ap()[:, t0:t0 + SEG].rearrange("(c p) t -> p c t", p=128), qkv[:], r=[t_qkv], w=[T_QN[i] for i in tiles])
                bsl = ab[:, :, dr_ * 4:dr_ * 4 + 4]
                asl = ab[:, :, 8 + dr_ * 4:8 + dr_ * 4 + 4]
                hb = lambda tt: tt[:, dr_ * 4:dr_ * 4 + 4].unsqueeze(1).to_broadcast([128, NCH, 4])
                S.op("act", lambda: A_.activation(out=beta[:], in_=bsl, func=AF.Exp, scale=-1.0), r=[t_ab], w=[t_gate])
                S.op("dve", lambda: V_.tensor_scalar(out=beta[:], in0=beta[:], scalar1=1.0, scalar2=None, op0=ALU.add),
                     r=[t_gate], w=[t_gate])
                S.op("dve", lambda: V_.reciprocal(out=beta[:], in_=beta[:]), r=[t_gate], w=[t_gate])
                S.op("dve", lambda: V_.tensor_scalar(out=negb[:], in0=beta[:], scalar1=-1.0, scalar2=None, op0=ALU.mult),
                     r=[t_gate], w=[t_gate])
                S.op("dve", lambda: V_.tensor_tensor(out=gg[:], in0=asl, in1=hb(dtb), op=ALU.add), r=[t_ab, t_hp], w=[t_gate])
                S.op("act", lambda: A_.activation(out=gg[:], in_=gg[:], func=AF.Exp), r=[t_gate], w=[t_gate])
                S.op("act", lambda: A_.activation(out=gg[:], in_=gg[:], func=AF.Ln, bias=1.0, scale=1.0), r=[t_gate], w=[t_gate])
                S.op("dve", lambda: V_.tensor_tensor(out=gg[:], in0=gg[:], in1=hb(nexpa), op=ALU.mult), r=[t_gate, t_hp], w=[t_gate])
                ggf = gg[:].rearrange("p c h -> p (c h)")
                S.op("pe", lambda: P_.matmul(PS[6][:, 0:64], lhsT=TRI[dr_], rhs=ggf, start=True, stop=True),
                     r=[t_gate, t_cst], w=[T_PS[6]])
                S.op("pe", lambda: P_.matmul(PS[7][:, 0:64], lhsT=ones_f[:], rhs=ggf, start=True, stop=True),
                     r=[t_gate, t_onesf], w=[T_PS[7]])
                fl = lambda tt: tt[:].rearrange("p c h -> p (c h)")
                S.op("dve", lambda: V_.tensor_copy(out=fl(Gc), in_=PS[6][:, 0:64]), r=[T_PS[6]], w=[t_gate])
                S.op("dve", lambda: V_.tensor_scalar(out=fl(nGc), in0=PS[6][:, 0:64], scalar1=-1.0, scalar2=None, op0=ALU.mult),
                     r=[T_PS[6]], w=[t_gate])
                S.op("act", lambda: A_.activation(out=fl(eG), in_=PS[6][:, 0:64], func=AF.Exp), r=[T_PS[6]], w=[t_gate])
                S.op("dve", lambda: V_.tensor_copy(out=fl(Gt), in_=PS[7][:, 0:64]), r=[T_PS[7]], w=[t_gate])
                S.op("act", lambda: A_.activation(out=fl(egt), in_=PS[7][:, 0:64], func=AF.Exp), r=[T_PS[7]], w=[t_gate])
                S.op("dve", lambda: V_.tensor_tensor(out=kds[:], in0=Gt[:], in1=Gc[:], op=ALU.subtract), r=[t_gate], w=[t_gate])
                S.op("act", lambda: A_.activation(out=kds[:], in_=kds[:], func=AF.Exp), r=[t_gate], w=[t_gate])

                mk = lambda m: cb2[:, m * 128:(m + 1) * 128].unsqueeze(1).to_broadcast([128, 4, 128])
                idb4 = ident_b.unsqueeze(1).to_broadcast([128, 4, 128])

                def mm4(bank, lh, rh):
                    def f():
                        ins = None
                        for h in range(4):
                            hs = slice(h * 128, (h + 1) * 128)
                            ins = P_.matmul(PS[bank][:, hs], lhsT=lh[:, hs], rhs=rh[:, hs], start=True, stop=True)
                        return ins
                    return f

                def prep_gen(c, k):
                    B = SL[k]
                    bA, bB = 2 * k, 2 * k + 1
                    cs = slice(c * 128, (c + 1) * 128)
                    gbc, DT, DTS, Pa, PaT, XT, T1, T1p, Off, OffT, tmpP = (B[n] for n in (
                        "gbc", "DT", "DTS", "Pa", "PaT", "XT", "T1", "T1p", "Off", "OffT", "tmpP"))
                    Pb = [B["Pb0"], B["Pb1"]]
                    PbT = [B["PbT0"], B["PbT1"]]
                    t_Pb = [B["t_Pb0"], B["t_Pb1"]]
                    t_PbT = [B["t_PbT0"], B["t_PbT1"]]
                    S.op("dve", lambda: V_.tensor_copy(out=gbc[:], in_=bc4(gg[:, c, :])), r=[t_gate], w=[B["t_gbc"]])

                    def mmg():
                        ins = None
                        for h in range(4):
                            P_.matmul(PS[bA][:, h * 128:(h + 1) * 128], lhsT=gbc[:, h, :], rhs=TRI[dr_], start=True, stop=False)
                            ins = P_.matmul(PS[bA][:, h * 128:(h + 1) * 128], lhsT=ident_f, rhs=NEGM[dr_], start=False, stop=True)
                        return ins
                    S.op("pe", mmg, r=[B["t_gbc"], t_cst], w=[T_PS[bA]])

                    def mmkk():
                        ins = None
                        for h in range(4):
                            ins = P_.matmul(PS[bB][:, h * 128:(h + 1) * 128], lhsT=qkv[:, 4 + h, cs], rhs=qkv[:, 4 + h, cs],
                                            start=True, stop=True)
                        return ins
                    S.op("pe", mmkk, r=[t_qkv], w=[T_PS[bB]])
                    yield
                    for h in range(4):
                        S.op("act", lambda h=h: A_.activation(out=DT[:, h * 128:(h + 1) * 128], in_=PS[bA][:, h * 128:(h + 1) * 128],
                                                              func=AF.Exp, bias=nGc[:, c, h:h + 1], scale=1.0),
                             r=[T_PS[bA], t_gate], w=[B["t_DT"]])
                    yield
                    S.op("pool", lambda: G_.tensor_tensor(out=v4(DTS[:]), in0=v4(DT[:]),
                                                          in1=STRICT[dr_].unsqueeze(1).to_broadcast([128, 4, 128]), op=ALU.mult),
                         r=[B["t_DT"], t_cbf], w=[B["t_DTS"]])

                    def mmqk():
                        ins = None
                        for h in range(4):
                            ins = P_.matmul(PS[bA][:, h * 128:(h + 1) * 128], lhsT=qkv[:, 4 + h, cs], rhs=qkv[:, h, cs],
                                            start=True, stop=True)
                        return ins
                    S.op("pe", mmqk, r=[t_qkv], w=[T_PS[bA]])
                    yield
                    S.op("dve", lambda: V_.tensor_tensor(out=qkT[:, c, :], in0=PS[bA][:], in1=DT[:], op=ALU.mult),
                         r=[T_PS[bA], B["t_DT"]], w=[t_qkT])
                    S.op("dve", lambda: V_.tensor_tensor(out=tmpP[:], in0=PS[bB][:], in1=DTS[:], op=ALU.mult),
                         r=[T_PS[bB], B["t_DTS"]], w=[B["t_tmpP"]])
                    yield
                    S.op("dve", lambda: V_.tensor_tensor(out=v4(Pa[:]), in0=v4(tmpP[:]), in1=bc4(negb[:, c, :]), op=ALU.mult),
                         r=[B["t_tmpP"], t_gate], w=[B["t_Pa"]])

                    def tpn():
                        ins = None
                        for h in range(4):
                            ins = P_.transpose(out=psbf(bB)[:, h * 128:(h + 1) * 128], in_=Pa[:, h * 128:(h + 1) * 128],
                                               identity=ident_b)
                        return ins
                    S.op("pe", tpn, r=[B["t_Pa"], t_cbf], w=[T_PS[bB]])
                    yield
                    S.op("act", lambda: A_.copy(out=PaT[:], in_=psbf(bB)[:, 0:512]), r=[T_PS[bB]], w=[B["t_PaT"]])
                    X = Yk[:, c, :]
                    t_X = t_Yc[c]
                    S.op("pool", lambda: G_.tensor_tensor(out=v4(Pb[0][:]), in0=v4(Pa[:]), in1=mk(0), op=ALU.mult),
                         r=[B["t_Pa"], t_cb2], w=[t_Pb[0]])
                    yield
                    S.op("pool", lambda: G_.tensor_tensor(out=v4(PbT[0][:]), in0=v4(PaT[:]), in1=mk(0), op=ALU.mult),
                         r=[B["t_PaT"], t_cb2], w=[t_PbT[0]])
                    S.op("dve", lambda: V_.tensor_tensor(out=v4(X), in0=v4(Pb[0][:]), in1=idb4, op=ALU.add),
                         r=[t_Pb[0], t_cbf], w=[t_X])
                    yield
                    S.op("dve", lambda: V_.tensor_tensor(out=v4(XT[:]), in0=v4(PbT[0][:]), in1=idb4, op=ALU.add),
                         r=[t_PbT[0], t_cbf], w=[B["t_XT"]])
                    cur = 0
                    for st in range(2):
                        nx = 1 - cur
                        S.op("pe", mm4(bA, PbT[cur], Pb[cur]), r=[t_Pb[cur], t_PbT[cur]], w=[T_PS[bA]])
                        S.op("pe", mm4(bB, Pb[cur], PbT[cur]), r=[t_Pb[cur], t_PbT[cur]], w=[T_PS[bB]])
                        yield
                        S.op("act", lambda nx=nx: A_.copy(out=Pb[nx][:], in_=PS[bA][:]), r=[T_PS[bA]], w=[t_Pb[nx]])
                        S.op("act", lambda nx=nx: A_.copy(out=PbT[nx][:], in_=PS[bB][:]), r=[T_PS[bB]], w=[t_PbT[nx]])
                        yield
                        S.op("pe", mm4(bA, PbT[nx], X), r=[t_PbT[nx], t_X], w=[T_PS[bA]])
                        S.op("pe", mm4(bB, Pb[nx], XT), r=[t_Pb[nx], B["t_XT"]], w=[T_PS[bB]])
                        yield
                        S.op("dve", lambda: V_.tensor_tensor(out=X, in0=PS[bA][:], in1=X, op=ALU.add), r=[T_PS[bA], t_X], w=[t_X])
                        S.op("dve", lambda: V_.tensor_tensor(out=XT[:], in0=PS[bB][:], in1=XT[:], op=ALU.add),
                             r=[T_PS[bB], B["t_XT"]], w=[B["t_XT"]])
                        yield
                        cur = nx
                    for lv in range(1, 5):
                        last_lv = (lv == 4)
                        S.op("pool", lambda lv=lv: G_.tensor_tensor(out=v4(OffT[:]), in0=v4(PaT[:]), in1=mk(lv), op=ALU.mult),
                             r=[B["t_PaT"], t_cb2], w=[B["t_OffT"]])
                        if not last_lv:
                            S.op("pool", lambda lv=lv: G_.tensor_tensor(out=v4(Off[:]), in0=v4(Pa[:]), in1=mk(lv), op=ALU.mult),
                                 r=[B["t_Pa"], t_cb2], w=[B["t_Off"]])
                        yield
                        S.op("pe", mm4(bA, OffT, X), r=[B["t_OffT"], t_X], w=[T_PS[bA]])
                        if not last_lv:
                            S.op("pe", mm4(bB, Off, XT), r=[B["t_Off"], B["t_XT"]], w=[T_PS[bB]])
                        yield
                        S.op("act", lambda: A_.copy(out=T1[:], in_=PS[bA][:]), r=[T_PS[bA]], w=[B["t_T1"]])
                        if not last_lv:
                            S.op("act", lambda: A_.copy(out=T1p[:], in_=PS[bB][:]), r=[T_PS[bB]], w=[B["t_T1p"]])
                        yield
                        S.op("pe", mm4(bA, XT, T1), r=[B["t_XT"], B["t_T1"]], w=[T_PS[bA]])
                        if not last_lv:
                            S.op("pe", mm4(bB, X, T1p), r=[t_X, B["t_T1p"]], w=[T_PS[bB]])
                        yield
                        S.op("dve", lambda: V_.tensor_tensor(out=X, in0=PS[bA][:], in1=X, op=ALU.add), r=[T_PS[bA], t_X], w=[t_X])
                        if not last_lv:
                            S.op("dve", lambda: V_.tensor_tensor(out=XT[:], in0=PS[bB][:], in1=XT[:], op=ALU.add),
                                 r=[T_PS[bB], B["t_XT"]], w=[B["t_XT"]])
                        yield

                if dr_ == 0:
                    fi = s - 1 if s > 0 else None
                else:
                    fi = s if s + 1 < nseg else None
                if fi is None:
                    S.op("dve", lambda: V_.memset(Sst[:], 0.0), w=[t_S])
                else:
                    S.op("dve", lambda fi=fi: V_.tensor_scalar(out=Sst[:], in0=Sst[:], scalar1=flags[:, fi:fi + 1], scalar2=None,
                                                               op0=ALU.mult), r=[t_S, t_flags], w=[t_S])
                S.op("act", lambda: A_.copy(out=Sbf[:], in_=Sst[:]), r=[t_S], w=[t_Sbf])

                order = list(range(NCH)) if dr_ == 0 else list(range(NCH - 1, -1, -1))

                def scan_gen(n_, c):
                    cs = slice(c * 128, (c + 1) * 128)
                    ob = n_ % 2
                    tg = s * 4 + c // 4
                    t_X = t_Yc[c]
                    if dr_ == 1:
                        S.dma("sp", ofl[ob][:], OF.ap()[t0 + c * 128:t0 + (c + 1) * 128, :], r=[T_OF[tg]], w=[t_ofl[ob]])
                        S.dma("sp", zt_[ob][:], ZZ.ap()[t0 + c * 128:t0 + (c + 1) * 128, :], r=[T_ZZ[tg]], w=[t_zt[ob]])

                    def tpk():
                        ins = None
                        for h in range(4):
                            ins = P_.transpose(out=psbf(4)[:, h * 128:(h + 1) * 128], in_=qkv[:, 4 + h, cs], identity=ident_b)
                        return ins

                    def tpv():
                        ins = None
                        for h in range(4):
                            ins = P_.transpose(out=psbf(4)[:, 512 + h * 128:512 + (h + 1) * 128], in_=qkv[:, 8 + h, cs], identity=ident_b)
                        return ins

                    def tpkv():
                        tpk()
                        return tpv()
                    S.op("pe", tpkv, r=[t_qkv, t_cbf], w=[T_PS[4]])
                    yield
                    S.op("dve", lambda: V_.tensor_tensor(out=v4(kd[ob][:]), in0=v4(psbf(4)[:, 0:512]), in1=bc4(kds[:, c, :]),
                                                         op=ALU.mult), r=[T_PS[4], t_gate], w=[t_kd[ob]])

                    def mmz():
                        ins = None
                        for h in range(4):
                            hs = slice(h * 128, (h + 1) * 128)
                            ins = P_.matmul(PS[5][:, hs], lhsT=qkv[:, 4 + h, cs], rhs=Sbf[:, hs], start=True, stop=True)
                        return ins
                    S.op("pe", mmz, r=[t_qkv, t_Sbf], w=[T_PS[5]])

                    def mmp1():
                        ins = None
                        for h in range(4):
                            hs = slice(h * 128, (h + 1) * 128)
                            ins = P_.matmul(PS[6][:, hs], lhsT=qkv[:, h, cs], rhs=Sbf[:, hs], start=True, stop=True)
                        return ins
                    S.op("pe", mmp1, r=[t_qkv, t_Sbf], w=[T_PS[6]])
                    yield
                    S.op("dve", lambda: V_.tensor_tensor(out=v4(tmpc[0][:]), in0=v4(PS[5][:]), in1=bc4(eG[:, c, :]), op=ALU.mult),
                         r=[T_PS[5], t_gate], w=[t_tmpc[0]])
                    yield
                    S.op("dve", lambda: V_.tensor_tensor(out=Rr[:], in0=psbf(4)[:, 512:1024], in1=tmpc[0][:], op=ALU.subtract),
                         r=[T_PS[4], t_tmpc[0]], w=[t_R])
                    yield

                    def mmv():
                        ins = None
                        for h in range(4):
                            hs = slice(h * 128, (h + 1) * 128)
                            ins = P_.matmul(PS[7][:, hs], lhsT=Yk[:, c, hs], rhs=Rr[:, hs], start=True, stop=True)
                        return ins
                    S.op("pe", mmv, r=[t_X, t_R], w=[T_PS[7]])
                    yield
                    S.op("act", lambda: A_.activation(out=tmpc[1][:], in_=PS[6][:], func=AF.Copy), r=[T_PS[6]], w=[t_tmpc[1]])
                    S.op("dve", lambda: V_.tensor_tensor(out=v4(vnew[:]), in0=v4(PS[7][:]), in1=bc4(beta[:, c, :]), op=ALU.mult),
                         r=[T_PS[7], t_gate], w=[t_vnew])
                    yield

                    def mmp2():
                        ins = None
                        for h in range(4):
                            hs = slice(h * 128, (h + 1) * 128)
                            ins = P_.matmul(PS[7][:, hs], lhsT=qkT[:, c, hs], rhs=vnew[:, hs], start=True, stop=True)
                        return ins
                    S.op("pe", mmp2, r=[t_qkT, t_vnew], w=[T_PS[7]])

                    def mms():
                        ins = None
                        for h in range(4):
                            hs = slice(h * 128, (h + 1) * 128)
                            ins = P_.matmul(PS[5][:, hs], lhsT=kd[ob][:, hs], rhs=vnew[:, hs], start=True, stop=True)
                        return ins
                    S.op("pe", mms, r=[t_kd[ob], t_vnew], w=[T_PS[5]])
                    yield
                    S.op("dve", lambda: V_.tensor_tensor(out=v4(Sst[:]), in0=v4(Sst[:]), in1=bc4(egt[:, c, :]), op=ALU.mult),
                         r=[t_S, t_gate], w=[t_S])
                    yield
                    S.op("dve", lambda: V_.tensor_tensor(out=Sst[:], in0=PS[5][:], in1=Sst[:], op=ALU.add), r=[T_PS[5], t_S], w=[t_S])
                    S.op("act", lambda: A_.copy(out=Sbf[:], in_=Sst[:]), r=[t_S], w=[t_Sbf])
                    yield
                    S.op("pool", lambda: G_.tensor_tensor(out=v4(tmpc[1][:]), in0=v4(tmpc[1][:]), in1=bc4(eG[:, c, :]), op=ALU.mult),
                         r=[t_tmpc[1], t_gate], w=[t_tmpc[1]])
                    yield
                    S.op("dve", lambda: V_.tensor_tensor(out=och[ob][:], in0=PS[7][:], in1=tmpc[1][:], op=ALU.add),
                         r=[T_PS[7], t_tmpc[1]], w=[t_och[ob]])
                    yield
                    if dr_ == 0:
                        S.dma("sp", OF.ap()[t0 + c * 128:t0 + (c + 1) * 128, :], och[ob][:], r=[t_och[ob]], w=[T_OF[tg]])
                    else:
                        S.op("dve", lambda: V_.tensor_tensor(out=och[ob][:], in0=och[ob][:], in1=ofl[ob][:], op=ALU.add),
                             r=[t_och[ob], t_ofl[ob]], w=[t_och[ob]])
                        yield
                        for h in range(4):
                            S.op("act", lambda h=h: A_.activation(out=tmpc[1][:, h * 128:(h + 1) * 128],
                                                                  in_=och[ob][:, h * 128:(h + 1) * 128], func=AF.Square,
                                                                  accum_out=ms[:, h:h + 1]),
                                 r=[t_och[ob]], w=[t_tmpc[1], t_ms])
                        yield
                        S.op("act", lambda: A_.activation(out=ms[:, 4:8], in_=ms[:, 0:4], func=AF.Ln, bias=epsD[:, 0:1], scale=1.0 / 128),
                             r=[t_ms, t_eps], w=[t_ms])
                        S.op("act", lambda: A_.activation(out=ms[:, 4:8], in_=ms[:, 4:8], func=AF.Exp, scale=-0.5), r=[t_ms], w=[t_ms])
                        yield
                        S.op("dve", lambda: V_.tensor_tensor(out=v4(och[ob][:]), in0=v4(och[ob][:]), in1=bc4(ms[:, 4:8]), op=ALU.mult),
                             r=[t_och[ob], t_ms], w=[t_och[ob]])
                        S.op("act", lambda: A_.activation(out=tmpc[0][:], in_=zt_[ob][:], func=AF.Silu), r=[t_zt[ob]], w=[t_tmpc[0]])
                        yield
                        S.op("pool", lambda: G_.tensor_tensor(out=v4(tmpc[0][:]), in0=v4(tmpc[0][:]),
                                                              in1=dng[:].unsqueeze(1).to_broadcast([128, 4, 128]), op=ALU.mult),
                             r=[t_tmpc[0], t_hp], w=[t_tmpc[0]])
                        yield
                        S.op("dve", lambda: V_.tensor_tensor(out=odtok[:], in0=och[ob][:], in1=tmpc[0][:], op=ALU.mult),
                             r=[t_och[ob], t_tmpc[0]], w=[t_odtok])
                        yield

                        def tpo():
                            ins = None
                            for h in range(4):
                                ins = P_.transpose(out=psbf(6)[:, h * 128:(h + 1) * 128], in_=odtok[:, h * 128:(h + 1) * 128],
                                                   identity=ident_b)
                            return ins
                        S.op("pe", tpo, r=[t_odtok, t_cbf], w=[T_PS[6]])
                        yield
                        S.op("act", lambda: A_.copy(out=odt[ob][:], in_=psbf(6)[:, 0:512].rearrange("p (h t) -> p h t", h=4)),
                             r=[T_PS[6]], w=[t_odt[ob]])
                        S.dma("sp", ODT.ap()[:, t0 + c * 128:t0 + (c + 1) * 128].rearrange("(h p) t -> p h t", p=128), odt[ob][:],
                              r=[t_odt[ob]], w=[T_ODT[tg]])

                t_Yc = [Trk() for _ in range(NCH)]
                prep_q = list(order)
                active = {}
                prep_done = set()
                scan_n = 0
                scan_g = None
                while scan_n < NCH:
                    for k in range(NSLOT):
                        if k not in active and prep_q:
                            c_ = prep_q.pop(0)
                            active[k] = (prep_gen(c_, k), c_)
                    if scan_g is None and order[scan_n] in prep_done:
                        scan_g = scan_gen(scan_n, order[scan_n])
                    for k in list(active.keys()):
                        g_, c_ = active[k]
                        try:
                            next(g_)
                        except StopIteration:
                            prep_done.add(c_)
                            del active[k]
                    if scan_g is not None:
                        try:
                            next(scan_g)
                        except StopIteration:
                            scan_g = None
                            scan_n += 1

            for s in range(nseg):
                seg_pass(s, 0)
            for s in range(nseg - 1, -1, -1):
                seg_pass(s, 1)
            S.barrier()

    def phase_D(l):
        with ExitStack() as pes:
            def sb(name, shape, dt):
                return pes.enter_context(nc.sbuf_tensor(name + "_L%d" % l, list(shape), dt))
            wba = sb("wba", [128, 4, D], BF16)
            wbd = sb("wbd", [128, 4, D], BF16)
            wo = sb("wo", [128, 8, D], BF16)
            t_wba, t_wbd, t_wo = Trk(), Trk(), Trk()
            S.dma("pool", wba[:], wbra_in.ap()[l].rearrange("(k p) c -> p k c", p=128), w=[t_wba])
            S.dma("pool", wbd[:], wbrd_in.ap()[l].rearrange("(k p) c -> p k c", p=128), w=[t_wbd])
            S.dma("pool", wo[:], wout_in.ap()[l].rearrange("(k p) c -> p k c", p=128), w=[t_wo])
            NB = 3
            oa = [sb("oa%d" % i, [128, 4, 512], BF16) for i in range(NB)]
            od = [sb("od%d" % i, [128, 4, 512], BF16) for i in range(NB)]
            gt = [sb("gtD%d" % i, [128, 16, 512], BF16) for i in range(NB)]
            xT = [sb("xTD%d" % i, [128, 8, 512], F32) for i in range(NB)]
            t_in = [Trk() for _ in range(NB)]
            t_x = [Trk() for _ in range(NB)]
            sgA = [sb("sgA%d" % i, [128, 512], F32) for i in range(2)]
            sgD = [sb("sgD%d" % i, [128, 512], F32) for i in range(2)]
            m1 = [sb("m1D%d" % i, [128, 512], F32) for i in range(2)]
            m2 = [sb("m2D%d" % i, [128, 512], F32) for i in range(2)]
            t_sgA, t_sgD, t_m1, t_m2 = ([Trk(), Trk()] for _ in range(4))
            mg = [sb("mgD%d" % i, [128, 8, 512], BF16) for i in range(2)]
            t_mg = [Trk(), Trk()]

            def loads(t):
                b = t % NB
                tk = slice(t * 512, (t + 1) * 512)
                S.dma("sp", oa[b][:], OAT.ap()[:, tk].rearrange("(c p) t -> p c t", p=128), r=[T_OAT[t]], w=[t_in[b]])
                S.dma("sp", od[b][:], ODT.ap()[:, tk].rearrange("(c p) t -> p c t", p=128), r=[T_ODT[t]], w=[t_in[b]])
                S.dma("sp", gt[b][:], GT.ap()[:, tk].rearrange("(c p) t -> p c t", p=128), r=[T_GT[t]], w=[t_in[b]])
                S.dma("sp", xT[b][:], XT.ap()[:, tk].rearrange("(c p) t -> p c t", p=128), r=[T_XT[t]], w=[t_x[b]])

            def br(t):
                b = t % NB
                mb = t % 2
                for c in range(8):
                    q2 = c % 2
                    ba, bd = 2 * q2, 2 * q2 + 1

                    def mma(c=c, ba=ba):
                        ins = None
                        for k in range(4):
                            ins = P_.matmul(PS[ba][:], lhsT=wba[:, k, c * 128:(c + 1) * 128], rhs=oa[b][:, k, :], start=(k == 0), stop=(k == 3))
                        return ins
                    S.op("pe", mma, r=[t_wba, t_in[b]], w=[T_PS[ba]])

                    def mmd(c=c, bd=bd):
                        ins = None
                        for k in range(4):
                            ins = P_.matmul(PS[bd][:], lhsT=wbd[:, k, c * 128:(c + 1) * 128], rhs=od[b][:, k, :], start=(k == 0), stop=(k == 3))
                        return ins
                    S.op("pe", mmd, r=[t_wbd, t_in[b]], w=[T_PS[bd]])
                    S.op("act", lambda c=c, q2=q2: A_.activation(out=sgA[q2][:], in_=gt[b][:, c, :], func=AF.Sigmoid), r=[t_in[b]], w=[t_sgA[q2]])
                    S.op("act", lambda c=c, q2=q2: A_.activation(out=sgD[q2][:], in_=gt[b][:, 8 + c, :], func=AF.Sigmoid), r=[t_in[b]], w=[t_sgD[q2]])
                    S.op("dve", lambda ba=ba, q2=q2: V_.tensor_tensor(out=m1[q2][:], in0=PS[ba][:], in1=sgA[q2][:], op=ALU.mult),
                         r=[T_PS[ba], t_sgA[q2]], w=[t_m1[q2]])
                    S.op("dve", lambda bd=bd, q2=q2: V_.tensor_tensor(out=m2[q2][:], in0=PS[bd][:], in1=sgD[q2][:], op=ALU.mult),
                         r=[T_PS[bd], t_sgD[q2]], w=[t_m2[q2]])
                    S.op("pool", lambda c=c, q2=q2: G_.tensor_tensor(out=mg[mb][:, c, :], in0=m1[q2][:], in1=m2[q2][:], op=ALU.add),
                         r=[t_m1[q2], t_m2[q2]], w=[t_mg[mb]])

            def outp(t):
                b = t % NB
                mb = t % 2
                seg = t // 4
                for c in range(8):
                    bank = 4 + (c % 4)

                    def mmo(c=c, bank=bank):
                        ins = None
                        for k in range(8):
                            ins = P_.matmul(PS[bank][:], lhsT=wo[:, k, c * 128:(c + 1) * 128], rhs=mg[mb][:, k, :], start=(k == 0), stop=(k == 7))
                        return ins
                    S.op("pe", mmo, r=[t_wo, t_mg[mb]], w=[T_PS[bank]])
                    S.op("dve", lambda c=c, bank=bank: V_.scalar_tensor_tensor(
                        out=xT[b][:, c, :], in0=PS[bank][:], scalar=modv(l, 2, c, seg), in1=xT[b][:, c, :], op0=ALU.mult, op1=ALU.add),
                        r=[T_PS[bank], t_x[b], t_mod], w=[t_x[b]])
                S.dma("sp", XT.ap()[:, t * 512:(t + 1) * 512].rearrange("(c p) t -> p c t", p=128), xT[b][:], r=[t_x[b]], w=[T_XT[t]])

            loads(0)
            if nt > 1:
                loads(1)
            br(0)
            for t in range(nt):
                if t + 2 < nt:
                    loads(t + 2)
                if t + 1 < nt:
                    br(t + 1)
                outp(t)
            S.barrier()

    def phase_E(l, last):
        TE = 256
        nte = ntok // TE
        with ExitStack() as pes:
            def sb(name, shape, dt):
                return pes.enter_context(nc.sbuf_tensor(name + "_L%d" % l, list(shape), dt))
            w1 = sb("w1", [128, 8, DFF], BF16)
            w2 = sb("w2", [128, 32, D], BF16)
            t_w1p = [Trk() for _ in range(4)]
            t_w2p = [Trk() for _ in range(4)]
            for cp in range(4):
                S.dma("pool", w1[:, :, cp * 1024:(cp + 1) * 1024], w1_in.ap()[l, :, cp * 1024:(cp + 1) * 1024].rearrange("(k p) c -> p k c", p=128),
                      w=[t_w1p[cp]])
            for kp in range(4):
                S.dma("pool", w2[:, kp * 8:(kp + 1) * 8, :], w2_in.ap()[l, kp * 1024:(kp + 1) * 1024, :].rearrange("(k p) c -> p k c", p=128),
                      w=[t_w2p[kp]])
            xT = [sb("xTE%d" % i, [128, 8, TE], F32) for i in range(2)]
            t_x = [Trk(), Trk()]
            nb_ = 1 if last else 2
            xsq2 = [sb("xsqE%d" % i, [128, 8, TE], BF16) for i in range(nb_)] * (2 // nb_)
            t_xsq2 = [Trk() for _ in range(nb_)] * (2 // nb_)
            hT2 = [sb("hTE%d" % i, [128, 8, TE], BF16) for i in range(nb_)] * (2 // nb_)
            t_h2 = [Trk() for _ in range(nb_)] * (2 // nb_)
            xsq = xsq2[0]
            t_xsq = t_xsq2[0]
            rstd = sb("rstdE", [128, TE], F32)
            t_rstd = Trk()
            tmp = [sb("tmpE%d" % i, [128, TE], F32) for i in range(2)]
            t_tmp = [Trk(), Trk()]
            rl = [sb("rlE%d" % i, [128, 2, TE], BF16) for i in range(2)]
            t_rl = [Trk(), Trk()]
            hid = sb("hidE", [128, 32, TE], BF16)
            t_hid = Trk()
            if last:
                yT = sb("yTE", [128, 8, TE], F32)
                t_yT = Trk()
                ytok = sb("ytokE", [128, 2, D], F32)
                t_ytok = Trk()

            def loads(t):
                b = t % 2
                S.dma("sp", xT[b][:], XT.ap()[:, t * TE:(t + 1) * TE].rearrange("(c p) t -> p c t", p=128), r=[T_XT[t // 2]], w=[t_x[b]])

            def prep(t):
                b = t % 2
                norm_mod(xT[b], t_x[b], xsq2[b], t_xsq2[b], hT2[b], t_h2[b], rstd, t_rstd, tmp, t_tmp, 2, A2, l, 3, (t * TE) // SEG, 0, TE)

            loads(0)
            prep(0)
            for t in range(nte):
                b = t % 2
                seg = (t * TE) // SEG
                if t + 1 < nte:
                    loads(t + 1)
                hT = hT2[b]
                t_h = t_h2[b]
                for c2 in range(16):
                    bank = 1 + (c2 % 3)
                    r2 = c2 % 2

                    def mm1(c2=c2, bank=bank):
                        ins = None
                        for j in range(2):
                            cc = (c2 * 2 + j) * 128
                            for k in range(8):
                                ins = P_.matmul(PS[bank][:, j * TE:(j + 1) * TE], lhsT=w1[:, k, cc:cc + 128], rhs=hT[:, k, :],
                                                start=(k == 0), stop=(k == 7))
                        return ins
                    S.op("pe", mm1, r=[t_w1p[(c2 * 256) // 1024], t_h], w=[T_PS[bank]])
                    S.op("act", lambda bank=bank, r2=r2: A_.activation(out=rl[r2][:].rearrange("p j t -> p (j t)"), in_=PS[bank][:],
                                                                       func=AF.Relu), r=[T_PS[bank]], w=[t_rl[r2]])
                    S.op("dve", lambda c2=c2, r2=r2: V_.tensor_tensor(out=hid[:, c2 * 2:c2 * 2 + 2, :], in0=rl[r2][:], in1=rl[r2][:], op=ALU.mult),
                         r=[t_rl[r2]], w=[t_hid])
                    if c2 == 11 and t + 1 < nte and not last:
                        prep(t + 1)
                if last and t + 1 < nte:
                    pass
                for c2 in range(4):
                    bank = 4 + (c2 % 4)

                    def mm2(c2=c2, bank=bank):
                        ins = None
                        for j in range(2):
                            cc = (c2 * 2 + j) * 128
                            for k in range(32):
                                ins = P_.matmul(PS[bank][:, j * TE:(j + 1) * TE], lhsT=w2[:, k, cc:cc + 128], rhs=hid[:, k, :],
                                                start=(k == 0), stop=(k == 31))
                        return ins
                    S.op("pe", mm2, r=t_w2p + [t_hid], w=[T_PS[bank]])
                    for j in range(2):
                        c = c2 * 2 + j
                        S.op("dve", lambda c=c, j=j, bank=bank: V_.scalar_tensor_tensor(
                            out=xT[b][:, c, :], in0=PS[bank][:, j * TE:(j + 1) * TE], scalar=modv(l, 5, c, seg), in1=xT[b][:, c, :],
                            op0=ALU.mult, op1=ALU.add), r=[T_PS[bank], t_x[b], t_mod], w=[t_x[b]])
                if not last:
                    S.dma("sp", XT.ap()[:, t * TE:(t + 1) * TE].rearrange("(c p) t -> p c t", p=128), xT[b][:], r=[t_x[b]], w=[T_XT[t // 2]])
                else:
                    for dch in range(8):
                        S.op("act", lambda dch=dch: A_.activation(out=xsq[:, dch, :], in_=xT[b][:, dch, :], func=AF.Square),
                             r=[t_x[b]], w=[t_xsq])

                    def mmf():
                        ins = None
                        for dch in range(8):
                            ins = P_.matmul(PS[0][:, 0:TE], lhsT=ones_b, rhs=xsq[:, dch, :], start=(dch == 0), stop=(dch == 7))
                        return ins
                    S.op("pe", mmf, r=[t_xsq, t_cbf], w=[T_PS[0]])
                    S.op("act", lambda: A_.activation(out=rstd[:], in_=PS[0][:, 0:TE], func=AF.Ln, bias=epsD[:, 0:1], scale=1.0 / D),
                         r=[T_PS[0], t_eps], w=[t_rstd])
                    S.op("act", lambda: A_.activation(out=rstd[:], in_=rstd[:], func=AF.Exp, scale=-0.5), r=[t_rstd], w=[t_rstd])
                    for dch in range(8):
                        S.op("dve", lambda dch=dch: V_.scalar_tensor_tensor(
                            out=yT[:, dch, :], in0=xT[b][:, dch, :], scalar=gf[:, dch:dch + 1], in1=rstd[:], op0=ALU.mult, op1=ALU.mult),
                            r=[t_x[b], t_rstd, t_par], w=[t_yT])
                    for blk in range(TE // 128):
                        for half in range(2):
                            bank = 1 + half

                            def tpf(blk=blk, half=half, bank=bank):
                                ins = None
                                for j in range(4):
                                    dch = half * 4 + j
                                    ins = P_.transpose(out=PS[bank][:, j * 128:(j + 1) * 128], in_=yT[:, dch, blk * 128:(blk + 1) * 128],
                                                       identity=ident_f)
                                return ins
                            S.op("pe", tpf, r=[t_yT, t_cst], w=[T_PS[bank]])
                            if half == 0:
                                S.op("act", lambda blk=blk, bank=bank: A_.copy(out=ytok[:, blk, 0:512], in_=PS[bank][:]), r=[T_PS[bank]], w=[t_ytok])
                            else:
                                S.op("dve", lambda blk=blk, bank=bank: V_.tensor_copy(out=ytok[:, blk, 512:1024], in_=PS[bank][:]),
                                     r=[T_PS[bank]], w=[t_ytok])
                    S.dma("sp", y_out.ap()[t * TE:(t + 1) * TE, :].rearrange("(b p) d -> p b d", p=128), ytok[:], r=[t_ytok], w=[Trk()])
                    if t + 1 < nte:
                        prep(t + 1)
            S.barrier()

    phases_all = phases
    for l in range(depth):
        phases = phases_last if (phases_last is not None and l == depth - 1) else phases_all
        if "A" in phases:
            phase_A(l)
        if "B" in phases:
            phase_B(l)
        if "C" in phases:
            phase_C(l)
        if "D" in phases:
            phase_D(l)
        if "E" in phases:
            phase_E(l, last=(l == depth - 1))
    S.barrier()
    es.close()
    return nc, S


def make_consts():
    c = np.zeros((128, 1024), np.float32)
    p = np.arange(128)[:, None]
    i = np.arange(128)[None, :]
    c[:, 0:128] = (p == i)
    c[:, 128:256] = (p <= i)
    c[:, 256:384] = (p >= i)
    c[:, 384:512] = np.where(p > i, NEGBIG, 0.0)
    c[:, 512:640] = np.where(p < i, NEGBIG, 0.0)
    c[:, 640:768] = (p < i)
    c[:, 768:896] = (p > i)
    jj = np.arange(64)
    c[0:64, 896:960] = (jj[:, None] + jj[None, :] == 63)
    qc = np.arange(64)
    qs = np.clip(qc - 8, 0, 48)
    kc = np.arange(64)
    valid = (kc[:, None] >= qs[None, :]) & (kc[:, None] < qs[None, :] + 16)
    c[0:64, 960:1024] = valid
    c[64:128, 960:1024] = valid
    return c


def make_consts2():
    c = np.zeros((128, 640), np.float32)
    p = np.arange(128)[:, None]
    i = np.arange(128)[None, :]
    prev = (p // 8 == i // 8)
    c[:, 0:128] = prev
    for m, b in enumerate((16, 32, 64, 128)):
        cur = (p // b == i // b)
        c[:, (m + 1) * 128:(m + 2) * 128] = cur & ~prev
        prev = cur
    return c


def pp(v, nchunk):
    v = np.asarray(v, np.float32)
    lead = v.shape[:-1]
    v = v.reshape(lead + (nchunk, 128))
    v = np.moveaxis(v, -1, 0)
    return np.ascontiguousarray(v.reshape(128, -1))


def core_plan():
    plan = []
    for c in range(NCORES):
        if c < 2:
            segs = [("p", c, i * SEG) for i in range(4)] + [("s", c, 0)]
            flags = [1, 1, 1, 0, 0, 0, 0, 0]
        else:
            segs = [("s", 2 + (c - 2) * 5 + i, 0) for i in range(5)]
            flags = [0] * 8
        plan.append((segs, flags))
    return plan


def shared_inputs(norm_mix_g, norm_mlp_g, w_ada, b_ada, w_in, na_rpb, dn_conv, dn_a_log, dn_dt_bias, dn_norm_g,
                  w_br_attn, w_br_dn, w_out, w_mlp1, w_mlp2, final_norm_g, depth=DEPTH):
    f = lambda a: np.ascontiguousarray(np.asarray(a, np.float32))
    rp = np.zeros((depth, 8, 15, 128), np.float32)
    rp[:, :, :, 48:79] = np.asarray(na_rpb, np.float32)[:depth]
    cw = np.asarray(dn_conv, np.float32)[:depth].reshape(depth, 5, 12, 128)
    cw = np.ascontiguousarray(np.moveaxis(cw, -1, 0).reshape(128, depth * 60))
    return {
        "g1": pp(np.asarray(norm_mix_g)[:depth], 8), "g2": pp(np.asarray(norm_mlp_g)[:depth], 8), "gf": pp(final_norm_g, 8),
        "bada": pp(np.asarray(b_ada)[:depth], 48), "w_ada": f(w_ada)[:depth], "w_in": f(w_in)[:depth], "rpbp": rp, "convw": cw,
        "alog": f(dn_a_log)[:depth].reshape(1, depth * 8), "dtb": f(dn_dt_bias)[:depth].reshape(1, depth * 8),
        "dng": f(dn_norm_g)[:depth].reshape(1, depth * 128),
        "w_br_attn": f(w_br_attn)[:depth], "w_br_dn": f(w_br_dn)[:depth], "w_out": f(w_out)[:depth],
        "w_mlp1": f(w_mlp1)[:depth], "w_mlp2": f(w_mlp2)[:depth], "consts": make_consts(), "consts2": make_consts2(),
    }


def core_inputs(segs, flags, x_prompt, x_sample, c_prompt, c_sample):
    xs, cs = [], []
    for (g, b, st) in segs:
        if g == "p":
            xs.append(np.asarray(x_prompt[b, st:st + SEG], np.float32))
            cs.append(np.asarray(c_prompt[b], np.float32))
        else:
            xs.append(np.asarray(x_sample[b, st:st + SEG], np.float32))
            cs.append(np.asarray(c_sample[b], np.float32))
    x = np.ascontiguousarray(np.concatenate(xs, axis=0))
    cm = np.zeros((8, D), np.float32)
    cm[:len(cs)] = np.stack(cs)
    cT = np.ascontiguousarray(cm.reshape(8, 8, 128).transpose(2, 1, 0))
    return {"x": x, "cT": cT, "flags": np.asarray(flags, np.float32).reshape(1, 8)}


def kernel(x_prompt, x_sample, c_prompt, c_sample, norm_mix_g, norm_mlp_g, w_ada, b_ada, w_in, na_rpb, dn_conv,
           dn_a_log, dn_dt_bias, dn_norm_g, w_br_attn, w_br_dn, w_out, w_mlp1, w_mlp2, final_norm_g):
    x_prompt = np.asarray(x_prompt)
    x_sample = np.asarray(x_sample)
    c_prompt = np.asarray(c_prompt)
    c_sample = np.asarray(c_sample)
    nc, _ = build_program()
    shared = shared_inputs(norm_mix_g, norm_mlp_g, w_ada, b_ada, w_in, na_rpb, dn_conv, dn_a_log, dn_dt_bias, dn_norm_g,
                           w_br_attn, w_br_dn, w_out, w_mlp1, w_mlp2, final_norm_g)
    plan = core_plan()
    in_maps = []
    for (segs, flags) in plan:
        m = dict(shared)
        m.update(core_inputs(segs, flags, x_prompt, x_sample, c_prompt, c_sample))
        in_maps.append(m)
    res = run_bass_kernel_spmd(nc, in_maps, core_ids=list(range(NCORES)))
    y_prompt = np.zeros(x_prompt.shape, np.float32)
    y_sample = np.zeros(x_sample.shape, np.float32)
    for ci, (segs, flags) in enumerate(plan):
        y = np.asarray(res.results[ci]["y"])
        for si, (g, b, st) in enumerate(segs):
            blk = y[si * SEG:(si + 1) * SEG]
            if g == "p":
                y_prompt[b, st:st + SEG] = blk
            else:
                y_sample[b] = blk
    return (y_prompt, y_sample)
```

```python
import os
import numpy as np
import ml_dtypes
from contextlib import ExitStack
import concourse.bass as bass
import concourse.mybir as mybir
from concourse.bass_utils import run_bass_kernel_spmd

F32 = mybir.dt.float32
BF16 = mybir.dt.bfloat16
AF = mybir.ActivationFunctionType
ALU = mybir.AluOpType

D = 1024
DEPTH = 2
NCORES = 8
SEG = 2048
IN_COLS = 5648
DFF = 4096
EPS = 1e-6
PADT = 256
NEGBIG = -30000.0

C_NQ, C_NK, C_NV = 0, 512, 1024
C_DQ = 1536
C_Z = 3072
C_AB = 3584
C_G = 3600


class Trk:
    __slots__ = ("w", "r", "excl")

    def __init__(self, excl=False):
        self.w = None
        self.r = []
        self.excl = excl


class Sched:
    def __init__(self, nc, es):
        self.nc = nc
        self.eng = {"pe": nc.tensor, "act": nc.scalar, "dve": nc.vector, "pool": nc.gpsimd, "sp": nc.sync}
        self.sem = {}
        self.cnt = {}
        for e in self.eng:
            self.sem[e] = es.enter_context(nc.semaphore("sem_" + e))
            self.cnt[e] = 0
        self.waited = {e: {} for e in self.eng}
        self.nslot = {"sp": 12, "pool": 6}
        self.slots = {}
        for q, n in self.nslot.items():
            self.slots[q] = []
            for i in range(n):
                key = "dq_%s_%d" % (q, i)
                self.sem[key] = es.enter_context(nc.semaphore(key))
                self.cnt[key] = 0
                self.slots[q].append(key)
        self.rr = {q: 0 for q in self.nslot}
        self.nwait = 0
        self.nops = 0

    def _wait(self, e, deps):
        best = {}
        for d in deps:
            if d is None:
                continue
            k, v = d
            if best.get(k, 0) < v:
                best[k] = v
        for k, v in best.items():
            if self.waited[e].get(k, 0) < v:
                self.eng[e].wait_ge(self.sem[k], v)
                self.waited[e][k] = v
                self.nwait += 1

    def _deps(self, e, r, w):
        deps = []
        for t in r:
            deps.append(t.w)
            if t.excl:
                for rd in t.r:
                    if rd[0] != e:
                        deps.append(rd)
        for t in w:
            deps.append(t.w)
            for rd in t.r:
                if rd[0] != e:
                    deps.append(rd)
        return deps

    def _stamp(self, st, r, w):
        for t in r:
            t.r.append(st)
        for t in w:
            t.w = st
            t.r = []

    def op(self, e, fn, r=(), w=()):
        self._wait(e, self._deps(e, r, w))
        ins = fn()
        self.cnt[e] += 1
        ins.then_inc(self.sem[e], 1)
        self._stamp((e, self.cnt[e]), r, w)
        self.nops += 1

    def dma(self, q, out, in_, r=(), w=(), **kw):
        key = self.slots[q][self.rr[q]]
        self.rr[q] = (self.rr[q] + 1) % self.nslot[q]
        deps = self._deps(q, r, w)
        deps.append((key, self.cnt[key]))
        self._wait(q, deps)
        self.eng[q].dma_start(out=out, in_=in_, **kw).then_inc(self.sem[key], 16)
        self.cnt[key] += 16
        self._stamp((key, self.cnt[key]), r, w)
        self.nops += 1

    def barrier(self):
        allst = [(k, v) for k, v in self.cnt.items() if v > 0]
        for e in self.eng:
            self._wait(e, [d for d in allst if d[0] != e])


def _bf(a):
    return np.asarray(a, dtype=np.float32)


def build_program(nseg=5, depth=DEPTH, debug=False, phases="ABCDE", phases_last=None):
    ntok = nseg * SEG
    nt = ntok // 512
    nc = bass.Bass("TRN2", target_bir_lowering=False)
    es = ExitStack()
    S = Sched(nc, es)

    def din(name, shape, dt=F32):
        return nc.dram_tensor(name, list(shape), dt, kind="ExternalInput")

    okind = "ExternalOutput" if debug else "Internal"

    def dscr(name, shape, dt):
        return nc.dram_tensor(name, list(shape), dt, kind=okind)

    x_in = din("x", [ntok, D])
    cT_in = din("cT", [128, 8, 8])
    flags_in = din("flags", [1, 8])
    g1_in = din("g1", [128, depth * 8])
    g2_in = din("g2", [128, depth * 8])
    gf_in = din("gf", [128, 8])
    bada_in = din("bada", [128, depth * 48])
    wada_in = din("w_ada", [depth, D, 6 * D])
    win_in = din("w_in", [depth, D, IN_COLS])
    rpb_in = din("rpbp", [depth, 8, 15, 128])
    conv_in = din("convw", [128, depth * 5 * 12])
    alog_in = din("alog", [1, depth * 8])
    dtb_in = din("dtb", [1, depth * 8])
    dng_in = din("dng", [1, depth * 128])
    wbra_in = din("w_br_attn", [depth, 512, D])
    wbrd_in = din("w_br_dn", [depth, 512, D])
    wout_in = din("w_out", [depth, D, D])
    w1_in = din("w_mlp1", [depth, D, DFF])
    w2_in = din("w_mlp2", [depth, DFF, D])
    cst_in = din("consts", [128, 1024])
    cst2_in = din("consts2", [128, 640])
    y_out = nc.dram_tensor("y", [ntok, D], F32, kind="ExternalOutput")

    XT = dscr("XT", [D, ntok], F32)
    QT = dscr("QT", [512, ntok], BF16)
    KT = dscr("KT", [512, ntok + 2 * PADT], BF16)
    VV = dscr("VV", [ntok + 2 * PADT, 512], BF16)
    DQ = dscr("DQ", [1536, ntok + 128], BF16)
    ZZ = dscr("ZZ", [ntok, 512], BF16)
    ABs = dscr("ABs", [128, ntok // 128, 16], F32)
    GT = dscr("GT", [2048, ntok], BF16)
    OAT = dscr("OAT", [512, ntok], BF16)
    ODT = dscr("ODT", [512, ntok], BF16)
    OF = dscr("OF", [ntok, 512], F32)
    QN = dscr("QN", [1536, ntok], BF16)

    def dtr(n):
        return [Trk() for _ in range(n)]

    T_XT, T_QT, T_DQ, T_ZZ, T_AB, T_GT, T_OAT, T_ODT, T_OF, T_QN = (dtr(nt) for _ in range(10))
    T_KT = dtr(nt + 2)
    T_VV = dtr(nt + 2)

    def sbp(name, shape, dt):
        return es.enter_context(nc.sbuf_tensor(name, list(shape), dt))

    cst = sbp("cst", [128, 1024], F32)
    t_cst = Trk()
    ident_f = cst[:, 0:128]
    TRI = [cst[:, 128:256], cst[:, 256:384]]
    NEGM = [cst[:, 384:512], cst[:, 512:640]]
    cbf = sbp("cbf", [128, 512], BF16)
    t_cbf = Trk()
    ident_b = cbf[:, 0:128]
    ones_b = cbf[:, 128:256]
    STRICT = [cbf[:, 256:384], cbf[:, 384:512]]
    cb2 = sbp("cb2", [128, 640], BF16)
    t_cb2 = Trk()
    ones_f = sbp("ones_f", [128, 128], F32)
    t_onesf = Trk()
    flags = sbp("flags_sb", [128, 8], F32)
    t_flags = Trk()
    g1 = sbp("g1_sb", [128, depth * 8], F32)
    g2 = sbp("g2_sb", [128, depth * 8], F32)
    gf = sbp("gf_sb", [128, 8], F32)
    bada = sbp("bada_sb", [128, depth * 48], F32)
    mod = sbp("mod_sb", [128, depth * 48, 8], F32)
    A1 = sbp("A1_sb", [128, depth * 8, 8], F32)
    A2 = sbp("A2_sb", [128, depth * 8, 8], F32)
    t_par = Trk()
    t_mod = Trk()

    PS = [es.enter_context(nc.psum_tensor("ps%d" % i, [128, 512], F32)) for i in range(8)]
    T_PS = [Trk(excl=True) for _ in range(8)]

    def psbf(i):
        return PS[i][:].bitcast(BF16)

    V_ = nc.vector
    A_ = nc.scalar
    P_ = nc.tensor
    G_ = nc.gpsimd

    S.dma("sp", cst[:], cst_in.ap(), w=[t_cst])
    S.dma("sp", flags[:], flags_in.ap().partition_broadcast(128), w=[t_flags])
    S.dma("sp", g1[:], g1_in.ap(), w=[t_par])
    S.dma("sp", g2[:], g2_in.ap(), w=[t_par])
    S.dma("sp", gf[:], gf_in.ap(), w=[t_par])
    S.dma("sp", bada[:], bada_in.ap(), w=[t_par])
    S.op("dve", lambda: V_.tensor_copy(out=cbf[:, 0:128], in_=cst[:, 0:128]), r=[t_cst], w=[t_cbf])
    S.op("dve", lambda: V_.memset(cbf[:, 128:256], 1.0), w=[t_cbf])
    S.op("dve", lambda: V_.tensor_copy(out=cbf[:, 256:512], in_=cst[:, 640:896]), r=[t_cst], w=[t_cbf])
    S.op("dve", lambda: V_.memset(ones_f[:], 1.0), w=[t_onesf])
    with ExitStack() as c2es:
        c2f = c2es.enter_context(nc.sbuf_tensor("c2f", [128, 640], F32))
        t_c2f = Trk()
        S.dma("sp", c2f[:], cst2_in.ap(), w=[t_c2f])
        S.op("dve", lambda: V_.tensor_copy(out=cb2[:], in_=c2f[:]), r=[t_c2f], w=[t_cb2])
        S.barrier()

    with ExitStack() as pes:
        zt = pes.enter_context(nc.sbuf_tensor("zt", [128, 4, 512], BF16))
        t_zt = Trk()
        S.op("dve", lambda: V_.memset(zt[:], 0.0), w=[t_zt])
        S.dma("sp", KT.ap()[:, 0:PADT].rearrange("(c p) t -> p c t", p=128), zt[:, :, 0:PADT], r=[t_zt], w=[T_KT[0]])
        S.dma("sp", KT.ap()[:, PADT + ntok:PADT + ntok + PADT].rearrange("(c p) t -> p c t", p=128), zt[:, :, 0:PADT],
              r=[t_zt], w=[T_KT[nt + 1]])
        S.dma("sp", VV.ap()[0:PADT, :].rearrange("(b p) c -> p b c", p=128), zt[:, 0:2, :], r=[t_zt], w=[T_VV[0]])
        S.dma("sp", VV.ap()[PADT + ntok:PADT + ntok + PADT, :].rearrange("(b p) c -> p b c", p=128), zt[:, 0:2, :],
              r=[t_zt], w=[T_VV[nt + 1]])

        cact = pes.enter_context(nc.sbuf_tensor("cact", [128, 8, 8], F32))
        t_cact = Trk()
        S.dma("sp", cact[:], cT_in.ap(), w=[t_cact])
        S.op("act", lambda: A_.activation(out=cact[:], in_=cact[:], func=AF.Silu), r=[t_cact], w=[t_cact])
        wa = [pes.enter_context(nc.sbuf_tensor("wa%d" % i, [128, 8, 512], F32)) for i in range(2)]
        t_wa = [Trk(), Trk()]
        it = 0
        for l in range(depth):
            for cb in range(12):
                b = it % 2
                S.dma("sp", wa[b][:], wada_in.ap()[l, :, cb * 512:(cb + 1) * 512].rearrange("(k p) c -> p k c", p=128),
                      w=[t_wa[b]])
                bank = it % 2

                def mm(b=b, bank=bank):
                    ins = None
                    for j in range(4):
                        for k in range(8):
                            ins = P_.matmul(PS[bank][:, j * 8:(j + 1) * 8], lhsT=wa[b][:, k, j * 128:(j + 1) * 128],
                                            rhs=cact[:, k, :], start=(k == 0), stop=(k == 7))
                    return ins
                S.op("pe", mm, r=[t_wa[b], t_cact], w=[T_PS[bank]])
                c0 = l * 48 + cb * 4
                S.op("dve", lambda bank=bank, c0=c0: V_.tensor_tensor(
                    out=mod[:, c0:c0 + 4, :], in0=PS[bank][:, 0:32].rearrange("p (j s) -> p j s", j=4),
                    in1=bada[:, c0:c0 + 4].unsqueeze(2).to_broadcast([128, 4, 8]), op=ALU.add),
                    r=[T_PS[bank], t_par], w=[t_mod])
                it += 1
        for l in range(depth):
            for (Ax, gx, off) in ((A1, g1, 8), (A2, g2, 32)):
                S.op("dve", lambda Ax=Ax, off=off, l=l: V_.tensor_scalar(
                    out=Ax[:, l * 8:(l + 1) * 8, :], in0=mod[:, l * 48 + off:l * 48 + off + 8, :], scalar1=1.0, scalar2=None,
                    op0=ALU.add), r=[t_mod], w=[t_mod])
                S.op("dve", lambda Ax=Ax, gx=gx, l=l: V_.tensor_tensor(
                    out=Ax[:, l * 8:(l + 1) * 8, :], in0=Ax[:, l * 8:(l + 1) * 8, :],
                    in1=gx[:, l * 8:(l + 1) * 8].unsqueeze(2).to_broadcast([128, 8, 8]), op=ALU.mult),
                    r=[t_mod, t_par], w=[t_mod])
        S.barrier()

    def modv(l, which, ch, seg):
        return mod[:, l * 48 + which * 8 + ch, seg:seg + 1]

    def norm_mod(xT, t_x, xsq, t_xsq, hT, t_h, rstd, t_rstd, tmp, t_tmp, nb, Ax, l, sh_which, seg, psb, N):
        for dch in range(8):
            S.op("act", lambda dch=dch: A_.activation(out=xsq[:, dch, :], in_=xT[:, dch, :], func=AF.Square),
                 r=[t_x], w=[t_xsq])

        def mm():
            ins = None
            for dch in range(8):
                ins = P_.matmul(PS[psb][:, 0:N], lhsT=ones_b, rhs=xsq[:, dch, :], start=(dch == 0), stop=(dch == 7))
            return ins
        S.op("pe", mm, r=[t_xsq, t_cbf], w=[T_PS[psb]])
        S.op("act", lambda: A_.activation(out=rstd[:], in_=PS[psb][:, 0:N], func=AF.Ln, bias=epsD[:, 0:1], scale=1.0 / D),
             r=[T_PS[psb], t_eps], w=[t_rstd])
        S.op("act", lambda: A_.activation(out=rstd[:], in_=rstd[:], func=AF.Exp, scale=-0.5), r=[t_rstd], w=[t_rstd])
        for dch in range(8):
            b = dch % nb
            S.op("dve", lambda dch=dch, b=b: V_.tensor_tensor(out=tmp[b][:], in0=xT[:, dch, :], in1=rstd[:], op=ALU.mult),
                 r=[t_x, t_rstd], w=[t_tmp[b]])
            S.op("act", lambda dch=dch, b=b: A_.activation(
                out=hT[:, dch, :], in_=tmp[b][:], func=AF.Identity, bias=modv(l, sh_which, dch, seg),
                scale=Ax[:, l * 8 + dch, seg:seg + 1]), r=[t_tmp[b], t_mod], w=[t_h])

    epsD = sbp("epsD", [128, 4], F32)
    t_eps = Trk()
    S.op("dve", lambda: V_.memset(epsD[:, 0:1], EPS), w=[t_eps])
    S.op("dve", lambda: V_.memset(epsD[:, 1:2], float(-0.5 * np.log(128.0))), w=[t_eps])

    def phase_A(l):
        with ExitStack() as pes:
            def sb(name, shape, dt):
                return pes.enter_context(nc.sbuf_tensor(name + "_L%d" % l, list(shape), dt))
            win = sb("win", [128, 8, IN_COLS], BF16)
            t_winp = [Trk() for _ in range(4)]
            for cp in range(4):
                c0 = cp * 1412
                S.dma("pool", win[:, :, c0:c0 + 1412], win_in.ap()[l, :, c0:c0 + 1412].rearrange("(k p) c -> p k c", p=128),
                      w=[t_winp[cp]])

            def twin(ca, cb):
                return [t_winp[i] for i in range(ca // 1412, (cb - 1) // 1412 + 1)]
            xtok = sb("xtok", [128, 4, D], F32) if l == 0 else None
            t_xtok = Trk()
            xT = [sb("xT%d" % i, [128, 8, 512], F32) for i in range(2)]
            t_xT = [Trk(), Trk()]
            xsq2 = [sb("xsq%d" % i, [128, 8, 512], BF16) for i in range(2)]
            t_xsq2 = [Trk(), Trk()]
            hT2 = [sb("hT%d" % i, [128, 8, 512], BF16) for i in range(2)]
            t_h2 = [Trk(), Trk()]
            rstd = sb("rstd", [128, 512], F32)
            t_rstd = Trk()
            tmp = [sb("tmpA%d" % i, [128, 512], F32) for i in range(2)]
            t_tmp = [Trk(), Trk()]
            stg = [sb("stg%d" % i, [128, 4, 512], BF16) for i in range(4)]
            t_stg = [Trk() for _ in range(4)]
            abst = sb("abst", [128, 4, 16], F32)
            t_abst = Trk()
            si = [0]

            def load_x(t):
                b = t % 2
                if l == 0:
                    S.dma("sp", xtok[:], x_in.ap()[t * 512:(t + 1) * 512, :].rearrange("(b p) d -> p b d", p=128), w=[t_xtok])
                else:
                    S.dma("sp", xT[b][:], XT.ap()[:, t * 512:(t + 1) * 512].rearrange("(c p) t -> p c t", p=128),
                          r=[T_XT[t]], w=[t_xT[b]])

            def prep(t):
                seg = t // 4
                b = t % 2
                if l == 0:
                    if t == 0:
                        load_x(0)
                    for dch in range(8):
                        bank = dch % 2

                        def tp(dch=dch, bank=bank):
                            ins = None
                            for blk in range(4):
                                ins = P_.transpose(out=PS[bank][:, blk * 128:(blk + 1) * 128],
                                                   in_=xtok[:, blk, dch * 128:(dch + 1) * 128], identity=ident_f)
                            return ins
                        S.op("pe", tp, r=[t_xtok, t_cst], w=[T_PS[bank]])
                        S.op("dve", lambda dch=dch, bank=bank: V_.tensor_copy(out=xT[b][:, dch, :], in_=PS[bank][:]),
                             r=[T_PS[bank]], w=[t_xT[b]])
                    S.dma("sp", XT.ap()[:, t * 512:(t + 1) * 512].rearrange("(c p) t -> p c t", p=128), xT[b][:],
                          r=[t_xT[b]], w=[T_XT[t]])
                norm_mod(xT[b], t_xT[b], xsq2[b], t_xsq2[b], hT2[b], t_h2[b], rstd, t_rstd, tmp, t_tmp, 2, A1, l, 0, seg, 2, 512)

            if l != 0:
                load_x(0)
            prep(0)
            for t in range(nt):
                seg = t // 4
                b = t % 2
                hT = hT2[b]
                t_h = t_h2[b]
                if t + 1 < nt:
                    load_x(t + 1)

                def fm_proj(col0, nch, dst, dst_trk, dst_col0, scale=None):
                    for g in range(0, nch, 4):
                        s_ = si[0] % 4
                        si[0] += 1
                        n4 = min(4, nch - g)
                        for c in range(n4):
                            bank = 3 + (c % 4)
                            cc = col0 + (g + c) * 128

                            def mm(cc=cc, bank=bank):
                                ins = None
                                for k in range(8):
                                    ins = P_.matmul(PS[bank][:], lhsT=win[:, k, cc:cc + 128], rhs=hT[:, k, :],
                                                    start=(k == 0), stop=(k == 7))
                                return ins
                            S.op("pe", mm, r=twin(cc, cc + 128) + [t_h], w=[T_PS[bank]])
                            if (c % 2) == 0:
                                if scale is None:
                                    S.op("act", lambda c=c, bank=bank, s_=s_: A_.copy(out=stg[s_][:, c, :], in_=PS[bank][:]),
                                         r=[T_PS[bank]], w=[t_stg[s_]])
                                else:
                                    S.op("act", lambda c=c, bank=bank, s_=s_: A_.mul(out=stg[s_][:, c, :], in_=PS[bank][:],
                                                                                     mul=scale),
                                         r=[T_PS[bank]], w=[t_stg[s_]])
                            else:
                                if scale is None:
                                    S.op("dve", lambda c=c, bank=bank, s_=s_: V_.tensor_copy(out=stg[s_][:, c, :], in_=PS[bank][:]),
                                         r=[T_PS[bank]], w=[t_stg[s_]])
                                else:
                                    S.op("dve", lambda c=c, bank=bank, s_=s_: V_.tensor_scalar(
                                        out=stg[s_][:, c, :], in0=PS[bank][:], scalar1=scale, scalar2=None, op0=ALU.mult),
                                        r=[T_PS[bank]], w=[t_stg[s_]])
                        r0 = dst_col0 + g * 128
                        S.dma("sp", dst[r0:r0 + n4 * 128, :].rearrange("(c p) t -> p c t", p=128), stg[s_][:, 0:n4, :],
                              r=[t_stg[s_]], w=[dst_trk])

                tk = slice(t * 512, (t + 1) * 512)
                fm_proj(C_NQ, 4, QT.ap()[:, tk], T_QT[t], 0, scale=0.125)
                fm_proj(C_NK, 4, KT.ap()[:, PADT + t * 512:PADT + (t + 1) * 512], T_KT[t + 1], 0)
                fm_proj(C_DQ, 12, DQ.ap()[:, 64 + t * 512:64 + (t + 1) * 512], T_DQ[t], 0)
                if t + 1 < nt:
                    prep(t + 1)
                fm_proj(C_G, 16, GT.ap()[:, tk], T_GT[t], 0)

                def tm_proj(col0, dst_ap, dst_trk):
                    s_ = si[0] % 4
                    si[0] += 1
                    for blk in range(4):
                        bank = 3 + blk

                        def mm(blk=blk, bank=bank):
                            ins = None
                            for k in range(8):
                                ins = P_.matmul(PS[bank][:], lhsT=hT[:, k, blk * 128:(blk + 1) * 128],
                                                rhs=win[:, k, col0:col0 + 512], start=(k == 0), stop=(k == 7))
                            return ins
                        S.op("pe", mm, r=twin(col0, col0 + 512) + [t_h], w=[T_PS[bank]])
                        if blk % 2 == 0:
                            S.op("act", lambda blk=blk, bank=bank, s_=s_: A_.copy(out=stg[s_][:, blk, :], in_=PS[bank][:]),
                                 r=[T_PS[bank]], w=[t_stg[s_]])
                        else:
                            S.op("dve", lambda blk=blk, bank=bank, s_=s_: V_.tensor_copy(out=stg[s_][:, blk, :], in_=PS[bank][:]),
                                 r=[T_PS[bank]], w=[t_stg[s_]])
                    S.dma("sp", dst_ap.rearrange("(b p) c -> p b c", p=128), stg[s_][:], r=[t_stg[s_]], w=[dst_trk])

                tm_proj(C_NV, VV.ap()[PADT + t * 512:PADT + (t + 1) * 512, :], T_VV[t + 1])
                tm_proj(C_Z, ZZ.ap()[tk, :], T_ZZ[t])

                def mmab():
                    ins = None
                    for blk in range(4):
                        for k in range(8):
                            ins = P_.matmul(PS[7][:, blk * 128:(blk + 1) * 128], lhsT=hT[:, k, blk * 128:(blk + 1) * 128],
                                            rhs=win[:, k, C_AB - 112:C_AB + 16], start=(k == 0), stop=(k == 7))
                    return ins
                S.op("pe", mmab, r=twin(C_AB - 112, C_AB + 16) + [t_h], w=[T_PS[7]])
                S.op("dve", lambda: V_.tensor_copy(out=abst[:], in_=PS[7][:].rearrange("p (b c) -> p b c", b=4)[:, :, 112:128]),
                     r=[T_PS[7]], w=[t_abst])
                S.dma("sp", ABs.ap()[:, t * 4:(t + 1) * 4, :], abst[:], r=[t_abst], w=[T_AB[t]])
            S.barrier()

    def phase_B(l):
        with ExitStack() as pes:
            def sb(name, shape, dt):
                return pes.enter_context(nc.sbuf_tensor(name + "_L%d" % l, list(shape), dt))
            T2 = sb("T2", [128, 14, 512], BF16)
            t_T2 = Trk()
            with ExitStack() as tes:
                hk = tes.enter_context(nc.sbuf_tensor("hk_L%d" % l, [64, 8, 15 * 64], F32))
                t_hk = Trk()
                src = bass.AP(rpb_in, l * 8 * 15 * 128, [[1, 64], [128, 15], [15 * 128, 8], [1, 64]])
                for h0 in range(8):
                    srch = bass.AP(rpb_in, l * 8 * 15 * 128 + h0 * 15 * 128, [[1, 64], [128, 15], [1, 64]])
                    S.dma("sp", hk[:, h0, :].rearrange("p (a b) -> p a b", a=15), srch, w=[t_hk])
                traw = tes.enter_context(nc.sbuf_tensor("traw_L%d" % l, [128, 512], F32))
                t_traw = Trk()
                Jx = cst[0:64, 896:960]
                for d in range(14):
                    bank = d % 2

                    def mm(d=d, bank=bank):
                        ins = None
                        for h in range(8):
                            ins = P_.matmul(PS[bank][:, h * 64:(h + 1) * 64], lhsT=hk[:, h, d * 64:(d + 2) * 64], rhs=Jx,
                                            start=True, stop=True)
                        return ins
                    S.op("pe", mm, r=[t_hk, t_cst], w=[T_PS[bank]])
                    S.op("act", lambda bank=bank: A_.activation(out=traw[:], in_=PS[bank][:], func=AF.Exp),
                         r=[T_PS[bank]], w=[t_traw])
                    S.op("dve", lambda d=d: V_.tensor_tensor(
                        out=T2[:, d, :].rearrange("p (h q) -> p h q", h=8), in0=traw[:].rearrange("p (h q) -> p h q", h=8),
                        in1=cst[:, 960:1024].unsqueeze(1).to_broadcast([128, 8, 64]), op=ALU.mult),
                        r=[t_traw, t_cst], w=[t_T2])
                S.barrier()

            qt = [sb("qt%d" % i, [128, 4, 512], BF16) for i in range(2)]
            t_qt = [Trk(), Trk()]
            qz = sb("qz", [128, 8, 512], BF16)
            t_qz = Trk()
            kw = [sb("kw%d" % i, [128, 4, 1024], BF16) for i in range(2)]
            t_kw = [Trk(), Trk()]
            ve = [sb("ve%d" % i, [128, 8, 512], BF16) for i in range(2)]
            vo = [sb("vo%d" % i, [128, 7, 512], BF16) for i in range(2)]
            t_ve = [Trk(), Trk()]
            t_vo = [Trk(), Trk()]
            ex = [sb("ex%d" % i, [128, 512], BF16) for i in range(4)]
            t_ex = [Trk() for _ in range(4)]
            pt = [sb("pt%d" % i, [128, 512], BF16) for i in range(8)]
            t_pt = [Trk() for _ in range(8)]
            lnd = [sb("lnd%d" % i, [64, 512], F32) for i in range(2)]
            t_lnd = [Trk(), Trk()]
            osb = [sb("osb%d" % i, [64, 512], F32) for i in range(2)]
            t_osb = [Trk(), Trk()]
            ost = [sb("ost%d" % i, [64, 8, 512], BF16) for i in range(2)]
            t_ost = [Trk(), Trk()]
            S.op("dve", lambda: V_.memset(qz[:], 0.0), w=[t_qz])

            def loads(t):
                b = t % 2
                tk = slice(t * 512, (t + 1) * 512)
                S.dma("sp", qt[b][:], QT.ap()[:, tk].rearrange("(c p) t -> p c t", p=128), r=[T_QT[t]], w=[t_qt[b]])
                k0 = t * 512
                trk = [T_KT[i] for i in (t, t + 1, t + 2) if 0 <= i < nt + 2]
                S.dma("sp", kw[b][:], KT.ap()[:, k0:k0 + 1024].rearrange("(c p) t -> p c t", p=128), r=trk, w=[t_kw[b]])
                trv = [T_VV[i] for i in (t, t + 1, t + 2) if 0 <= i < nt + 2]
                S.dma("sp", ve[b][:], VV.ap()[k0:k0 + 1024, :].rearrange("(m p) c -> p m c", p=128), r=trv, w=[t_ve[b]])
                S.dma("sp", vo[b][:], VV.ap()[k0 + 64:k0 + 64 + 896, :].rearrange("(m p) c -> p m c", p=128), r=trv,
                      w=[t_vo[b]])

            loads(0)
            rowi = [0]
            for t in range(nt):
                b = t % 2
                seg = t // 4
                tpos = t % 4
                if t + 1 < nt:
                    loads(t + 1)
                for hp in range(2):
                    S.op("pool", lambda hp=hp: G_.tensor_copy(
                        out=qz[hp * 64:(hp + 1) * 64, :, :].rearrange("p (c two) t -> p c two t", two=2)[:, :, hp, :],
                        in_=qt[b][hp * 64:(hp + 1) * 64, :, :]), r=[t_qt[b]], w=[t_qz])
                ob = t % 2
                jobs = []
                for i in range(8):
                    alts = []
                    edge_start = (tpos == 0 and i < 4)
                    edge_end = (tpos == 3 and i > 4)
                    if edge_start:
                        fidx = seg - 1
                        if fidx >= 0:
                            alts = [("std", i, -4), ("clamp", 4, -i)]
                        else:
                            alts = [("clamp", 4, -i)]
                            fidx = None
                    elif edge_end:
                        fidx = seg
                        if seg + 1 < nseg:
                            alts = [("std", i, -4), ("clamp", 4, -i)]
                        else:
                            alts = [("clamp", 4, -i)]
                            fidx = None
                    else:
                        alts = [("std", i, -4)]
                        fidx = None
                    jobs.append((i, alts, fidx))

                def stage1(i, rel, o, rb):
                    for kb in range(4):
                        bank = kb

                        def mm(kb=kb, bank=bank):
                            ins = None
                            ks = (rel + 2 * kb) * 64
                            for h in range(8):
                                ins = P_.matmul(PS[bank][:, h * 64:(h + 1) * 64], lhsT=kw[b][:, h // 2, ks:ks + 128],
                                                rhs=qz[:, h, i * 64:(i + 1) * 64], start=True, stop=True)
                            return ins
                        S.op("pe", mm, r=[t_kw[b], t_qz], w=[T_PS[bank]])
                        S.op("act", lambda kb=kb, bank=bank: A_.activation(out=ex[kb][:], in_=PS[bank][:], func=AF.Exp),
                             r=[T_PS[bank]], w=[t_ex[kb]])
                        d = o + 2 * kb + 7
                        pi = rb * 4 + kb
                        S.op("dve", lambda kb=kb, d=d, pi=pi: V_.tensor_tensor(out=pt[pi][:], in0=ex[kb][:], in1=T2[:, d, :],
                                                                              op=ALU.mult),
                             r=[t_ex[kb], t_T2], w=[t_pt[pi]])

                def stage2(i, rel, rb):
                    dbank = 4 + rb
                    obank = 6 + rb

                    def mmd():
                        ins = None
                        for kb in range(4):
                            ins = P_.matmul(PS[dbank][0:64, :], lhsT=ones_b[:, 0:64], rhs=pt[rb * 4 + kb][:],
                                            start=(kb == 0), stop=(kb == 3))
                        return ins
                    S.op("pe", mmd, r=[t_pt[rb * 4 + k_] for k_ in range(4)] + [t_cbf], w=[T_PS[dbank]])

                    def mmo():
                        ins = None
                        for h in range(8):
                            for kb in range(4):
                                rr = rel + 2 * kb
                                vsrc = ve[b][:, rr // 2, h * 64:(h + 1) * 64] if rr % 2 == 0 else \
                                    vo[b][:, (rr - 1) // 2, h * 64:(h + 1) * 64]
                                ins = P_.matmul(PS[obank][0:64, h * 64:(h + 1) * 64], lhsT=vsrc,
                                                rhs=pt[rb * 4 + kb][:, h * 64:(h + 1) * 64], start=(kb == 0), stop=(kb == 3))
                        return ins
                    S.op("pe", mmo, r=[t_pt[rb * 4 + k_] for k_ in range(4)] + [t_ve[b], t_vo[b]], w=[T_PS[obank]])
                    S.op("act", lambda: A_.activation(out=lnd[rb][:], in_=PS[dbank][0:64, :], func=AF.Ln),
                         r=[T_PS[dbank]], w=[t_lnd[rb]])
                    S.op("act", lambda: A_.activation(out=lnd[rb][:], in_=lnd[rb][:], func=AF.Exp, scale=-1.0),
                         r=[t_lnd[rb]], w=[t_lnd[rb]])

                def finish(i, res, fidx):
                    dst = ost[ob][:, :, i * 64:(i + 1) * 64]
                    if len(res) == 1:
                        rb, obank = res[0]
                        S.op("dve", lambda: V_.tensor_tensor(
                            out=dst, in0=PS[obank][0:64, :].rearrange("p (h q) -> p h q", h=8),
                            in1=lnd[rb][:].rearrange("p (h q) -> p h q", h=8), op=ALU.mult),
                            r=[T_PS[obank], t_lnd[rb]], w=[t_ost[ob]])
                    else:
                        (rb0, ob0), (rb1, ob1) = res
                        S.op("dve", lambda: V_.tensor_tensor(out=osb[0][:], in0=PS[ob0][0:64, :], in1=lnd[rb0][:], op=ALU.mult),
                             r=[T_PS[ob0], t_lnd[rb0]], w=[t_osb[0]])
                        S.op("dve", lambda: V_.tensor_tensor(out=osb[1][:], in0=PS[ob1][0:64, :], in1=lnd[rb1][:], op=ALU.mult),
                             r=[T_PS[ob1], t_lnd[rb1]], w=[t_osb[1]])
                        S.op("dve", lambda: V_.tensor_tensor(out=osb[0][:], in0=osb[0][:], in1=osb[1][:], op=ALU.subtract),
                             r=[t_osb[0], t_osb[1]], w=[t_osb[0]])
                        S.op("dve", lambda: V_.scalar_tensor_tensor(
                            out=dst, in0=osb[0][:].rearrange("p (h q) -> p h q", h=8), scalar=flags[0:64, fidx:fidx + 1],
                            in1=osb[1][:].rearrange("p (h q) -> p h q", h=8), op0=ALU.mult, op1=ALU.add),
                            r=[t_osb[0], t_osb[1], t_flags], w=[t_ost[ob]])

                flat = []
                for (i, alts, fidx) in jobs:
                    for ai, (nm, rel, o) in enumerate(alts):
                        flat.append((i, rel, o, ai == len(alts) - 1, fidx))
                pend = None
                resacc = []
                for (i, rel, o, lastalt, fidx) in flat:
                    rb = rowi[0] % 2
                    rowi[0] += 1
                    stage1(i, rel, o, rb)
                    if pend is not None:
                        pi_, prel, prb, plast, pfidx = pend
                        stage2(pi_, prel, prb)
                        resacc.append((prb, 6 + prb))
                        if plast:
                            finish(pi_, resacc, pfidx)
                            resacc = []
                    pend = (i, rel, rb, lastalt, fidx)
                pi_, prel, prb, plast, pfidx = pend
                stage2(pi_, prel, prb)
                resacc.append((prb, 6 + prb))
                finish(pi_, resacc, pfidx)
                S.dma("sp", OAT.ap()[:, t * 512:(t + 1) * 512].rearrange("(h p) t -> p h t", p=64), ost[ob][:],
                      r=[t_ost[ob]], w=[T_OAT[t]])
            S.barrier()

    def phase_C(l):
        with ExitStack() as pes:
            def sb(name, shape, dt):
                return pes.enter_context(nc.sbuf_tensor(name + "_L%d" % l, list(shape), dt))
            NCH = SEG // 128
            diagw = sb("diagw", [128, 60, 128], BF16)
            t_diagw = Trk()
            cw = sb("cw", [128, depth * 60], F32)
            t_cw = Trk()
            S.dma("sp", cw[:], conv_in.ap(), w=[t_cw])
            for j in range(60):
                S.op("dve", lambda j=j: V_.tensor_scalar(out=diagw[:, j, :], in0=ident_f, scalar1=cw[:, l * 60 + j:l * 60 + j + 1],
                                                         scalar2=None, op0=ALU.mult), r=[t_cw, t_cst], w=[t_diagw])
            nexpa = sb("nexpa", [128, 8], F32)
            dtb = sb("dtb_sb", [128, 8], F32)
            dng = sb("dng_sb", [128, 128], F32)
            t_hp = Trk()
            S.dma("sp", nexpa[:], alog_in.ap()[:, l * 8:(l + 1) * 8].partition_broadcast(128), w=[t_hp])
            S.dma("sp", dtb[:], dtb_in.ap()[:, l * 8:(l + 1) * 8].partition_broadcast(128), w=[t_hp])
            S.dma("sp", dng[:], dng_in.ap()[:, l * 128:(l + 1) * 128].partition_broadcast(128), w=[t_hp])
            S.op("act", lambda: A_.activation(out=nexpa[:], in_=nexpa[:], func=AF.Exp), r=[t_hp], w=[t_hp])
            S.op("dve", lambda: V_.tensor_scalar(out=nexpa[:], in0=nexpa[:], scalar1=-1.0, scalar2=None, op0=ALU.mult),
                 r=[t_hp], w=[t_hp])

            rawb = [sb("raw%d" % i, [128, 12, 516], BF16) for i in range(2)]
            t_rawb = [Trk(), Trk()]
            qkv = sb("qkv", [128, 12, SEG], BF16)
            t_qkv = Trk()
            sq = [sb("sqC%d" % i, [128, 512], BF16) for i in range(2)]
            t_sq = [Trk(), Trk()]
            rn = [sb("rnC%d" % i, [128, 512], F32) for i in range(2)]
            t_rn = [Trk(), Trk()]
            ab = sb("abC", [128, NCH, 16], F32)
            t_ab = Trk()
            beta = sb("beta", [128, NCH, 4], F32)
            gg = sb("gg", [128, NCH, 4], F32)
            Gc = sb("Gc", [128, NCH, 4], F32)
            nGc = sb("nGc", [128, NCH, 4], F32)
            eG = sb("eG", [128, NCH, 4], F32)
            negb = sb("negb", [128, NCH, 4], F32)
            Gt = sb("Gt", [128, NCH, 4], F32)
            egt = sb("egt", [128, NCH, 4], F32)
            kds = sb("kds", [128, NCH, 4], F32)
            t_gate = Trk()
            NSLOT = 2
            def slotbufs(k):
                d = {}
                d["gbc"] = sb("gbc%d" % k, [128, 4, 128], F32)
                for nm in ("DT", "DTS", "Pa", "PaT", "Pb0", "Pb1", "PbT0", "PbT1", "XT", "T1", "T1p", "Off", "OffT"):
                    d[nm] = sb("%s_s%d" % (nm, k), [128, 512], BF16)
                d["tmpP"] = sb("tmpP%d" % k, [128, 512], F32)
                for nm in list(d.keys()):
                    d["t_" + nm] = Trk()
                return d
            SL = [slotbufs(k) for k in range(NSLOT)]
            T4b = Trk()
            qkT = sb("qkT", [128, NCH, 512], BF16)
            t_qkT = Trk()
            Yk = sb("Yk", [128, NCH, 512], BF16)
            t_Y = Trk()
            kd = [sb("kd%d" % i, [128, 512], BF16) for i in range(2)]
            t_kd = [Trk(), Trk()]
            Sst = sb("Sst", [128, 512], F32)
            Sbf = sb("Sbf", [128, 512], BF16)
            t_S = Trk()
            t_Sbf = Trk()
            tmpc = [sb("tmpc%d" % i, [128, 512], F32) for i in range(2)]
            t_tmpc = [Trk(), Trk()]
            Rr = sb("Rr", [128, 512], BF16)
            t_R = Trk()
            vnew = sb("vnew", [128, 512], BF16)
            t_vnew = Trk()
            och = [sb("och%d" % i, [128, 512], F32) for i in range(2)]
            t_och = [Trk(), Trk()]
            ofl = [sb("ofl%d" % i, [128, 512], F32) for i in range(2)]
            t_ofl = [Trk(), Trk()]
            zt_ = [sb("ztC%d" % i, [128, 512], BF16) for i in range(2)]
            t_zt = [Trk(), Trk()]
            ms = sb("msC", [128, 8], F32)
            t_ms = Trk()
            odt = [sb("odt%d" % i, [128, 4, 128], BF16) for i in range(2)]
            t_odt = [Trk(), Trk()]
            odtok = sb("odtok", [128, 512], BF16)
            t_odtok = Trk()

            def bc4(apx):
                return apx.unsqueeze(2).to_broadcast([128, 4, 128])

            def v4(apx):
                return apx.rearrange("p (h d) -> p h d", h=4)

            def seg_pass(s, dr_):
                t0 = s * SEG
                tiles = [s * 4 + i for i in range(4)]
                if dr_ == 0:
                    k_ = 0
                    for tb in range(4):
                        raw = rawb[tb % 2]
                        t_raw = t_rawb[tb % 2]
                        trk = [T_DQ[s * 4 + tb]]
                        if s * 4 + tb - 1 >= 0:
                            trk.append(T_DQ[s * 4 + tb - 1])
                        if s * 4 + tb + 1 < nt:
                            trk.append(T_DQ[s * 4 + tb + 1])
                        c0_ = 64 + t0 + tb * 512 - 2
                        S.dma("sp", raw[:], DQ.ap()[:, c0_:c0_ + 516].rearrange("(c p) t -> p c t", p=128), r=trk, w=[t_raw])
                        if tb == 0:
                            if s > 0:
                                S.op("dve", lambda raw=raw: V_.tensor_scalar(out=raw[:, :, 0:2], in0=raw[:, :, 0:2], scalar1=flags[:, s - 1:s],
                                                                             scalar2=None, op0=ALU.mult), r=[t_raw, t_flags], w=[t_raw])
                            else:
                                S.op("dve", lambda raw=raw: V_.memset(raw[:, :, 0:2], 0.0), w=[t_raw])
                        if tb == 3:
                            if s + 1 < nseg:
                                S.op("dve", lambda raw=raw: V_.tensor_scalar(out=raw[:, :, 514:516], in0=raw[:, :, 514:516],
                                                                             scalar1=flags[:, s:s + 1], scalar2=None, op0=ALU.mult),
                                     r=[t_raw, t_flags], w=[t_raw])
                            else:
                                S.op("dve", lambda raw=raw: V_.memset(raw[:, :, 514:516], 0.0), w=[t_raw])
                        for ch in range(12):
                            bank = k_ % 2
                            k_ += 1

                            def mm(ch=ch, raw=raw, bank=bank):
                                ins = None
                                for tap in range(5):
                                    ins = P_.matmul(PS[bank][:], lhsT=diagw[:, tap * 12 + ch, :], rhs=raw[:, ch, tap:tap + 512],
                                                    start=(tap == 0), stop=(tap == 4))
                                return ins
                            S.op("pe", mm, r=[t_diagw, t_raw], w=[T_PS[bank]])
                            S.op("act", lambda ch=ch, tb=tb, bank=bank: A_.activation(
                                out=qkv[:, ch, tb * 512:(tb + 1) * 512], in_=PS[bank][:], func=AF.Silu), r=[T_PS[bank]], w=[t_qkv])
                else:
                    S.dma("sp", qkv[:], QN.ap()[:, t0:t0 + SEG].rearrange("(c p) t -> p c t", p=128), r=[T_QN[i] for i in tiles], w=[t_qkv])
                S.dma("sp", ab[:], ABs.ap()[:, s * NCH:(s + 1) * NCH, :], r=[T_AB[i] for i in tiles], w=[t_ab])
                if dr_ == 0:
                    k_ = 0
                    for ch in range(8):
                        for tb in range(4):
                            b2 = k_ % 2
                            bank = 2 + b2
                            k_ += 1
                            sl = qkv[:, ch, tb * 512:(tb + 1) * 512]
                            S.op("act", lambda sl=sl, b2=b2: A_.activation(out=sq[b2][:], in_=sl, func=AF.Square), r=[t_qkv], w=[t_sq[b2]])
                            S.op("pe", lambda b2=b2, bank=bank: P_.matmul(PS[bank][:], lhsT=ones_b, rhs=sq[b2][:], start=True, stop=True),
                                 r=[t_sq[b2], t_cbf], w=[T_PS[bank]])
                            S.op("act", lambda b2=b2, bank=bank: A_.activation(out=rn[b2][:], in_=PS[bank][:], func=AF.Ln,
                                                                              bias=epsD[:, 0:1], scale=1.0),
                                 r=[T_PS[bank], t_eps], w=[t_rn[b2]])
                            if ch < 4:
                                S.op("act", lambda b2=b2: A_.activation(out=rn[b2][:], in_=rn[b2][:], func=AF.Exp, scale=-0.5,
                                                                        bias=epsD[:, 1:2]), r=[t_rn[b2], t_eps], w=[t_rn[b2]])
                            else:
                                S.op("act", lambda b2=b2: A_.activation(out=rn[b2][:], in_=rn[b2][:], func=AF.Exp, scale=-0.5),
                                     r=[t_rn[b2]], w=[t_rn[b2]])
                            S.op("dve", lambda sl=sl, b2=b2: V_.tensor_tensor(out=sl, in0=sl, in1=rn[b2][:], op=ALU.mult),
                                 r=[t_qkv, t_rn[b2]], w=[t_qkv])
                    S.dma("sp", QN.ap()[:, t0:t0 + SEG].rearrange("(c p) t -> p c t", p=128), qkv[:], r=[t_qkv], w=[T_QN[i] for i in tiles])
                bsl = ab[:, :, dr_ * 4:dr_ * 4 + 4]
                asl = ab[:, :, 8 + dr_ * 4:8 + dr_ * 4 + 4]
                hb = lambda tt: tt[:, dr_ * 4:dr_ * 4 + 4].unsqueeze(1).to_broadcast([128, NCH, 4])
                S.op("act", lambda: A_.activation(out=beta[:], in_=bsl, func=AF.Exp, scale=-1.0), r=[t_ab], w=[t_gate])
                S.op("dve", lambda: V_.tensor_scalar(out=beta[:], in0=beta[:], scalar1=1.0, scalar2=None, op0=ALU.add),
                     r=[t_gate], w=[t_gate])
                S.op("dve", lambda: V_.reciprocal(out=beta[:], in_=beta[:]), r=[t_gate], w=[t_gate])
                S.op("dve", lambda: V_.tensor_scalar(out=negb[:], in0=beta[:], scalar1=-1.0, scalar2=None, op0=ALU.mult),
                     r=[t_gate], w=[t_gate])
                S.op("dve", lambda: V_.tensor_tensor(out=gg[:], in0=asl, in1=hb(dtb), op=ALU.add), r=[t_ab, t_hp], w=[t_gate])
                S.op("act", lambda: A_.activation(out=gg[:], in_=gg[:], func=AF.Exp), r=[t_gate], w=[t_gate])
                S.op("act", lambda: A_.activation(out=gg[:], in_=gg[:], func=AF.Ln, bias=1.0, scale=1.0), r=[t_gate], w=[t_gate])
                S.op("dve", lambda: V_.tensor_tensor(out=gg[:], in0=gg[:], in1=hb(nexpa), op=ALU.mult), r=[t_gate, t_hp], w=[t_gate])
                ggf = gg[:].rearrange("p c h -> p (c h)")
                S.op("pe", lambda: P_.matmul(PS[6][:, 0:64], lhsT=TRI[dr_], rhs=ggf, start=True, stop=True),
                     r=[t_gate, t_cst], w=[T_PS[6]])
                S.op("pe", lambda: P_.matmul(PS[7][:, 0:64], lhsT=ones_f[:], rhs=ggf, start=True, stop=True),
                     r=[t_gate, t_onesf], w=[T_PS[7]])
                fl = lambda tt: tt[:].rearrange("p c h -> p (c h)")
                S.op("dve", lambda: V_.tensor_copy(out=fl(Gc), in_=PS[6][:, 0:64]), r=[T_PS[6]], w=[t_gate])
                S.op("dve", lambda: V_.tensor_scalar(out=fl(nGc), in0=PS[6][:, 0:64], scalar1=-1.0, scalar2=None, op0=ALU.mult),
                     r=[T_PS[6]], w=[t_gate])
                S.op("act", lambda: A_.activation(out=fl(eG), in_=PS[6][:, 0:64], func=AF.Exp), r=[T_PS[6]], w=[t_gate])
                S.op("dve", lambda: V_.tensor_copy(out=fl(Gt), in_=PS[7][:, 0:64]), r=[T_PS[7]], w=[t_gate])
                S.op("act", lambda: A_.activation(out=fl(egt), in_=PS[7][:, 0:64], func=AF.Exp), r=[T_PS[7]], w=[t_gate])
                S.op("dve", lambda: V_.tensor_tensor(out=kds[:], in0=Gt[:], in1=Gc[:], op=ALU.subtract), r=[t_gate], w=[t_gate])
                S.op("act", lambda: A_.activation(out=kds[:], in_=kds[:], func=AF.Exp), r=[t_gate], w=[t_gate])

                mk = lambda m: cb2[:, m * 128:(m + 1) * 128].unsqueeze(1).to_broadcast([128, 4, 128])
                idb4 = ident_b.unsqueeze(1).to_broadcast([128, 4, 128])

                def mm4(bank, lh, rh):
                    def f():
                        ins = None
                        for h in range(4):
                            hs = slice(h * 128, (h + 1) * 128)
                            ins = P_.matmul(PS[bank][:, hs], lhsT=lh[:, hs], rhs=rh[:, hs], start=True, stop=True)
                        return ins
                    return f

                def prep_gen(c, k):
                    B = SL[k]
                    bA, bB = 2 * k, 2 * k + 1
                    cs = slice(c * 128, (c + 1) * 128)
                    gbc, DT, DTS, Pa, PaT, XT, T1, T1p, Off, OffT, tmpP = (B[n] for n in (
                        "gbc", "DT", "DTS", "Pa", "PaT", "XT", "T1", "T1p", "Off", "OffT", "tmpP"))
                    Pb = [B["Pb0"], B["Pb1"]]
                    PbT = [B["PbT0"], B["PbT1"]]
                    t_Pb = [B["t_Pb0"], B["t_Pb1"]]
                    t_PbT = [B["t_PbT0"], B["t_PbT1"]]
                    S.op("dve", lambda: V_.tensor_copy(out=gbc[:], in_=bc4(gg[:, c, :])), r=[t_gate], w=[B["t_gbc"]])

                    def mmg():
                        ins = None
                        for h in range(4):
                            P_.matmul(PS[bA][:, h * 128:(h + 1) * 128], lhsT=gbc[:, h, :], rhs=TRI[dr_], start=True, stop=False)
                            ins = P_.matmul(PS[bA][:, h * 128:(h + 1) * 128], lhsT=ident_f, rhs=NEGM[dr_], start=False, stop=True)
                        return ins
                    S.op("pe", mmg, r=[B["t_gbc"], t_cst], w=[T_PS[bA]])

                    def mmkk():
                        ins = None
                        for h in range(4):
                            ins = P_.matmul(PS[bB][:, h * 128:(h + 1) * 128], lhsT=qkv[:, 4 + h, cs], rhs=qkv[:, 4 + h, cs],
                                            start=True, stop=True)
                        return ins
                    S.op("pe", mmkk, r=[t_qkv], w=[T_PS[bB]])
                    yield
                    for h in range(4):
                        S.op("act", lambda h=h: A_.activation(out=DT[:, h * 128:(h + 1) * 128], in_=PS[bA][:, h * 128:(h + 1) * 128],
                                                              func=AF.Exp, bias=nGc[:, c, h:h + 1], scale=1.0),
                             r=[T_PS[bA], t_gate], w=[B["t_DT"]])
                    yield
                    S.op("pool", lambda: G_.tensor_tensor(out=v4(DTS[:]), in0=v4(DT[:]),
                                                          in1=STRICT[dr_].unsqueeze(1).to_broadcast([128, 4, 128]), op=ALU.mult),
                         r=[B["t_DT"], t_cbf], w=[B["t_DTS"]])

                    def mmqk():
                        ins = None
                        for h in range(4):
                            ins = P_.matmul(PS[bA][:, h * 128:(h + 1) * 128], lhsT=qkv[:, 4 + h, cs], rhs=qkv[:, h, cs],
                                            start=True, stop=True)
                        return ins
                    S.op("pe", mmqk, r=[t_qkv], w=[T_PS[bA]])
                    yield
                    S.op("dve", lambda: V_.tensor_tensor(out=qkT[:, c, :], in0=PS[bA][:], in1=DT[:], op=ALU.mult),
                         r=[T_PS[bA], B["t_DT"]], w=[t_qkT])
                    S.op("dve", lambda: V_.tensor_tensor(out=tmpP[:], in0=PS[bB][:], in1=DTS[:], op=ALU.mult),
                         r=[T_PS[bB], B["t_DTS"]], w=[B["t_tmpP"]])
                    yield
                    S.op("dve", lambda: V_.tensor_tensor(out=v4(Pa[:]), in0=v4(tmpP[:]), in1=bc4(negb[:, c, :]), op=ALU.mult),
                         r=[B["t_tmpP"], t_gate], w=[B["t_Pa"]])

                    def tpn():
                        ins = None
                        for h in range(4):
                            ins = P_.transpose(out=psbf(bB)[:, h * 128:(h + 1) * 128], in_=Pa[:, h * 128:(h + 1) * 128],
                                               identity=ident_b)
                        return ins
                    S.op("pe", tpn, r=[B["t_Pa"], t_cbf], w=[T_PS[bB]])
                    yield
                    S.op("act", lambda: A_.copy(out=PaT[:], in_=psbf(bB)[:, 0:512]), r=[T_PS[bB]], w=[B["t_PaT"]])
                    X = Yk[:, c, :]
                    t_X = t_Yc[c]
                    S.op("pool", lambda: G_.tensor_tensor(out=v4(Pb[0][:]), in0=v4(Pa[:]), in1=mk(0), op=ALU.mult),
                         r=[B["t_Pa"], t_cb2], w=[t_Pb[0]])
                    yield
                    S.op("pool", lambda: G_.tensor_tensor(out=v4(PbT[0][:]), in0=v4(PaT[:]), in1=mk(0), op=ALU.mult),
                         r=[B["t_PaT"], t_cb2], w=[t_PbT[0]])
                    S.op("dve", lambda: V_.tensor_tensor(out=v4(X), in0=v4(Pb[0][:]), in1=idb4, op=ALU.add),
                         r=[t_Pb[0], t_cbf], w=[t_X])
                    yield
                    S.op("dve", lambda: V_.tensor_tensor(out=v4(XT[:]), in0=v4(PbT[0][:]), in1=idb4, op=ALU.add),
                         r=[t_PbT[0], t_cbf], w=[B["t_XT"]])
                    cur = 0
                    for st in range(2):
                        nx = 1 - cur
                        S.op("pe", mm4(bA, PbT[cur], Pb[cur]), r=[t_Pb[cur], t_PbT[cur]], w=[T_PS[bA]])
                        S.op("pe", mm4(bB, Pb[cur], PbT[cur]), r=[t_Pb[cur], t_PbT[cur]], w=[T_PS[bB]])
                        yield
                        S.op("act", lambda nx=nx: A_.copy(out=Pb[nx][:], in_=PS[bA][:]), r=[T_PS[bA]], w=[t_Pb[nx]])
                        S.op("act", lambda nx=nx: A_.copy(out=PbT[nx][:], in_=PS[bB][:]), r=[T_PS[bB]], w=[t_PbT[nx]])
                        yield
                        S.op("pe", mm4(bA, PbT[nx], X), r=[t_PbT[nx], t_X], w=[T_PS[bA]])
                        S.op("pe", mm4(bB, Pb[nx], XT), r=[t_Pb[nx], B["t_XT"]], w=[T_PS[bB]])
                        yield
                        S.op("dve", lambda: V_.tensor_tensor(out=X, in0=PS[bA][:], in1=X, op=ALU.add), r=[T_PS[bA], t_X], w=[t_X])
                        S.op("dve", lambda: V_.tensor_tensor(out=XT[:], in0=PS[bB][:], in1=XT[:], op=ALU.add),
                             r=[T_PS[bB], B["t_XT"]], w=[B["t_XT"]])
                        yield
                        cur = nx
                    for lv in range(1, 5):
                        last_lv = (lv == 4)
                        S.op("pool", lambda lv=lv: G_.tensor_tensor(out=v4(OffT[:]), in0=v4(PaT[:]), in1=mk(lv), op=ALU.mult),
                             r=[B["t_PaT"], t_cb2], w=[B["t_OffT"]])
                        if not last_lv:
                            S.op("pool", lambda lv=lv: G_.tensor_tensor(out=v4(Off[:]), in0=v4(Pa[:]), in1=mk(lv), op=ALU.mult),
                                 r=[B["t_Pa"], t_cb2], w=[B["t_Off"]])
                        yield
                        S.op("pe", mm4(bA, OffT, X), r=[B["t_OffT"], t_X], w=[T_PS[bA]])
                        if not last_lv:
                            S.op("pe", mm4(bB, Off, XT), r=[B["t_Off"], B["t_XT"]], w=[T_PS[bB]])
                        yield
                        S.op("act", lambda: A_.copy(out=T1[:], in_=PS[bA][:]), r=[T_PS[bA]], w=[B["t_T1"]])
                        if not last_lv:
                            S.op("act", lambda: A_.copy(out=T1p[:], in_=PS[bB][:]), r=[T_PS[bB]], w=[B["t_T1p"]])
                        yield
                        S.op("pe", mm4(bA, XT, T1), r=[B["t_XT"], B["t_T1"]], w=[T_PS[bA]])
                        if not last_lv:
                            S.op("pe", mm4(bB, X, T1p), r=[t_X, B["t_T1p"]], w=[T_PS[bB]])
                        yield
                        S.op("dve", lambda: V_.tensor_tensor(out=X, in0=PS[bA][:], in1=X, op=ALU.add), r=[T_PS[bA], t_X], w=[t_X])
                        if not last_lv:
                            S.op("dve", lambda: V_.tensor_tensor(out=XT[:], in0=PS[bB][:], in1=XT[:], op=ALU.add),
                                 r=[T_PS[bB], B["t_XT"]], w=[B["t_XT"]])
                        yield

                if dr_ == 0:
                    fi = s - 1 if s > 0 else None
                else:
                    fi = s if s + 1 < nseg else None
                if fi is None:
                    S.op("dve", lambda: V_.memset(Sst[:], 0.0), w=[t_S])
                else:
                    S.op("dve", lambda fi=fi: V_.tensor_scalar(out=Sst[:], in0=Sst[:], scalar1=flags[:, fi:fi + 1], scalar2=None,
                                                               op0=ALU.mult), r=[t_S, t_flags], w=[t_S])
                S.op("act", lambda: A_.copy(out=Sbf[:], in_=Sst[:]), r=[t_S], w=[t_Sbf])

                order = list(range(NCH)) if dr_ == 0 else list(range(NCH - 1, -1, -1))

                def scan_gen(n_, c):
                    cs = slice(c * 128, (c + 1) * 128)
                    ob = n_ % 2
                    tg = s * 4 + c // 4
                    t_X = t_Yc[c]
                    if dr_ == 1:
                        S.dma("sp", ofl[ob][:], OF.ap()[t0 + c * 128:t0 + (c + 1) * 128, :], r=[T_OF[tg]], w=[t_ofl[ob]])
                        S.dma("sp", zt_[ob][:], ZZ.ap()[t0 + c * 128:t0 + (c + 1) * 128, :], r=[T_ZZ[tg]], w=[t_zt[ob]])

                    def tpk():
                        ins = None
                        for h in range(4):
                            ins = P_.transpose(out=psbf(4)[:, h * 128:(h + 1) * 128], in_=qkv[:, 4 + h, cs], identity=ident_b)
                        return ins

                    def tpv():
                        ins = None
                        for h in range(4):
                            ins = P_.transpose(out=psbf(4)[:, 512 + h * 128:512 + (h + 1) * 128], in_=qkv[:, 8 + h, cs], identity=ident_b)
                        return ins

                    def tpkv():
                        tpk()
                        return tpv()
                    S.op("pe", tpkv, r=[t_qkv, t_cbf], w=[T_PS[4]])
                    yield
                    S.op("dve", lambda: V_.tensor_tensor(out=v4(kd[ob][:]), in0=v4(psbf(4)[:, 0:512]), in1=bc4(kds[:, c, :]),
                                                         op=ALU.mult), r=[T_PS[4], t_gate], w=[t_kd[ob]])

                    def mmz():
                        ins = None
                        for h in range(4):
                            hs = slice(h * 128, (h + 1) * 128)
                            ins = P_.matmul(PS[5][:, hs], lhsT=qkv[:, 4 + h, cs], rhs=Sbf[:, hs], start=True, stop=True)
                        return ins
                    S.op("pe", mmz, r=[t_qkv, t_Sbf], w=[T_PS[5]])

                    def mmp1():
                        ins = None
                        for h in range(4):
                            hs = slice(h * 128, (h + 1) * 128)
                            ins = P_.matmul(PS[6][:, hs], lhsT=qkv[:, h, cs], rhs=Sbf[:, hs], start=True, stop=True)
                        return ins
                    S.op("pe", mmp1, r=[t_qkv, t_Sbf], w=[T_PS[6]])
                    yield
                    S.op("dve", lambda: V_.tensor_tensor(out=v4(tmpc[0][:]), in0=v4(PS[5][:]), in1=bc4(eG[:, c, :]), op=ALU.mult),
                         r=[T_PS[5], t_gate], w=[t_tmpc[0]])
                    yield
                    S.op("dve", lambda: V_.tensor_tensor(out=Rr[:], in0=psbf(4)[:, 512:1024], in1=tmpc[0][:], op=ALU.subtract),
                         r=[T_PS[4], t_tmpc[0]], w=[t_R])
                    yield

                    def mmv():
                        ins = None
                        for h in range(4):
                            hs = slice(h * 128, (h + 1) * 128)
                            ins = P_.matmul(PS[7][:, hs], lhsT=Yk[:, c, hs], rhs=Rr[:, hs], start=True, stop=True)
                        return ins
                    S.op("pe", mmv, r=[t_X, t_R], w=[T_PS[7]])
                    yield
                    S.op("act", lambda: A_.activation(out=tmpc[1][:], in_=PS[6][:], func=AF.Copy), r=[T_PS[6]], w=[t_tmpc[1]])
                    S.op("dve", lambda: V_.tensor_tensor(out=v4(vnew[:]), in0=v4(PS[7][:]), in1=bc4(beta[:, c, :]), op=ALU.mult),
                         r=[T_PS[7], t_gate], w=[t_vnew])
                    yield

                    def mmp2():
                        ins = None
                        for h in range(4):
                            hs = slice(h * 128, (h + 1) * 128)
                            ins = P_.matmul(PS[7][:, hs], lhsT=qkT[:, c, hs], rhs=vnew[:, hs], start=True, stop=True)
                        return ins
                    S.op("pe", mmp2, r=[t_qkT, t_vnew], w=[T_PS[7]])

                    def mms():
                        ins = None
                        for h in range(4):
                            hs = slice(h * 128, (h + 1) * 128)
                            ins = P_.matmul(PS[5][:, hs], lhsT=kd[ob][:, hs], rhs=vnew[:, hs], start=True, stop=True)
                        return ins
                    S.op("pe", mms, r=[t_kd[ob], t_vnew], w=[T_PS[5]])
                    yield
                    S.op("dve", lambda: V_.tensor_tensor(out=v4(Sst[:]), in0=v4(Sst[:]), in1=bc4(egt[:, c, :]), op=ALU.mult),
                         r=[t_S, t_gate], w=[t_S])
                    yield
                    S.op("dve", lambda: V_.tensor_tensor(out=Sst[:], in0=PS[5][:], in1=Sst[:], op=ALU.add), r=[T_PS[5], t_S], w=[t_S])
                    S.op("act", lambda: A_.copy(out=Sbf[:], in_=Sst[:]), r=[t_S], w=[t_Sbf])
                    yield
                    S.op("pool", lambda: G_.tensor_tensor(out=v4(tmpc[1][:]), in0=v4(tmpc[1][:]), in1=bc4(eG[:, c, :]), op=ALU.mult),
                         r=[t_tmpc[1], t_gate], w=[t_tmpc[1]])
                    yield
                    S.op("dve", lambda: V_.tensor_tensor(out=och[ob][:], in0=PS[7][:], in1=tmpc[1][:], op=ALU.add),
                         r=[T_PS[7], t_tmpc[1]], w=[t_och[ob]])
                    yield
                    if dr_ == 0:
                        S.dma("sp", OF.ap()[t0 + c * 128:t0 + (c + 1) * 128, :], och[ob][:], r=[t_och[ob]], w=[T_OF[tg]])
                    else:
                        S.op("dve", lambda: V_.tensor_tensor(out=och[ob][:], in0=och[ob][:], in1=ofl[ob][:], op=ALU.add),
                             r=[t_och[ob], t_ofl[ob]], w=[t_och[ob]])
                        yield
                        for h in range(4):
                            S.op("act", lambda h=h: A_.activation(out=tmpc[1][:, h * 128:(h + 1) * 128],
                                                                  in_=och[ob][:, h * 128:(h + 1) * 128], func=AF.Square,
                                                                  accum_out=ms[:, h:h + 1]),
                                 r=[t_och[ob]], w=[t_tmpc[1], t_ms])
                        yield
                        S.op("act", lambda: A_.activation(out=ms[:, 4:8], in_=ms[:, 0:4], func=AF.Ln, bias=epsD[:, 0:1], scale=1.0 / 128),
                             r=[t_ms, t_eps], w=[t_ms])
                        S.op("act", lambda: A_.activation(out=ms[:, 4:8], in_=ms[:, 4:8], func=AF.Exp, scale=-0.5), r=[t_ms], w=[t_ms])
                        yield
                        S.op("dve", lambda: V_.tensor_tensor(out=v4(och[ob][:]), in0=v4(och[ob][:]), in1=bc4(ms[:, 4:8]), op=ALU.mult),
                             r=[t_och[ob], t_ms], w=[t_och[ob]])
                        S.op("act", lambda: A_.activation(out=tmpc[0][:], in_=zt_[ob][:], func=AF.Silu), r=[t_zt[ob]], w=[t_tmpc[0]])
                        yield
                        S.op("pool", lambda: G_.tensor_tensor(out=v4(tmpc[0][:]), in0=v4(tmpc[0][:]),
                                                              in1=dng[:].unsqueeze(1).to_broadcast([128, 4, 128]), op=ALU.mult),
                             r=[t_tmpc[0], t_hp], w=[t_tmpc[0]])
                        yield
                        S.op("dve", lambda: V_.tensor_tensor(out=odtok[:], in0=och[ob][:], in1=tmpc[0][:], op=ALU.mult),
                             r=[t_och[ob], t_tmpc[0]], w=[t_odtok])
                        yield

                        def tpo():
                            ins = None
                            for h in range(4):
                                ins = P_.transpose(out=psbf(6)[:, h * 128:(h + 1) * 128], in_=odtok[:, h * 128:(h + 1) * 128],
                                                   identity=ident_b)
                            return ins
                        S.op("pe", tpo, r=[t_odtok, t_cbf], w=[T_PS[6]])
                        yield
                        S.op("act", lambda: A_.copy(out=odt[ob][:], in_=psbf(6)[:, 0:512].rearrange("p (h t) -> p h t", h=4)),
                             r=[T_PS[6]], w=[t_odt[ob]])
                        S.dma("sp", ODT.ap()[:, t0 + c * 128:t0 + (c + 1) * 128].rearrange("(h p) t -> p h t", p=128), odt[ob][:],
                              r=[t_odt[ob]], w=[T_ODT[tg]])

                t_Yc = [Trk() for _ in range(NCH)]
                prep_q = list(order)
                active = {}
                prep_done = set()
                scan_n = 0
                scan_g = None
                while scan_n < NCH:
                    for k in range(NSLOT):
                        if k not in active and prep_q:
                            c_ = prep_q.pop(0)
                            active[k] = (prep_gen(c_, k), c_)
                    if scan_g is None and order[scan_n] in prep_done:
                        scan_g = scan_gen(scan_n, order[scan_n])
                    for k in list(active.keys()):
                        g_, c_ = active[k]
                        try:
                            next(g_)
                        except StopIteration:
                            prep_done.add(c_)
                            del active[k]
                    if scan_g is not None:
                        try:
                            next(scan_g)
                        except StopIteration:
                            scan_g = None
                            scan_n += 1

            for s in range(nseg):
                seg_pass(s, 0)
            for s in range(nseg - 1, -1, -1):
                seg_pass(s, 1)
            S.barrier()

    def phase_D(l):
        with ExitStack() as pes:
            def sb(name, shape, dt):
                return pes.enter_context(nc.sbuf_tensor(name + "_L%d" % l, list(shape), dt))
            wba = sb("wba", [128, 4, D], BF16)
            wbd = sb("wbd", [128, 4, D], BF16)
            wo = sb("wo", [128, 8, D], BF16)
            t_wba, t_wbd, t_wo = Trk(), Trk(), Trk()
            S.dma("pool", wba[:], wbra_in.ap()[l].rearrange("(k p) c -> p k c", p=128), w=[t_wba])
            S.dma("pool", wbd[:], wbrd_in.ap()[l].rearrange("(k p) c -> p k c", p=128), w=[t_wbd])
            S.dma("pool", wo[:], wout_in.ap()[l].rearrange("(k p) c -> p k c", p=128), w=[t_wo])
            NB = 3
            oa = [sb("oa%d" % i, [128, 4, 512], BF16) for i in range(NB)]
            od = [sb("od%d" % i, [128, 4, 512], BF16) for i in range(NB)]
            gt = [sb("gtD%d" % i, [128, 16, 512], BF16) for i in range(NB)]
            xT = [sb("xTD%d" % i, [128, 8, 512], F32) for i in range(NB)]
            t_in = [Trk() for _ in range(NB)]
            t_x = [Trk() for _ in range(NB)]
            sgA = [sb("sgA%d" % i, [128, 512], F32) for i in range(2)]
            sgD = [sb("sgD%d" % i, [128, 512], F32) for i in range(2)]
            m1 = [sb("m1D%d" % i, [128, 512], F32) for i in range(2)]
            m2 = [sb("m2D%d" % i, [128, 512], F32) for i in range(2)]
            t_sgA, t_sgD, t_m1, t_m2 = ([Trk(), Trk()] for _ in range(4))
            mg = [sb("mgD%d" % i, [128, 8, 512], BF16) for i in range(2)]
            t_mg = [Trk(), Trk()]

            def loads(t):
                b = t % NB
                tk = slice(t * 512, (t + 1) * 512)
                S.dma("sp", oa[b][:], OAT.ap()[:, tk].rearrange("(c p) t -> p c t", p=128), r=[T_OAT[t]], w=[t_in[b]])
                S.dma("sp", od[b][:], ODT.ap()[:, tk].rearrange("(c p) t -> p c t", p=128), r=[T_ODT[t]], w=[t_in[b]])
                S.dma("sp", gt[b][:], GT.ap()[:, tk].rearrange("(c p) t -> p c t", p=128), r=[T_GT[t]], w=[t_in[b]])
                S.dma("sp", xT[b][:], XT.ap()[:, tk].rearrange("(c p) t -> p c t", p=128), r=[T_XT[t]], w=[t_x[b]])

            def br(t):
                b = t % NB
                mb = t % 2
                for c in range(8):
                    q2 = c % 2
                    ba, bd = 2 * q2, 2 * q2 + 1

                    def mma(c=c, ba=ba):
                        ins = None
                        for k in range(4):
                            ins = P_.matmul(PS[ba][:], lhsT=wba[:, k, c * 128:(c + 1) * 128], rhs=oa[b][:, k, :], start=(k == 0), stop=(k == 3))
                        return ins
                    S.op("pe", mma, r=[t_wba, t_in[b]], w=[T_PS[ba]])

                    def mmd(c=c, bd=bd):
                        ins = None
                        for k in range(4):
                            ins = P_.matmul(PS[bd][:], lhsT=wbd[:, k, c * 128:(c + 1) * 128], rhs=od[b][:, k, :], start=(k == 0), stop=(k == 3))
                        return ins
                    S.op("pe", mmd, r=[t_wbd, t_in[b]], w=[T_PS[bd]])
                    S.op("act", lambda c=c, q2=q2: A_.activation(out=sgA[q2][:], in_=gt[b][:, c, :], func=AF.Sigmoid), r=[t_in[b]], w=[t_sgA[q2]])
                    S.op("act", lambda c=c, q2=q2: A_.activation(out=sgD[q2][:], in_=gt[b][:, 8 + c, :], func=AF.Sigmoid), r=[t_in[b]], w=[t_sgD[q2]])
                    S.op("dve", lambda ba=ba, q2=q2: V_.tensor_tensor(out=m1[q2][:], in0=PS[ba][:], in1=sgA[q2][:], op=ALU.mult),
                         r=[T_PS[ba], t_sgA[q2]], w=[t_m1[q2]])
                    S.op("dve", lambda bd=bd, q2=q2: V_.tensor_tensor(out=m2[q2][:], in0=PS[bd][:], in1=sgD[q2][:], op=ALU.mult),
                         r=[T_PS[bd], t_sgD[q2]], w=[t_m2[q2]])
                    S.op("pool", lambda c=c, q2=q2: G_.tensor_tensor(out=mg[mb][:, c, :], in0=m1[q2][:], in1=m2[q2][:], op=ALU.add),
                         r=[t_m1[q2], t_m2[q2]], w=[t_mg[mb]])

            def outp(t):
                b = t % NB
                mb = t % 2
                seg = t // 4
                for c in range(8):
                    bank = 4 + (c % 4)

                    def mmo(c=c, bank=bank):
                        ins = None
                        for k in range(8):
                            ins = P_.matmul(PS[bank][:], lhsT=wo[:, k, c * 128:(c + 1) * 128], rhs=mg[mb][:, k, :], start=(k == 0), stop=(k == 7))
                        return ins
                    S.op("pe", mmo, r=[t_wo, t_mg[mb]], w=[T_PS[bank]])
                    S.op("dve", lambda c=c, bank=bank: V_.scalar_tensor_tensor(
                        out=xT[b][:, c, :], in0=PS[bank][:], scalar=modv(l, 2, c, seg), in1=xT[b][:, c, :], op0=ALU.mult, op1=ALU.add),
                        r=[T_PS[bank], t_x[b], t_mod], w=[t_x[b]])
                S.dma("sp", XT.ap()[:, t * 512:(t + 1) * 512].rearrange("(c p) t -> p c t", p=128), xT[b][:], r=[t_x[b]], w=[T_XT[t]])

            loads(0)
            if nt > 1:
                loads(1)
            br(0)
            for t in range(nt):
                if t + 2 < nt:
                    loads(t + 2)
                if t + 1 < nt:
                    br(t + 1)
                outp(t)
            S.barrier()

    def phase_E(l, last):
        TE = 256
        nte = ntok // TE
        with ExitStack() as pes:
            def sb(name, shape, dt):
                return pes.enter_context(nc.sbuf_tensor(name + "_L%d" % l, list(shape), dt))
            w1 = sb("w1", [128, 8, DFF], BF16)
            w2 = sb("w2", [128, 32, D], BF16)
            t_w1p = [Trk() for _ in range(4)]
            t_w2p = [Trk() for _ in range(4)]
            for cp in range(4):
                S.dma("pool", w1[:, :, cp * 1024:(cp + 1) * 1024], w1_in.ap()[l, :, cp * 1024:(cp + 1) * 1024].rearrange("(k p) c -> p k c", p=128),
                      w=[t_w1p[cp]])
            for kp in range(4):
                S.dma("pool", w2[:, kp * 8:(kp + 1) * 8, :], w2_in.ap()[l, kp * 1024:(kp + 1) * 1024, :].rearrange("(k p) c -> p k c", p=128),
                      w=[t_w2p[kp]])
            xT = [sb("xTE%d" % i, [128, 8, TE], F32) for i in range(2)]
            t_x = [Trk(), Trk()]
            nb_ = 2
            xsq2 = [sb("xsqE%d" % i, [128, 8, TE], BF16) for i in range(nb_)] * (2 // nb_)
            t_xsq2 = [Trk() for _ in range(nb_)] * (2 // nb_)
            hT2 = [sb("hTE%d" % i, [128, 8, TE], BF16) for i in range(nb_)] * (2 // nb_)
            t_h2 = [Trk() for _ in range(nb_)] * (2 // nb_)
            xsq = xsq2[0]
            t_xsq = t_xsq2[0]
            rstd = sb("rstdE", [128, TE], F32)
            t_rstd = Trk()
            tmp = [sb("tmpE%d" % i, [128, TE], F32) for i in range(2)]
            t_tmp = [Trk(), Trk()]
            rl = [sb("rlE%d" % i, [128, 2, TE], BF16) for i in range(2)]
            t_rl = [Trk(), Trk()]
            hid = sb("hidE", [128, 32, TE], BF16)
            t_hid = Trk()
            if last:
                ytok = sb("ytokE", [128, 2, D], F32)
                t_ytok = Trk()

            def loads(t):
                b = t % 2
                S.dma("sp", xT[b][:], XT.ap()[:, t * TE:(t + 1) * TE].rearrange("(c p) t -> p c t", p=128), r=[T_XT[t // 2]], w=[t_x[b]])

            def prep(t):
                b = t % 2
                norm_mod(xT[b], t_x[b], xsq2[b], t_xsq2[b], hT2[b], t_h2[b], rstd, t_rstd, tmp, t_tmp, 2, A2, l, 3, (t * TE) // SEG, 0, TE)

            loads(0)
            prep(0)
            for t in range(nte):
                b = t % 2
                seg = (t * TE) // SEG
                if t + 1 < nte:
                    loads(t + 1)
                hT = hT2[b]
                t_h = t_h2[b]
                for c2 in range(16):
                    bank = 1 + (c2 % 3)
                    r2 = c2 % 2

                    def mm1(c2=c2, bank=bank):
                        ins = None
                        for j in range(2):
                            cc = (c2 * 2 + j) * 128
                            for k in range(8):
                                ins = P_.matmul(PS[bank][:, j * TE:(j + 1) * TE], lhsT=w1[:, k, cc:cc + 128], rhs=hT[:, k, :],
                                                start=(k == 0), stop=(k == 7))
                        return ins
                    S.op("pe", mm1, r=[t_w1p[(c2 * 256) // 1024], t_h], w=[T_PS[bank]])
                    S.op("act", lambda bank=bank, r2=r2: A_.activation(out=rl[r2][:].rearrange("p j t -> p (j t)"), in_=PS[bank][:],
                                                                       func=AF.Relu), r=[T_PS[bank]], w=[t_rl[r2]])
                    S.op("dve", lambda c2=c2, r2=r2: V_.tensor_tensor(out=hid[:, c2 * 2:c2 * 2 + 2, :], in0=rl[r2][:], in1=rl[r2][:], op=ALU.mult),
                         r=[t_rl[r2]], w=[t_hid])
                    if c2 == 11 and t + 1 < nte:
                        prep(t + 1)
                for c2 in range(4):
                    bank = 4 + (c2 % 4)

                    def mm2(c2=c2, bank=bank):
                        ins = None
                        for j in range(2):
                            cc = (c2 * 2 + j) * 128
                            for k in range(32):
                                ins = P_.matmul(PS[bank][:, j * TE:(j + 1) * TE], lhsT=w2[:, k, cc:cc + 128], rhs=hid[:, k, :],
                                                start=(k == 0), stop=(k == 31))
                        return ins
                    S.op("pe", mm2, r=t_w2p + [t_hid], w=[T_PS[bank]])
                    for j in range(2):
                        c = c2 * 2 + j
                        S.op("dve", lambda c=c, j=j, bank=bank: V_.scalar_tensor_tensor(
                            out=xT[b][:, c, :], in0=PS[bank][:, j * TE:(j + 1) * TE], scalar=modv(l, 5, c, seg), in1=xT[b][:, c, :],
                            op0=ALU.mult, op1=ALU.add), r=[T_PS[bank], t_x[b], t_mod], w=[t_x[b]])
                if not last:
                    S.dma("sp", XT.ap()[:, t * TE:(t + 1) * TE].rearrange("(c p) t -> p c t", p=128), xT[b][:], r=[t_x[b]], w=[T_XT[t // 2]])
                else:
                    for dch in range(8):
                        S.op("act", lambda dch=dch: A_.activation(out=xsq2[b][:, dch, :], in_=xT[b][:, dch, :], func=AF.Square),
                             r=[t_x[b]], w=[t_xsq2[b]])

                    def mmf():
                        ins = None
                        for dch in range(8):
                            ins = P_.matmul(PS[0][:, 0:TE], lhsT=ones_b, rhs=xsq2[b][:, dch, :], start=(dch == 0), stop=(dch == 7))
                        return ins
                    S.op("pe", mmf, r=[t_xsq2[b], t_cbf], w=[T_PS[0]])
                    S.op("act", lambda: A_.activation(out=rstd[:], in_=PS[0][:, 0:TE], func=AF.Ln, bias=epsD[:, 0:1], scale=1.0 / D),
                         r=[T_PS[0], t_eps], w=[t_rstd])
                    S.op("act", lambda: A_.activation(out=rstd[:], in_=rstd[:], func=AF.Exp, scale=-0.5), r=[t_rstd], w=[t_rstd])
                    for dch in range(8):
                        S.op("dve", lambda dch=dch: V_.scalar_tensor_tensor(
                            out=xT[b][:, dch, :], in0=xT[b][:, dch, :], scalar=gf[:, dch:dch + 1], in1=rstd[:], op0=ALU.mult, op1=ALU.mult),
                            r=[t_x[b], t_rstd, t_par], w=[t_x[b]])
                    for blk in range(TE // 128):
                        for half in range(2):
                            bank = 1 + half

                            def tpf(blk=blk, half=half, bank=bank):
                                ins = None
                                for j in range(4):
                                    dch = half * 4 + j
                                    ins = P_.transpose(out=PS[bank][:, j * 128:(j + 1) * 128], in_=xT[b][:, dch, blk * 128:(blk + 1) * 128],
                                                       identity=ident_f)
                                return ins
                            S.op("pe", tpf, r=[t_x[b], t_cst], w=[T_PS[bank]])
                            if half == 0:
                                S.op("act", lambda blk=blk, bank=bank: A_.copy(out=ytok[:, blk, 0:512], in_=PS[bank][:]), r=[T_PS[bank]], w=[t_ytok])
                            else:
                                S.op("dve", lambda blk=blk, bank=bank: V_.tensor_copy(out=ytok[:, blk, 512:1024], in_=PS[bank][:]),
                                     r=[T_PS[bank]], w=[t_ytok])
                    S.dma("sp", y_out.ap()[t * TE:(t + 1) * TE, :].rearrange("(b p) d -> p b d", p=128), ytok[:], r=[t_ytok], w=[Trk()])
            S.barrier()

    phases_all = phases
    for l in range(depth):
        phases = phases_last if (phases_last is not None and l == depth - 1) else phases_all
        if "A" in phases:
            phase_A(l)
        if "B" in phases:
            phase_B(l)
        if "C" in phases:
            phase_C(l)
        if "D" in phases:
            phase_D(l)
        if "E" in phases:
            phase_E(l, last=(l == depth - 1))
    S.barrier()
    es.close()
    return nc, S


def make_consts():
    c = np.zeros((128, 1024), np.float32)
    p = np.arange(128)[:, None]
    i = np.arange(128)[None, :]
    c[:, 0:128] = (p == i)
    c[:, 128:256] = (p <= i)
    c[:, 256:384] = (p >= i)
    c[:, 384:512] = np.where(p > i, NEGBIG, 0.0)
    c[:, 512:640] = np.where(p < i, NEGBIG, 0.0)
    c[:, 640:768] = (p < i)
    c[:, 768:896] = (p > i)
    jj = np.arange(64)
    c[0:64, 896:960] = (jj[:, None] + jj[None, :] == 63)
    qc = np.arange(64)
    qs = np.clip(qc - 8, 0, 48)
    kc = np.arange(64)
    valid = (kc[:, None] >= qs[None, :]) & (kc[:, None] < qs[None, :] + 16)
    c[0:64, 960:1024] = valid
    c[64:128, 960:1024] = valid
    return c


def make_consts2():
    c = np.zeros((128, 640), np.float32)
    p = np.arange(128)[:, None]
    i = np.arange(128)[None, :]
    prev = (p // 8 == i // 8)
    c[:, 0:128] = prev
    for m, b in enumerate((16, 32, 64, 128)):
        cur = (p // b == i // b)
        c[:, (m + 1) * 128:(m + 2) * 128] = cur & ~prev
        prev = cur
    return c


def pp(v, nchunk):
    v = np.asarray(v, np.float32)
    lead = v.shape[:-1]
    v = v.reshape(lead + (nchunk, 128))
    v = np.moveaxis(v, -1, 0)
    return np.ascontiguousarray(v.reshape(128, -1))


def core_plan():
    plan = []
    for c in range(NCORES):
        if c < 2:
            segs = [("p", c, i * SEG) for i in range(4)] + [("s", c, 0)]
            flags = [1, 1, 1, 0, 0, 0, 0, 0]
        else:
            segs = [("s", 2 + (c - 2) * 5 + i, 0) for i in range(5)]
            flags = [0] * 8
        plan.append((segs, flags))
    return plan


def shared_inputs(norm_mix_g, norm_mlp_g, w_ada, b_ada, w_in, na_rpb, dn_conv, dn_a_log, dn_dt_bias, dn_norm_g,
                  w_br_attn, w_br_dn, w_out, w_mlp1, w_mlp2, final_norm_g, depth=DEPTH):
    f = lambda a: np.ascontiguousarray(np.asarray(a, np.float32))
    rp = np.zeros((depth, 8, 15, 128), np.float32)
    rp[:, :, :, 48:79] = np.asarray(na_rpb, np.float32)[:depth]
    cw = np.asarray(dn_conv, np.float32)[:depth].reshape(depth, 5, 12, 128)
    cw = np.ascontiguousarray(np.moveaxis(cw, -1, 0).reshape(128, depth * 60))
    return {
        "g1": pp(np.asarray(norm_mix_g)[:depth], 8), "g2": pp(np.asarray(norm_mlp_g)[:depth], 8), "gf": pp(final_norm_g, 8),
        "bada": pp(np.asarray(b_ada)[:depth], 48), "w_ada": f(w_ada)[:depth], "w_in": f(w_in)[:depth], "rpbp": rp, "convw": cw,
        "alog": f(dn_a_log)[:depth].reshape(1, depth * 8), "dtb": f(dn_dt_bias)[:depth].reshape(1, depth * 8),
        "dng": f(dn_norm_g)[:depth].reshape(1, depth * 128),
        "w_br_attn": f(w_br_attn)[:depth], "w_br_dn": f(w_br_dn)[:depth], "w_out": f(w_out)[:depth],
        "w_mlp1": f(w_mlp1)[:depth], "w_mlp2": f(w_mlp2)[:depth], "consts": make_consts(), "consts2": make_consts2(),
    }


def core_inputs(segs, flags, x_prompt, x_sample, c_prompt, c_sample):
    xs, cs = [], []
    for (g, b, st) in segs:
        if g == "p":
            xs.append(np.asarray(x_prompt[b, st:st + SEG], np.float32))
            cs.append(np.asarray(c_prompt[b], np.float32))
        else:
            xs.append(np.asarray(x_sample[b, st:st + SEG], np.float32))
            cs.append(np.asarray(c_sample[b], np.float32))
    x = np.ascontiguousarray(np.concatenate(xs, axis=0))
    cm = np.zeros((8, D), np.float32)
    cm[:len(cs)] = np.stack(cs)
    cT = np.ascontiguousarray(cm.reshape(8, 8, 128).transpose(2, 1, 0))
    return {"x": x, "cT": cT, "flags": np.asarray(flags, np.float32).reshape(1, 8)}


def kernel(x_prompt, x_sample, c_prompt, c_sample, norm_mix_g, norm_mlp_g, w_ada, b_ada, w_in, na_rpb, dn_conv,
           dn_a_log, dn_dt_bias, dn_norm_g, w_br_attn, w_br_dn, w_out, w_mlp1, w_mlp2, final_norm_g):
    x_prompt = np.asarray(x_prompt)
    x_sample = np.asarray(x_sample)
    c_prompt = np.asarray(c_prompt)
    c_sample = np.asarray(c_sample)
    nc, _ = build_program()
    shared = shared_inputs(norm_mix_g, norm_mlp_g, w_ada, b_ada, w_in, na_rpb, dn_conv, dn_a_log, dn_dt_bias, dn_norm_g,
                           w_br_attn, w_br_dn, w_out, w_mlp1, w_mlp2, final_norm_g)
    plan = core_plan()
    in_maps = []
    for (segs, flags) in plan:
        m = dict(shared)
        m.update(core_inputs(segs, flags, x_prompt, x_sample, c_prompt, c_sample))
        in_maps.append(m)
    res = run_bass_kernel_spmd(nc, in_maps, core_ids=list(range(NCORES)))
    y_prompt = np.zeros(x_prompt.shape, np.float32)
    y_sample = np.zeros(x_sample.shape, np.float32)
    for ci, (segs, flags) in enumerate(plan):
        y = np.asarray(res.results[ci]["y"])
        for si, (g, b, st) in enumerate(segs):
            blk = y[si * SEG:(si + 1) * SEG]
            if g == "p":
                y_prompt[b, st:st + SEG] = blk
            else:
                y_sample[b] = blk
    return (y_prompt, y_sample)
```
